# Optimizing a Trainium2 kernel written in Bass

```python
import jax, jax.numpy as jnp
from jax import lax
import numpy as np

D_MODEL = 1024
BATCH = 16
SEQ = 2048
DEPTH = 2
DEC_BATCH = 128
DEC_SEQ = 4
PAST_LEN = 16384
PAGE_SIZE = 128

SWA_HEADS = 8
SWA_KV_HEADS = 2
SWA_HEAD_DIM = 64
SWA_WINDOW = 128
ROPE_DIM = SWA_HEAD_DIM // 4
ROPE_THETA = 500000.0
LRU_WIDTH = 512
LRU_BLOCKS = 8
LRU_BLOCK_W = LRU_WIDTH // LRU_BLOCKS
LRU_CONV_W = 4
LRU_C = 8.0
GLA_HEADS = 4
GLA_DK = 64
GLA_DV = 128
GLA_RANK = 16
GLA_TAU = 16.0
GLA_CHUNK = 32
N_MEM = 256
MEM_HEADS = 4
MEM_HEAD_DIM = 128
D_FF = 2816
FFN_CONV_W = 3

EPS = 1e-6
NEG = -1e30
F32 = jnp.float32
SWA_Q = SWA_HEADS * SWA_HEAD_DIM
SWA_KV = SWA_KV_HEADS * SWA_HEAD_DIM
SWA_GROUP = SWA_HEADS // SWA_KV_HEADS
GLA_K = GLA_HEADS * GLA_DK
GLA_V = GLA_HEADS * GLA_DV
MEM_W = MEM_HEADS * MEM_HEAD_DIM
IN_SPLITS = (SWA_Q, SWA_KV, SWA_KV, LRU_WIDTH, LRU_WIDTH, GLA_K, GLA_K, GLA_V, GLA_V, GLA_RANK, D_MODEL, D_MODEL, D_MODEL)
IN_COLS = sum(IN_SPLITS)

kernel_name = 'hybrid_swa_rglru_gla_memxattn_convffn_step'


def rmsnorm(x, g):
    xf = x.astype(F32)
    y = xf * lax.rsqrt(jnp.mean(xf * xf, axis=-1, keepdims=True) + EPS)
    return (y * g.astype(F32)).astype(x.dtype)


def partial_rope(x, pos):
    half = ROPE_DIM // 2
    inv = ROPE_THETA ** (-jnp.arange(half, dtype=F32) * 2.0 / ROPE_DIM)
    ang = pos.astype(F32)[:, None] * inv[None, :]
    cos = jnp.cos(ang)[:, None, :]
    sin = jnp.sin(ang)[:, None, :]
    xr = x[..., :ROPE_DIM].astype(F32)
    x1, x2 = xr[..., :half], xr[..., half:]
    rot = jnp.concatenate([x1 * cos - x2 * sin, x2 * cos + x1 * sin], axis=-1).astype(x.dtype)
    return jnp.concatenate([rot, x[..., ROPE_DIM:]], axis=-1)


def sink_softmax(s, sink):
    m = jnp.maximum(jnp.max(s, axis=-1, keepdims=True), sink)
    e = jnp.exp(s - m)
    return e / (jnp.sum(e, axis=-1, keepdims=True) + jnp.exp(sink - m))


def swa_banded(q, k, v, sink):
    B, T = q.shape[:2]
    W, KV, G, hd = SWA_WINDOW, SWA_KV_HEADS, SWA_GROUP, SWA_HEAD_DIM
    nb = T // W
    qb = q.reshape(B, nb, W, KV, G, hd)

    def band(z):
        prev = jnp.concatenate([jnp.zeros_like(z[:, :W]), z[:, :-W]], axis=1)
        return jnp.concatenate([prev.reshape(B, nb, W, KV, hd), z.reshape(B, nb, W, KV, hd)], axis=2)

    kb, vb = band(k), band(v)
    s = jnp.einsum('bnqkgd,bnskd->bnkgqs', qb.astype(F32), kb.astype(F32)) * SWA_HEAD_DIM ** -0.5
    qi = jnp.arange(W)[:, None] + W
    ki = jnp.arange(2 * W)[None, :]
    rel = qi - ki
    band_ok = (rel >= 0) & (rel <= W)
    blk = jnp.arange(nb)[:, None, None]
    mask = band_ok[None] & ((blk > 0) | (ki >= W)[None])
    s = jnp.where(mask[None, :, None, None], s, NEG)
    pr = sink_softmax(s, sink.astype(F32).reshape(KV, G)[None, None, :, :, None, None])
    o = jnp.einsum('bnkgqs,bnskd->bnqkgd', pr.astype(v.dtype), vb)
    return o.reshape(B, T, SWA_Q)


def swa_step(q, k, v, k_buf, v_buf, sink):
    B, S = q.shape[:2]
    W, KV, G = k_buf.shape[1], SWA_KV_HEADS, SWA_GROUP
    kk = jnp.concatenate([k_buf.astype(k.dtype), k], axis=1)
    vv = jnp.concatenate([v_buf.astype(v.dtype), v], axis=1)
    qg = q.reshape(B, S, KV, G, SWA_HEAD_DIM)
    s = jnp.einsum('bqkgd,bskd->bkgqs', qg.astype(F32), kk.astype(F32)) * SWA_HEAD_DIM ** -0.5
    rel = (jnp.arange(S)[:, None] + W) - jnp.arange(W + S)[None, :]
    mask = (rel >= 0) & (rel <= SWA_WINDOW)
    s = jnp.where(mask, s, NEG)
    pr = sink_softmax(s, sink.astype(F32).reshape(KV, G)[None, :, :, None, None])
    o = jnp.einsum('bkgqs,bskd->bqkgd', pr.astype(vv.dtype), vv).reshape(B, S, SWA_Q)
    return o, kk[:, -SWA_WINDOW:], vv[:, -SWA_WINDOW:]


def causal_dwconv(x, buf, w, b):
    K = w.shape[0]
    T = x.shape[1]
    xp = jnp.concatenate([buf.astype(x.dtype), x], axis=1)
    y = b + sum(xp[:, j:j + T] * w[j] for j in range(K))
    return y, xp[:, T:]


def _lin_combine(left, right):
    a_l, b_l = left
    a_r, b_r = right
    return a_l * a_r, a_r * b_l + b_r


def rg_lru(x, h0, wa, ba, wx, bx, lam):
    B, T, _ = x.shape
    xf = x.astype(F32)
    xb = xf.reshape(B, T, LRU_BLOCKS, LRU_BLOCK_W)
    r = jax.nn.sigmoid(jnp.einsum('btni,nij->btnj', xb, wa.astype(F32)).reshape(B, T, LRU_WIDTH) + ba.astype(F32))
    i = jax.nn.sigmoid(jnp.einsum('btni,nij->btnj', xb, wx.astype(F32)).reshape(B, T, LRU_WIDTH) + bx.astype(F32))
    log_a = -LRU_C * r * jax.nn.softplus(-lam.astype(F32))
    a = jnp.exp(log_a)
    b = jnp.sqrt(-jnp.expm1(2.0 * log_a)) * (i * xf)
    b = b.at[:, 0].add(a[:, 0] * h0.astype(F32))
    _, h = lax.associative_scan(_lin_combine, (a, b), axis=1)
    return h.astype(x.dtype), h[:, -1].astype(x.dtype)


def gla(q, k, v, log_alpha, s0):
    B, T, H, DK = q.shape
    DV = v.shape[-1]
    C = GLA_CHUNK if T % GLA_CHUNK == 0 else T
    n = T // C

    def to_chunks(z):
        return z.astype(F32).reshape(B, n, C, H, z.shape[-1]).transpose(1, 0, 3, 2, 4)

    qc, kc, vc, gc = to_chunks(q), to_chunks(k), to_chunks(v), to_chunks(log_alpha)
    causal = jnp.tril(jnp.ones((C, C), dtype=bool))

    def step(S, inp):
        qi, ki, vi, gi = inp
        cum = jnp.cumsum(gi, axis=2)
        q_t = qi * jnp.exp(cum)
        k_t = ki * jnp.exp(-cum)
        att = jnp.where(causal, jnp.einsum('bhtd,bhsd->bhts', q_t, k_t), 0.0)
        o = jnp.einsum('bhts,bhsv->bhtv', att, vi) + jnp.einsum('bhtd,bhdv->bhtv', q_t, S)
        last = cum[:, :, -1:, :]
        S_new = jnp.exp(last[:, :, 0, :])[..., None] * S + jnp.einsum('bhsd,bhsv->bhdv', ki * jnp.exp(last - cum), vi)
        return S_new, o

    S, o = lax.scan(step, s0.astype(F32), (qc, kc, vc, gc))
    o = o.transpose(1, 0, 3, 2, 4).reshape(B, T, H, DV)
    return o, S.astype(s0.dtype)


def mixer_block(x, pos, p, prompt, k_buf, v_buf, lru_h, lru_buf, gla_s):
    B, T, _ = x.shape
    h = rmsnorm(x, p['norm_mix_g'])
    offs = np.cumsum(IN_SPLITS)[:-1].tolist()
    (q_a, k_a, v_a, x_b, y_b, q_c, k_c, v_c, r_c, a_c, g_a, g_b, g_c) = jnp.split(h @ p['w_in'], offs, axis=-1)

    q = partial_rope(rmsnorm(q_a.reshape(B, T, SWA_HEADS, SWA_HEAD_DIM), p['swa_qn_g']), pos)
    k = partial_rope(rmsnorm(k_a.reshape(B, T, SWA_KV_HEADS, SWA_HEAD_DIM), p['swa_kn_g']), pos)
    v = v_a.reshape(B, T, SWA_KV_HEADS, SWA_HEAD_DIM)
    if prompt:
        o_a = swa_banded(q, k, v, p['swa_sink'])
        new_kb, new_vb = k[:, -SWA_WINDOW:], v[:, -SWA_WINDOW:]
    else:
        o_a, new_kb, new_vb = swa_step(q, k, v, k_buf, v_buf, p['swa_sink'])

    xc, new_lru_buf = causal_dwconv(x_b, lru_buf, p['lru_conv_w'], p['lru_conv_b'])
    hr, new_h = rg_lru(xc, lru_h, p['lru_wa'], p['lru_ba'], p['lru_wx'], p['lru_bx'], p['lru_lambda'])
    o_b = jax.nn.gelu(y_b) * hr

    log_alpha = jax.nn.log_sigmoid((a_c @ p['gla_wa2'] + p['gla_ba']).astype(F32)) / GLA_TAU
    o, new_s = gla(q_c.reshape(B, T, GLA_HEADS, GLA_DK) * GLA_DK ** -0.5,
                   k_c.reshape(B, T, GLA_HEADS, GLA_DK),
                   v_c.reshape(B, T, GLA_HEADS, GLA_DV),
                   log_alpha.reshape(B, T, GLA_HEADS, GLA_DK), gla_s)
    o_c = rmsnorm(o, p['gla_on_g']).astype(x.dtype).reshape(B, T, GLA_V) * jax.nn.silu(r_c)

    merged = (jax.nn.sigmoid(g_a) * (o_a @ p['w_branch_a'])
              + jax.nn.sigmoid(g_b) * (o_b @ p['w_branch_b'])
              + jax.nn.sigmoid(g_c) * (o_c @ p['w_branch_c']))
    return merged @ p['w_out'], (new_kb, new_vb, new_h, new_lru_buf, new_s)


def memory_kv(mem, p):
    B = mem.shape[0]
    m = rmsnorm(mem, p['norm_mem_g'])
    k = rmsnorm((m @ p['x_wk']).reshape(B, -1, MEM_HEADS, MEM_HEAD_DIM), p['x_kn_g'])
    v = (m @ p['x_wv']).reshape(B, -1, MEM_HEADS, MEM_HEAD_DIM)
    return k, v


def cross_attn(h, mk, mv, p):
    B, T, _ = h.shape
    q = rmsnorm((h @ p['x_wq']).reshape(B, T, MEM_HEADS, MEM_HEAD_DIM), p['x_qn_g'])
    s = jnp.einsum('bthd,bshd->bhts', q.astype(F32), mk.astype(F32)) * MEM_HEAD_DIM ** -0.5
    pr = jax.nn.softmax(s, axis=-1)
    o = jnp.einsum('bhts,bshd->bthd', pr.astype(mv.dtype), mv).reshape(B, T, MEM_W)
    return o @ p['x_wo']


def conv_ffn(h, buf, p):
    u = h @ p['ffn_w_up']
    u, new_buf = causal_dwconv(u, buf, p['ffn_conv_w'], p['ffn_conv_b'])
    g, val = jnp.split(u, 2, axis=-1)
    return (jax.nn.silu(g) * val) @ p['ffn_w_down'], new_buf


def decoder_layer(x, pos, mk, mv, p, prompt, k_buf, v_buf, lru_h, lru_buf, gla_s, ffn_buf):
    mix, (nkb, nvb, nh, nlb, ns) = mixer_block(x, pos, p, prompt, k_buf, v_buf, lru_h, lru_buf, gla_s)
    x = x + mix
    x = x + cross_attn(rmsnorm(x, p['norm_x_g']), mk, mv, p)
    f, nfb = conv_ffn(rmsnorm(x, p['norm_ffn_g']), ffn_buf, p)
    x = x + f
    return x, (nkb, nvb, nh, nlb, ns, nfb)


def setup_inputs(seed: int = 0) -> dict:
    key = jax.random.key(seed)
    keys = iter(jax.random.split(key, 64))

    def nrm(shape, scale=1.0):
        return jax.random.normal(next(keys), shape, F32) * scale

    def gain(shape):
        return 1.0 + nrm(shape, 0.02)

    L, D = DEPTH, D_MODEL
    u = jax.random.uniform(next(keys), (L, LRU_WIDTH), F32, 0.9, 0.999)
    a = u ** (1.0 / LRU_C)
    return {
        'x_prompt': nrm((BATCH, SEQ, D)),
        'x_sample': nrm((DEC_BATCH, DEC_SEQ, D)),
        'cache_swa_k': nrm((L, DEC_BATCH, SWA_WINDOW, SWA_KV_HEADS, SWA_HEAD_DIM)),
        'cache_swa_v': nrm((L, DEC_BATCH, SWA_WINDOW, SWA_KV_HEADS, SWA_HEAD_DIM)),
        'state_lru_h': nrm((L, DEC_BATCH, LRU_WIDTH), 0.5),
        'state_lru_conv': nrm((L, DEC_BATCH, LRU_CONV_W - 1, LRU_WIDTH)),
        'state_gla_s': nrm((L, DEC_BATCH, GLA_HEADS, GLA_DK, GLA_DV), 0.5),
        'cache_mem_k': nrm((L, DEC_BATCH, N_MEM, MEM_HEADS, MEM_HEAD_DIM)),
        'cache_mem_v': nrm((L, DEC_BATCH, N_MEM, MEM_HEADS, MEM_HEAD_DIM)),
        'state_ffn_conv': nrm((L, DEC_BATCH, FFN_CONV_W - 1, 2 * D_FF)),
        'mem_prompt': nrm((BATCH, N_MEM, D)),
        'norm_mix_g': gain((L, D)),
        'w_in': nrm((L, D, IN_COLS), D ** -0.5),
        'swa_qn_g': gain((L, SWA_HEAD_DIM)),
        'swa_kn_g': gain((L, SWA_HEAD_DIM)),
        'swa_sink': nrm((L, SWA_HEADS), 0.5),
        'lru_conv_w': nrm((L, LRU_CONV_W, LRU_WIDTH), LRU_CONV_W ** -0.5),
        'lru_conv_b': nrm((L, LRU_WIDTH), 0.01),
        'lru_wa': nrm((L, LRU_BLOCKS, LRU_BLOCK_W, LRU_BLOCK_W), LRU_BLOCK_W ** -0.5),
        'lru_ba': nrm((L, LRU_WIDTH), 0.01),
        'lru_wx': nrm((L, LRU_BLOCKS, LRU_BLOCK_W, LRU_BLOCK_W), LRU_BLOCK_W ** -0.5),
        'lru_bx': nrm((L, LRU_WIDTH), 0.01),
        'lru_lambda': jnp.log(a) - jnp.log1p(-a),
        'gla_wa2': nrm((L, GLA_RANK, GLA_K), GLA_RANK ** -0.5),
        'gla_ba': nrm((L, GLA_K), 0.01),
        'gla_on_g': gain((L, GLA_DV)),
        'w_branch_a': nrm((L, SWA_Q, D), SWA_Q ** -0.5),
        'w_branch_b': nrm((L, LRU_WIDTH, D), LRU_WIDTH ** -0.5),
        'w_branch_c': nrm((L, GLA_V, D), GLA_V ** -0.5),
        'w_out': nrm((L, D, D), D ** -0.5),
        'norm_x_g': gain((L, D)),
        'norm_mem_g': gain((L, D)),
        'x_wq': nrm((L, D, MEM_W), D ** -0.5),
        'x_wk': nrm((L, D, MEM_W), D ** -0.5),
        'x_wv': nrm((L, D, MEM_W), D ** -0.5),
        'x_qn_g': gain((L, MEM_HEAD_DIM)),
        'x_kn_g': gain((L, MEM_HEAD_DIM)),
        'x_wo': nrm((L, MEM_W, D), MEM_W ** -0.5),
        'norm_ffn_g': gain((L, D)),
        'ffn_w_up': nrm((L, D, 2 * D_FF), D ** -0.5),
        'ffn_conv_w': nrm((L, FFN_CONV_W, 2 * D_FF), FFN_CONV_W ** -0.5),
        'ffn_conv_b': nrm((L, 2 * D_FF), 0.01),
        'ffn_w_down': nrm((L, D_FF, D), D_FF ** -0.5),
    }


def reference(x_prompt, x_sample, cache_swa_k, cache_swa_v, state_lru_h, state_lru_conv, state_gla_s,
              cache_mem_k, cache_mem_v, state_ffn_conv, mem_prompt,
              norm_mix_g, w_in, swa_qn_g, swa_kn_g, swa_sink, lru_conv_w, lru_conv_b, lru_wa, lru_ba,
              lru_wx, lru_bx, lru_lambda, gla_wa2, gla_ba, gla_on_g, w_branch_a, w_branch_b, w_branch_c,
              w_out, norm_x_g, norm_mem_g, x_wq, x_wk, x_wv, x_qn_g, x_kn_g, x_wo, norm_ffn_g,
              ffn_w_up, ffn_conv_w, ffn_conv_b, ffn_w_down):
    P = dict(norm_mix_g=norm_mix_g, w_in=w_in, swa_qn_g=swa_qn_g, swa_kn_g=swa_kn_g, swa_sink=swa_sink,
             lru_conv_w=lru_conv_w, lru_conv_b=lru_conv_b, lru_wa=lru_wa, lru_ba=lru_ba, lru_wx=lru_wx,
             lru_bx=lru_bx, lru_lambda=lru_lambda, gla_wa2=gla_wa2, gla_ba=gla_ba, gla_on_g=gla_on_g,
             w_branch_a=w_branch_a, w_branch_b=w_branch_b, w_branch_c=w_branch_c, w_out=w_out,
             norm_x_g=norm_x_g, norm_mem_g=norm_mem_g, x_wq=x_wq, x_wk=x_wk, x_wv=x_wv, x_qn_g=x_qn_g,
             x_kn_g=x_kn_g, x_wo=x_wo, norm_ffn_g=norm_ffn_g, ffn_w_up=ffn_w_up, ffn_conv_w=ffn_conv_w,
             ffn_conv_b=ffn_conv_b, ffn_w_down=ffn_w_down)
    Bp, Tp = x_prompt.shape[:2]
    Ts = x_sample.shape[1]
    dt = x_prompt.dtype
    pos_p = jnp.arange(Tp, dtype=jnp.int32)
    pos_s = PAST_LEN + jnp.arange(Ts, dtype=jnp.int32)
    xp, xs = x_prompt, x_sample
    sp, ss, mks, mvs = [], [], [], []
    for l in range(DEPTH):
        p = {name: arr[l] for name, arr in P.items()}
        mk, mv = memory_kv(mem_prompt, p)
        xp, st = decoder_layer(xp, pos_p, mk, mv, p, True, None, None,
                               jnp.zeros((Bp, LRU_WIDTH), dt),
                               jnp.zeros((Bp, LRU_CONV_W - 1, LRU_WIDTH), dt),
                               jnp.zeros((Bp, GLA_HEADS, GLA_DK, GLA_DV), dt),
                               jnp.zeros((Bp, FFN_CONV_W - 1, 2 * D_FF), dt))
        sp.append(st)
        mks.append(mk)
        mvs.append(mv)
        xs, st = decoder_layer(xs, pos_s, cache_mem_k[l], cache_mem_v[l], p, False,
                               cache_swa_k[l], cache_swa_v[l], state_lru_h[l], state_lru_conv[l],
                               state_gla_s[l], state_ffn_conv[l])
        ss.append(st)
    return (xp, xs,
            jnp.stack([s[0] for s in sp]), jnp.stack([s[1] for s in sp]), jnp.stack([s[2] for s in sp]),
            jnp.stack([s[3] for s in sp]), jnp.stack([s[4] for s in sp]), jnp.stack(mks), jnp.stack(mvs),
            jnp.stack([s[5] for s in sp]),
            jnp.stack([s[0] for s in ss]), jnp.stack([s[1] for s in ss]), jnp.stack([s[2] for s in ss]),
            jnp.stack([s[3] for s in ss]), jnp.stack([s[4] for s in ss]), jnp.stack([s[5] for s in ss]))
```

```python
import numpy as np
import concourse.bass as bass
import concourse.mybir as mybir
from concourse.bass_utils import run_bass_kernel_spmd

F32 = mybir.dt.float32
F32R = mybir.dt.float32r
FAST_MM = True
ALU = mybir.AluOpType
AF = mybir.ActivationFunctionType

D = 1024
DEPTH = 2
PAST_LEN = 16384
SWA_W = 128
ROPE_DIM = 16
ROPE_THETA = 500000.0
LRU_W = 512
GLA_K = 256
GLA_V = 512
N_MEM = 256
D_FF = 2816
EPS = 1e-6
O_QA, O_KA, O_VA, O_XB, O_YB, O_QC, O_KC, O_VC, O_RC, O_AC, O_GA, O_GB, O_GC = (
    0, 512, 640, 768, 1280, 1792, 2048, 2304, 2816, 3328, 3344, 4368, 5392)
IN_COLS = 6416
NCH_FF = D_FF // 128

ENGS = ("pe", "act", "dve", "pool", "sp")
CENG = {"pe": 0, "act": 1, "dve": 2, "pool": 3, "sp": 4}


class Buf:
    __slots__ = ("t", "name", "lw", "rd", "rnd", "pool")

    def __init__(self, t, name, rnd=False):
        self.t = t
        self.name = name
        self.lw = None
        self.rd = []
        self.rnd = rnd
        self.pool = None

    def __getitem__(self, idx):
        return V(self, self.t[idx])

    @property
    def a(self):
        return V(self, self.t[:])


class V:
    __slots__ = ("b", "ap")

    def __init__(self, b, ap):
        self.b = b
        self.ap = ap

    def __getitem__(self, idx):
        return V(self.b, self.ap[idx])

    def re(self, pat, **kw):
        return V(self.b, self.ap.rearrange(pat, **kw))


def _ap(x):
    if isinstance(x, V):
        if FAST_MM and x.b is not None and x.b.rnd:
            return x.ap.bitcast(F32)
        return x.ap
    return x


def RND(v):
    return v


def _out(x):
    return x.ap if isinstance(x, V) else x


def _isr(x):
    return isinstance(x, V) and x.b is not None and x.b.rnd


def _bufs(xs):
    out = []
    for x in xs:
        if isinstance(x, V):
            if x.b is not None:
                out.append(x.b)
        elif isinstance(x, Buf):
            out.append(x)
    return out


class Op:
    __slots__ = ("eng", "fn", "deps", "is_dma", "idx", "pos", "waits", "dmaslot", "tag")

    def __init__(self, eng, fn, deps, is_dma):
        self.eng = eng
        self.fn = fn
        self.deps = deps
        self.is_dma = is_dma
        self.waits = []
        self.dmaslot = None


class Prog:
    def __init__(self, nc, dma_ring=24):
        self.nc = nc
        self.ops = []
        self.by_eng = {e: [] for e in ENGS}
        self.dma_ring = dma_ring
        self.final_wait = []
        self._stack = []
        self.tag = ""

    def sb(self, name, shape, dtype=F32, rnd=False):
        if rnd and FAST_MM:
            dtype = F32R
        cm = self.nc.sbuf_tensor(name, list(shape), dtype)
        t = cm.__enter__()
        self._stack.append(cm)
        return Buf(t, name, rnd)

    def ps(self, name, shape, dtype=F32):
        cm = self.nc.psum_tensor(name, list(shape), dtype)
        t = cm.__enter__()
        self._stack.append(cm)
        return Buf(t, name)

    def op(self, eng, fn, reads=(), writes=(), is_dma=False, out_dma=False):
        rb = _bufs(reads)
        wb = _bufs(writes)
        deps = set()
        for b in rb:
            if b.lw is not None:
                deps.add(b.lw)
        for b in wb:
            if b.lw is not None:
                deps.add(b.lw)
            deps.update(b.rd)
        o = Op(eng, fn, deps, is_dma)
        o.tag = self.tag
        o.idx = len(self.ops)
        self.ops.append(o)
        o.pos = len(self.by_eng[eng])
        self.by_eng[eng].append(o)
        for b in rb:
            b.rd.append(o.idx)
        for b in wb:
            b.lw = o.idx
            b.rd = []
        if out_dma:
            self.final_wait.append(o.idx)
        return o

    def mm(self, out, lhsT, rhs, start=True, stop=True, fast=None):
        o, l, r = _ap(out), _ap(lhsT), _ap(rhs)
        if fast is None:
            fast = FAST_MM
        if fast and _isr(lhsT) and _isr(rhs) and o.start_partition() == 0 and (r.shape[-1] % 2 == 0):
            l = lhsT.ap
            r = rhs.ap
        return self.op("pe", lambda e: e.matmul(o, l, r, start=start, stop=stop),
                       reads=[lhsT, rhs], writes=[out])

    def transpose(self, out, in_, ident):
        o, i, d = _ap(out), _ap(in_), _ap(ident)
        op = self.op("pe", lambda e: e.transpose(o, i, d), reads=[in_, ident], writes=[out])
        op.tag = op.tag + "|T"
        return op

    def act(self, out, in_, func, bias=None, scale=None):
        o, i = _out(out), _ap(in_)
        kw = {}
        rd = [in_]
        if bias is not None:
            kw["bias"] = _ap(bias)
            rd.append(bias)
        if scale is not None:
            kw["scale"] = _ap(scale)
            rd.append(scale)
        return self.op("act", lambda e: e.activation(o, i, func, **kw), reads=rd, writes=[out])

    def tt(self, out, in0, in1, op, eng="dve"):
        o, a, b = _out(out), _ap(in0), _ap(in1)
        return self.op(eng, lambda e: e.tensor_tensor(o, a, b, op), reads=[in0, in1], writes=[out])

    def ts(self, out, in0, s1, op0, s2=None, op1=None, eng="dve"):
        o, a = _out(out), _ap(in0)
        rd = [in0, s1, s2]
        s1a, s2a = _ap(s1), _ap(s2)
        if op1 is None:
            return self.op(eng, lambda e: e.tensor_scalar(o, a, s1a, None, op0), reads=rd, writes=[out])
        return self.op(eng, lambda e: e.tensor_scalar(o, a, s1a, s2a, op0, op1), reads=rd, writes=[out])

    def stt(self, out, in0, scalar, in1, op0, op1):
        o, a, b = _out(out), _ap(in0), _ap(in1)
        s = _ap(scalar)
        return self.op("dve", lambda e: e.scalar_tensor_tensor(o, a, s, b, op0, op1),
                       reads=[in0, in1, scalar], writes=[out])

    def scan(self, out, d0, d1, initial, op0=ALU.mult, op1=ALU.add):
        o, a, b = _ap(out), _ap(d0), _ap(d1)
        ini = _ap(initial)
        return self.op("dve", lambda e: e.tensor_tensor_scan(o, a, b, ini, op0, op1),
                       reads=[d0, d1, initial], writes=[out])

    def copy(self, out, in_, eng="act"):
        o, i = _out(out), _ap(in_)
        if eng == "act":
            return self.op("act", lambda e: e.copy(o, i), reads=[in_], writes=[out])
        return self.op(eng, lambda e: e.tensor_copy(o, i), reads=[in_], writes=[out])

    def recip(self, out, in_):
        o, i = _ap(out), _ap(in_)
        return self.op("dve", lambda e: e.reciprocal(o, i), reads=[in_], writes=[out])

    def memset(self, out, val, eng="pool"):
        o = _out(out)
        return self.op(eng, lambda e: e.memset(o, val), reads=[], writes=[out])

    def dma(self, out, in_, eng="sp", out_dma=False):
        o, i = _out(out), _out(in_)
        return self.op(eng, lambda e: e.dma_start(out=o, in_=i), reads=[in_], writes=[out],
                       is_dma=True, out_dma=out_dma)

    def emit(self):
        nc = self.nc
        ops = self.ops
        NE = len(ENGS)
        K = self.dma_ring
        needed = set()
        for o in ops:
            needed.update(o.deps)
        needed.update(self.final_wait)
        sems = {}
        for e in ENGS:
            cm = nc.semaphore("s_" + e)
            sems[e] = cm.__enter__()
            self._stack.append(cm)
        rings = {}
        for e in ENGS:
            if any(o.is_dma for o in self.by_eng[e]):
                r = []
                for k in range(K):
                    cm = nc.semaphore("d_%s_%d" % (e, k))
                    r.append(cm.__enter__())
                    self._stack.append(cm)
                rings[e] = r
        cnt = {e: 0 for e in ENGS}
        dcnt = {e: 0 for e in ENGS}
        token = {}
        for e in ENGS:
            for o in self.by_eng[e]:
                if o.is_dma:
                    n = dcnt[e]
                    dcnt[e] += 1
                    o.dmaslot = n
                    token[o.idx] = (rings[e][n % K], 16 * (n // K + 1))
                elif o.idx in needed:
                    cnt[e] += 1
                    token[o.idx] = (sems[e], cnt[e])
        known = {e: [-1] * NE for e in ENGS}
        known_dma = {e: set() for e in ENGS}
        vcs = [None] * len(ops)
        for o in ops:
            e = o.eng
            kn = known[e]
            waits = {}
            for d in sorted(o.deps):
                p = ops[d]
                if p.is_dma:
                    if d in known_dma[e]:
                        continue
                    known_dma[e].add(d)
                    s, v = token[d]
                else:
                    f = CENG[p.eng]
                    if p.eng == "pe" and e == "pe":
                        continue
                    if kn[f] >= p.pos:
                        continue
                    s, v = token[d]
                    if kn[f] < p.pos:
                        kn[f] = p.pos
                pv = vcs[d]
                for i in range(NE):
                    if pv[i] > kn[i]:
                        kn[i] = pv[i]
                k = id(s)
                if k not in waits or waits[k][1] < v:
                    waits[k] = (s, v)
            o.waits = list(waits.values())
            v = list(kn)
            if not o.is_dma:
                v[CENG[e]] = o.pos
            vcs[o.idx] = v
        by_eng = self.by_eng
        fb = {}
        for i in self.final_wait:
            s, v = token[i]
            k = id(s)
            if k not in fb or fb[k][1] < v:
                fb[k] = (s, v)
        final = list(fb.values())
        self.n_waits = sum(len(o.waits) for o in ops)

        def run(eng_name):
            def body(e):
                for o in by_eng[eng_name]:
                    if o.is_dma and o.dmaslot >= K:
                        e.wait_ge(rings[eng_name][o.dmaslot % K], 16 * (o.dmaslot // K))
                    for (s, v) in o.waits:
                        e.wait_ge(s, v)
                    ins = o.fn(e)
                    if o.idx in token:
                        ins.then_inc(token[o.idx][0], 16 if o.is_dma else 1)
                if eng_name == "sp":
                    for (s, v) in final:
                        e.wait_ge(s, v)
            return body

        with nc.Block() as block:
            block.tensor(run("pe"))
            block.scalar(run("act"))
            block.vector(run("dve"))
            block.gpsimd(run("pool"))
            block.sync(run("sp"))

    def close(self):
        while self._stack:
            cm = self._stack.pop()
            cm.__exit__(None, None, None)


class Pool:
    def __init__(self, P, name, n, shape, rnd=False):
        self.free_list = [P.sb("%s%d" % (name, i), shape, rnd=rnd) for i in range(n)]
        for b in self.free_list:
            b.pool = self
        self.name = name
        self.n = n

    def alloc(self):
        assert self.free_list, "pool %s exhausted" % self.name
        return self.free_list.pop(0)

    def free(self, *bs):
        for b in bs:
            b.pool.free_list.append(b)


def host_consts(cfg):
    T, SEQ, NSq = cfg["T"], cfg["SEQ"], cfg["NSq"]
    NST = 4 * NSq
    c = {}
    c["ident"] = np.eye(128, dtype=np.float32)
    c["ones"] = np.ones((128, 128), np.float32)
    bd = np.zeros((128, 128), np.float32)
    bd[:64, :64] = 1
    bd[64:, 64:] = 1
    c["bd64"] = bd
    prot = np.zeros((128, 128), np.float32)
    for m in range(128):
        d = m % 64
        if d < 8:
            prot[m + 8, m] = 1
        elif d < 16:
            prot[m - 8, m] = 1
    c["prot"] = prot
    s = np.arange(128)[:, None]
    t = np.arange(128)[None, :]
    mtri = (s <= t).astype(np.float32)
    c["mtri"] = mtri
    c["umat"] = (s > t).astype(np.float32)
    c["mcur4"] = np.tile(mtri, (1, 4))
    c["mprev4"] = np.tile((s >= t).astype(np.float32), (1, 4))
    tok_seq = np.arange(NST) // 4
    tok_t = np.arange(NST) % 4
    same = tok_seq[:, None] == tok_seq[None, :]
    ms = np.zeros((128, 128), np.float32)
    ms[:NST, :NST] = (same & (tok_t[:, None] <= tok_t[None, :]))
    us = np.zeros((128, 128), np.float32)
    us[:NST, :NST] = (same & (tok_t[:, None] > tok_t[None, :]))
    c["ms"] = ms
    c["us"] = us
    col_seq = np.repeat(np.arange(NSq), 16)
    col_t = np.tile(np.arange(4), NSq * 4)
    msc = np.zeros((128, 256), np.float32)
    msc[:, :NSq * 16] = (np.arange(128)[:, None] >= col_t[None, :])
    c["msc"] = msc
    msn = np.zeros((128, 256), np.float32)
    msn[:NST, :NSq * 16] = ((tok_seq[:, None] == col_seq[None, :]) & (tok_t[:, None] <= col_t[None, :]))
    c["msn"] = msn
    rm = np.zeros((128, 16), np.float32)
    rm[np.arange(NST), tok_seq] = 1
    c["rowmask"] = rm
    half = ROPE_DIM // 2
    inv = (ROPE_THETA ** (-np.arange(half, dtype=np.float32) * np.float32(2.0) / np.float32(ROPE_DIM))).astype(np.float32)

    def tables(pos):
        ang = pos.astype(np.float32)[None, :] * inv[:, None]
        cs, sn = np.cos(ang).astype(np.float32), np.sin(ang).astype(np.float32)
        C = np.ones((128, pos.shape[0]), np.float32)
        S = np.zeros((128, pos.shape[0]), np.float32)
        for hh in range(2):
            C[64 * hh:64 * hh + 8] = cs
            C[64 * hh + 8:64 * hh + 16] = cs
            S[64 * hh:64 * hh + 8] = -sn
            S[64 * hh + 8:64 * hh + 16] = sn
        return C, S
    c["cosp"], c["sinp"] = tables(np.arange(SEQ))
    cs_, ss_ = tables(PAST_LEN + np.arange(4))
    c["coss"] = np.tile(cs_, (1, NSq))
    c["sins"] = np.tile(ss_, (1, NSq))
    return c


RND_C = {"ones", "bd64", "prot", "mtri", "umat", "ms", "us"}
RND_W = {"lru_wa", "lru_wx", "w_in", "gla_wa2", "gla_ba", "w_branch_a", "w_branch_b", "w_branch_c", "w_out", "x_wq", "x_wk", "x_wv",
         "x_wo", "ffn_w_up", "ffn_w_down"}
WEIGHT_SHAPES = {
    "norm_mix_g": (DEPTH, D), "w_in": (DEPTH, D, IN_COLS), "swa_qn_g": (DEPTH, 64), "swa_kn_g": (DEPTH, 64),
    "swa_sink": (DEPTH, 8), "lru_conv_w": (DEPTH, 4, 512), "lru_conv_b": (DEPTH, 512),
    "lru_wa": (DEPTH, 8, 64, 64), "lru_ba": (DEPTH, 512), "lru_wx": (DEPTH, 8, 64, 64), "lru_bx": (DEPTH, 512),
    "lru_lambda": (DEPTH, 512), "gla_wa2": (DEPTH, 16, 256), "gla_ba": (DEPTH, 256), "gla_on_g": (DEPTH, 128),
    "w_branch_a": (DEPTH, 512, D), "w_branch_b": (DEPTH, 512, D), "w_branch_c": (DEPTH, 512, D),
    "w_out": (DEPTH, D, D), "norm_x_g": (DEPTH, D), "norm_mem_g": (DEPTH, D), "x_wq": (DEPTH, D, 512),
    "x_wk": (DEPTH, D, 512), "x_wv": (DEPTH, D, 512), "x_qn_g": (DEPTH, 128), "x_kn_g": (DEPTH, 128),
    "x_wo": (DEPTH, 512, D), "norm_ffn_g": (DEPTH, D), "ffn_w_up": (DEPTH, D, 2 * D_FF),
    "ffn_conv_w": (DEPTH, 3, 2 * D_FF), "ffn_conv_b": (DEPTH, 2 * D_FF), "ffn_w_down": (DEPTH, D_FF, D),
}


def build(cfg):
    T, SEQ, NP, NSq = cfg["T"], cfg["SEQ"], cfg["NP"], cfg["NSq"]
    NST = 4 * NSq
    NB = T // 128
    assert SEQ % T == 0 and T % 128 == 0
    nc = bass.Bass("TRN2", target_bir_lowering=False)
    if FAST_MM:
        nc.dge_precook = False
    P = Prog(nc)
    L = DEPTH

    def din(name, shape):
        return nc.dram_tensor(name, list(shape), F32, kind="ExternalInput").ap()

    def dout(name, shape):
        return nc.dram_tensor(name, list(shape), F32, kind="ExternalOutput").ap()

    xp = din("x_prompt", [NP, SEQ, D])
    xs = din("x_sample", [NSq, 4, D])
    csk = din("cache_swa_k", [L, NSq, 128, 2, 64])
    csv = din("cache_swa_v", [L, NSq, 128, 2, 64])
    slh = din("state_lru_h", [L, NSq, 512])
    slc = din("state_lru_conv", [L, NSq, 3, 512])
    sgs = din("state_gla_s", [L, NSq, 4, 64, 128])
    cmk = din("cache_mem_k", [L, NSq, 256, 4, 128])
    cmv = din("cache_mem_v", [L, NSq, 256, 4, 128])
    sfc = din("state_ffn_conv", [L, NSq, 2, 2 * D_FF])
    memp = din("mem_prompt", [NP, N_MEM, D])
    def dinr(name, shape):
        return nc.dram_tensor(name, list(shape), F32R if FAST_MM else F32, kind="ExternalInput").ap()

    Wd = {k: (dinr(k, s) if k in RND_W else din(k, s)) for k, s in WEIGHT_SHAPES.items()}
    hc = host_consts(cfg)
    Cd = {k: (dinr("c_" + k, v.shape) if k in RND_C else din("c_" + k, v.shape)) for k, v in hc.items()}

    y_p = dout("y_prompt", [NP, SEQ, D])
    y_s = dout("y_sample", [NSq, 4, D])
    o_pk = dout("p_swa_k", [L, NP, 128, 2, 64])
    o_pv = dout("p_swa_v", [L, NP, 128, 2, 64])
    o_ph = dout("p_lru_h", [L, NP, 512])
    o_pc = dout("p_lru_conv", [L, NP, 3, 512])
    o_ps = dout("p_gla_s", [L, NP, 4, 64, 128])
    o_pmk = dout("p_mem_k", [L, NP, 256, 4, 128])
    o_pmv = dout("p_mem_v", [L, NP, 256, 4, 128])
    o_pf = dout("p_ffn_conv", [L, NP, 2, 2 * D_FF])
    o_sk = dout("s_swa_k", [L, NSq, 128, 2, 64])
    o_sv = dout("s_swa_v", [L, NSq, 128, 2, 64])
    o_sh = dout("s_lru_h", [L, NSq, 512])
    o_sc = dout("s_lru_conv", [L, NSq, 3, 512])
    o_ss = dout("s_gla_s", [L, NSq, 4, 64, 128])
    o_sf = dout("s_ffn_conv", [L, NSq, 2, 2 * D_FF])

    wk = Pool(P, "wk", cfg.get("n_wk", 18), [128, T])
    wkr = Pool(P, "wkr", cfg.get("n_wkr", 31), [128, T], rnd=True)
    wk5 = Pool(P, "wk5", cfg.get("n_wk5", 4), [128, 512])
    wk5r = Pool(P, "wk5r", cfg.get("n_wk5r", 7), [128, 512], rnd=True)
    wpool = Pool(P, "wt", cfg.get("n_wt", 5), [128, 2048], rnd=True)
    psb = [P.ps("psb%d" % i, [128, 512]) for i in range(8)]
    ps_free = list(psb)

    def psum():
        assert ps_free, "psum exhausted"
        return ps_free.pop(0)

    def pfree(*bs):
        ps_free.extend(bs)

    Cs = {}
    for k, v in hc.items():
        if k in ("cosp", "sinp"):
            continue
        Cs[k] = P.sb("sc_" + k, list(v.shape), rnd=(k in RND_C))
        P.dma(Cs[k].a, Cd[k])
    ident, ones, bd64, prot = Cs["ident"], Cs["ones"], Cs["bd64"], Cs["prot"]
    epsb = P.sb("epsb", [128, 1])
    P.memset(epsb.a, EPS)
    oneb = P.sb("oneb", [128, 1])
    P.memset(oneb.a, 1.0)

    def load_cols(name, view, R, dup64=False):
        st = wk5.alloc()
        if dup64:
            P.dma(st[0:R, 0:64], view)
            P.dma(st[0:R, 64:128], view)
        else:
            P.dma(st[0:R, 0:128], view)
        ps = psum()
        P.transpose(ps[:, 0:R], st[0:R, 0:128], ident[0:R, 0:R])
        out = P.sb("pc_" + name, [128, R])
        P.copy(out.a, ps[:, 0:R])
        pfree(ps)
        wk5.free(st)
        return out

    g_mix = load_cols("gmix", Wd["norm_mix_g"].rearrange("l (c p) -> (l c) p", p=128), 16)
    g_x = load_cols("gx", Wd["norm_x_g"].rearrange("l (c p) -> (l c) p", p=128), 16)
    g_mem = load_cols("gmem", Wd["norm_mem_g"].rearrange("l (c p) -> (l c) p", p=128), 16)
    g_ffn = load_cols("gffn", Wd["norm_ffn_g"].rearrange("l (c p) -> (l c) p", p=128), 16)
    g_qn = load_cols("gqn", Wd["swa_qn_g"], 2, dup64=True)
    g_kn = load_cols("gkn", Wd["swa_kn_g"], 2, dup64=True)
    cw = load_cols("cw", Wd["lru_conv_w"].rearrange("l j (c p) -> (l j c) p", p=128), 32)
    cb = load_cols("cb", Wd["lru_conv_b"].rearrange("l (c p) -> (l c) p", p=128), 8)
    lba = load_cols("lba", Wd["lru_ba"].rearrange("l (c p) -> (l c) p", p=128), 8)
    lbx = load_cols("lbx", Wd["lru_bx"].rearrange("l (c p) -> (l c) p", p=128), 8)
    lam = load_cols("lam", Wd["lru_lambda"].rearrange("l (c p) -> (l c) p", p=128), 8)
    g_on = load_cols("gon", Wd["gla_on_g"], 2)
    g_xq = load_cols("gxq", Wd["x_qn_g"], 2)
    g_xk = load_cols("gxk", Wd["x_kn_g"], 2)
    fcb = load_cols("fcb", Wd["ffn_conv_b"].rearrange("l (c p) -> (l c) p", p=128), 88)
    fcw = P.sb("pc_fcw", [128, 264])
    fcw_view = Wd["ffn_conv_w"].rearrange("l j (c p) -> (l j c) p", p=128)
    for i in range(3):
        tmp = load_cols("fcw%d" % i, fcw_view[88 * i:88 * (i + 1), :], 88)
        P.copy(fcw[:, 88 * i:88 * (i + 1)], tmp.a, eng="pool")
    lbah = P.sb("lbah", [128, 8])
    lbxh = P.sb("lbxh", [128, 8])
    P.ts(lbah.a, lba.a, 0.5, ALU.mult)
    P.ts(lbxh.a, lbx.a, 0.5, ALU.mult)
    qtr = P.sb("qtr", [128, 1])
    P.memset(qtr.a, 0.25)
    m8sph = P.sb("m8sph", [128, 8])
    m8sp = P.sb("m8sp", [128, 8])
    P.act(m8sp.a, lam.a, AF.Exp, scale=-1.0)
    P.act(m8sp.a, m8sp.a, AF.Ln, bias=oneb[:, 0:1])
    P.ts(m8sp.a, m8sp.a, -8.0, ALU.mult)
    P.ts(m8sph.a, m8sp.a, 0.5, ALU.mult)
    sx = P.sb("sinkx", [64, 16])
    P.dma(sx.a, Wd["swa_sink"].rearrange("l h -> (l h)").partition_broadcast(64))
    P.act(sx.a, sx.a, AF.Exp)
    BDa = [[None] * 4 for _ in range(L)]
    BDx = [[None] * 4 for _ in range(L)]
    for l in range(L):
        for c in range(4):
            for nm, dst, src in (("a", BDa, Wd["lru_wa"]), ("x", BDx, Wd["lru_wx"])):
                b = P.sb("bd%s%d%d" % (nm, l, c), [128, 128], rnd=True)
                P.ts(b.a, ones[:, :], 0.0, ALU.mult)
                P.dma(b[0:64, 0:64], src[l, 2 * c])
                P.dma(b[64:128, 64:128], src[l, 2 * c + 1])
                dst[l][c] = b
    wa2 = [P.sb("wa2_%d" % l, [16, 256], rnd=True) for l in range(L)]
    gba = [P.sb("gba_%d" % l, [1, 256], rnd=True) for l in range(L)]
    for l in range(L):
        P.dma(wa2[l].a, Wd["gla_wa2"][l])
        P.dma(gba[l].a, Wd["gla_ba"][l:l + 1, :])

    XT = [P.sb("XT%d" % c, [128, T]) for c in range(8)]
    KT = [[P.sb("KT%d%d" % (l, kv), [64, 128 + T], rnd=True) for kv in range(2)] for l in range(L)]
    Vb = [P.sb("Vb%d" % l, [128, NB + 1, 128], rnd=True) for l in range(L)]
    XB = [[P.sb("XB%d%d" % (l, c), [128, 3 + T]) for c in range(4)] for l in range(L)]
    Hst = [[P.sb("Hst%d%d" % (l, c), [128, 16]) for c in range(4)] for l in range(L)]
    Sst = [[P.sb("Sst%d%d" % (l, j), [128, 128]) for j in range(2)] for l in range(L)]
    FC = [[P.sb("FC%d_%d" % (l, ch), [128, 2]) for ch in range(44)] for l in range(L)]
    MKT = [P.sb("MKT%d" % l, [128, 4, 256], rnd=True) for l in range(L)]
    MV = [P.sb("MV%d" % l, [128, 2, 512], rnd=True) for l in range(L)]
    QTg = [P.sb("QTg%d" % kv, [64, NB * 512], rnd=True) for kv in range(2)]
    OTg = [P.sb("OTg%d" % kv, [64, NB * 512], rnd=True) for kv in range(2)]
    KCt = P.sb("KCt", [128, NB, 256])
    VCt = P.sb("VCt", [128, NB, 512], rnd=True)
    Gt = P.sb("Gt", [128, NB, 256], rnd=True)

    def load_w(src, K, kp=128):
        nk = K // kp
        ncols = src.shape[1]
        assert nk * ncols <= 2048
        wt = wpool.alloc()
        wv = wt[0:kp, 0:nk * ncols].re("p (k c) -> p k c", k=nk)
        P.dma(wv, src.rearrange("(k p) c -> p k c", p=kp))
        return wt, wv

    def proj_fm(src2d, K, rhs_list, N, consume, GW=None, kp=128, Mch=128, ps_off=0):
        ncols = src2d.shape[1]
        nk = K // kp
        if GW is None:
            GW = max(128, (2048 // nk) // 128 * 128)
        ci = 0
        pend = None
        post = None
        for g0 in range(0, ncols, GW):
            gw = min(GW, ncols - g0)
            wt, wv = load_w(src2d[:, g0:g0 + gw], K, kp)
            for m0 in range(0, gw, Mch):
                mw = min(Mch, gw - m0)
                ps = psum()
                for k in range(nk):
                    P.mm(ps[0:mw, ps_off:ps_off + N], wv[:, k, m0:m0 + mw], rhs_list[k], start=(k == 0), stop=(k == nk - 1))
                if pend is not None:
                    r = consume(*pend)
                    if post is not None:
                        post()
                    post = r
                pend = (ci, mw, ps)
                ci += 1
            wpool.free(wt)
        if pend is not None:
            r = consume(*pend)
            if post is not None:
                post()
            if r is not None:
                r()

    def proj_tm(src2d, K, lhs_list, blocks, consume, GW=256):
        ncols = src2d.shape[1]
        nk = K // 128
        for g0 in range(0, ncols, GW):
            gw = min(GW, ncols - g0)
            wt, wv = load_w(src2d[:, g0:g0 + gw], K)
            for bi, (c0, bw) in enumerate(blocks):
                ps = psum()
                for k in range(nk):
                    P.mm(ps[0:bw, 0:gw], lhs_list[k][:, c0:c0 + bw], wv[:, k, :], start=(k == 0), stop=(k == nk - 1))
                consume(bi, bw, g0, gw, ps)
            wpool.free(wt)

    def rms_apply(srcs, N, lhs_ones, Dn, gcol, outs):
        ps = psum()
        n = len(srcs)
        for c, x in enumerate(srcs):
            sq = wkr.alloc()
            P.act(RND(sq[:, 0:N]), x, AF.Square)
            P.mm(ps[:, 0:N], lhs_ones, sq[:, 0:N], start=(c == 0), stop=(c == n - 1))
            wk.free(sq)
        rstd = wk.alloc()
        P.act(rstd[:, 0:N], ps[:, 0:N], AF.Ln, scale=1.0 / Dn, bias=epsb[:, 0:1])
        pfree(ps)
        P.act(rstd[:, 0:N], rstd[:, 0:N], AF.Exp, scale=-0.5)
        for c, x in enumerate(srcs):
            P.stt(RND(outs[c]), x, gcol(c), rstd[:, 0:N], ALU.mult, ALU.mult)
        wk.free(rstd)

    def load_fm(rows_ap, R, outs, col0):
        st0, st1 = wk5.alloc(), wk5.alloc()
        P.dma(st0[0:R, :], rows_ap[:, 0:512])
        P.dma(st1[0:R, :], rows_ap[:, 512:1024])
        for half, st in enumerate((st0, st1)):
            ps = psum()
            for q in range(4):
                P.transpose(ps[:, q * 128:q * 128 + R], st[0:R, q * 128:(q + 1) * 128], ident[0:R, 0:R])
            for q in range(4):
                P.copy(outs[half * 4 + q][:, col0:col0 + R], ps[:, q * 128:q * 128 + R])
            pfree(ps)
        wk5.free(st0, st1)

    def store_fm(srcs, R, col0, rows_ap, out_dma=True):
        n = len(srcs)
        for g0 in range(0, n, 4):
            gn = min(4, n - g0)
            ps = psum()
            for q in range(gn):
                P.transpose(ps[0:R, q * 128:(q + 1) * 128], srcs[g0 + q][:, col0:col0 + R], ident[:, :])
            st = wk5.alloc()
            P.copy(st[0:R, 0:gn * 128], ps[0:R, 0:gn * 128])
            pfree(ps)
            P.dma(rows_ap[:, g0 * 128:(g0 + gn) * 128], st[0:R, 0:gn * 128], eng="pool", out_dma=out_dma)
            wk5.free(st)

    def layer(l, tl):
        kind, N = tl["kind"], tl["N"]
        prompt = kind == "p"
        first, last, sq_i = tl["first"], tl["last"], tl["seq"]
        nseg = 1 if prompt else NSq
        Lg = N // nseg
        nb = NB if prompt else 1
        blocks = [(b * 128, 128) for b in range(NB)] if prompt else [(0, NST)]
        w_in = Wd["w_in"][l]

        def seg3(v, w):
            return v.re("p (s w) -> p s w", w=w)

        P.tag = "norm_mix"
        HT = [wkr.alloc() for _ in range(8)]
        rms_apply([XT[c][:, 0:N] for c in range(8)], N, ones[:, :], D,
                  lambda c: g_mix[:, l * 8 + c:l * 8 + c + 1], [HT[c][:, 0:N] for c in range(8)])
        HTn = [HT[c][:, 0:N] for c in range(8)]

        P.tag = "swa"
        if prompt:
            cosv = wk.alloc()
            sinv = wk.alloc()
            P.dma(cosv[:, 0:N], Cd["cosp"][:, tl["pos0"]:tl["pos0"] + N])
            P.dma(sinv[:, 0:N], Cd["sinp"][:, tl["pos0"]:tl["pos0"] + N])
            cosa, sina = cosv[:, 0:N], sinv[:, 0:N]
        else:
            cosa, sina = Cs["coss"][:, 0:N], Cs["sins"][:, 0:N]
        if not prompt:
            QTs = [wk5r.alloc() for _ in range(2)]
            OTs = [wk5r.alloc() for _ in range(2)]
            KTn = [wkr.alloc() for _ in range(2)]

        def normrope(ps, gcolv, writer):
            xn = wkr.alloc()
            rms_apply([ps[:, 0:N]], N, bd64[:, :], 64.0, lambda c: gcolv, [xn[:, 0:N]])
            pfree(ps)

            def post():
                ps2 = psum()
                P.mm(ps2[:, 0:N], prot[:, :], xn[:, 0:N])
                t1 = wk.alloc()
                P.tt(t1[:, 0:N], xn[:, 0:N], cosa, ALU.mult, eng="pool")
                t2 = wk.alloc()
                P.tt(t2[:, 0:N], ps2[:, 0:N], sina, ALU.mult)
                pfree(ps2)
                wk.free(xn)
                writer(t1, t2)
                wk.free(t1, t2)
            return post

        def q_consume(c, mw, ps):
            def writer(t1, t2):
                kv = c // 2
                for e in range(2):
                    j = 2 * (c % 2) + e
                    if prompt:
                        dst = QTg[kv][0:64, :].re("p (q j t) -> p q j t", j=4, t=128)[:, :, j, :]
                        a = t1[64 * e:64 * e + 64, 0:N].re("p (q t) -> p q t", t=128)
                        b = t2[64 * e:64 * e + 64, 0:N].re("p (q t) -> p q t", t=128)
                    else:
                        dst = QTs[kv][0:64, 0:NSq * 16].re("p (s j t) -> p s j t", j=4, t=4)[:, :, j, :]
                        a = t1[64 * e:64 * e + 64, 0:N].re("p (s t) -> p s t", t=4)
                        b = t2[64 * e:64 * e + 64, 0:N].re("p (s t) -> p s t", t=4)
                    P.tt(RND(dst), a, b, ALU.add)
            return normrope(ps, g_qn[:, l:l + 1], writer)

        kvout = {}
        if (not prompt) or last:
            kvout["k"] = wk.alloc()
            kvout["v"] = wk.alloc()
        NO = min(N, 128)

        def k_consume(c, mw, ps):
            def writer(t1, t2):
                for kv in range(2):
                    dst = KT[l][kv][0:64, 128:128 + N] if prompt else KTn[kv][0:64, 0:N]
                    P.tt(RND(dst), t1[64 * kv:64 * kv + 64, 0:N], t2[64 * kv:64 * kv + 64, 0:N], ALU.add)
                    if "k" in kvout:
                        P.tt(kvout["k"][0:64, kv * 128:kv * 128 + NO], t1[64 * kv:64 * kv + 64, N - NO:N],
                             t2[64 * kv:64 * kv + 64, N - NO:N], ALU.add, eng="pool")
            return normrope(ps, g_kn[:, l:l + 1], writer)
        proj_fm(w_in[:, O_QA:O_QA + 640], D, HTn, N,
                lambda c, mw, ps: q_consume(c, mw, ps) if c < 4 else k_consume(c - 4, mw, ps))

        if prompt:
            def v_consume(bi, bw, g0, gw, ps):
                P.copy(RND(Vb[l][:, 1 + bi, :]), ps[:, 0:128])
                if last and bi == NB - 1:
                    P.copy(kvout["v"][:, 0:128], ps[:, 0:128])
                pfree(ps)
            proj_tm(w_in[:, O_VA:O_VA + 128], D, HTn, blocks, v_consume)
            P.tag = "swa_attn"
            units = []
            for kv in range(2):
                for qb in range(NB):
                    kbs = []
                    if not (first and qb == 0):
                        kbs.append((qb * 128, qb, Cs["mprev4"]))
                    kbs.append((128 + qb * 128, qb + 1, Cs["mcur4"]))
                    for i, (kc0, vblk, mask) in enumerate(kbs):
                        units.append((kv, qb, kc0, vblk, mask, i == 0, i == len(kbs) - 1))

            def sscores(u):
                kv, qb, kc0 = u[0], u[1], u[2]
                pss = psum()
                P.mm(pss[:, 0:512], KT[l][kv][0:64, kc0:kc0 + 128], QTg[kv][0:64, qb * 512:(qb + 1) * 512])
                return pss
            def expmask(u, pss):
                pT_ = wk5r.alloc()
                P.act(pT_.a, pss.a, AF.Exp, scale=0.125)
                pfree(pss)
                P.tt(RND(pT_.a), pT_.a, u[4].a, ALU.mult)
                return pT_
            pT_n = expmask(units[0], sscores(units[0]))
            pss_n = sscores(units[1]) if len(units) > 1 else None
            pso = psd = None
            for ui, (kv, qb, kc0, vblk, mask, is_first, is_last) in enumerate(units):
                pT = pT_n
                if is_first:
                    pso, psd = psum(), psum()
                P.mm(pso[0:64, 0:512], Vb[l][:, vblk, 64 * kv:64 * kv + 64], pT.a, start=is_first, stop=is_last)
                P.mm(psd[0:64, 0:512], ones[:, 0:64], pT.a, start=is_first, stop=is_last)
                wk5.free(pT)
                if ui + 1 < len(units):
                    pT_n = expmask(units[ui + 1], pss_n)
                    pss_n = sscores(units[ui + 2]) if ui + 2 < len(units) else None
                if is_last:
                    den = wk5.alloc()
                    for j in range(4):
                        P.act(den[0:64, j * 128:(j + 1) * 128], psd[0:64, j * 128:(j + 1) * 128], AF.Ln,
                              bias=sx[:, l * 8 + kv * 4 + j:l * 8 + kv * 4 + j + 1])
                    P.act(den[0:64, :], den[0:64, :], AF.Exp, scale=-1.0)
                    P.tt(RND(OTg[kv][0:64, qb * 512:(qb + 1) * 512]), pso[0:64, :], den[0:64, :], ALU.mult)
                    wk5.free(den)
                    pfree(pso, psd)
            if last:
                st = wk5.alloc()
                ps = psum()
                for kv in range(2):
                    P.transpose(ps[:, 64 * kv:64 * kv + 64], kvout["k"][0:64, kv * 128:(kv + 1) * 128], ident[0:64, 0:64])
                P.copy(st[:, 0:128], ps[:, 0:128])
                pfree(ps)
                P.dma(o_pk[l, sq_i].rearrange("t k d -> t (k d)"), st[:, 0:128], eng="pool", out_dma=True)
                wk5.free(st)
                P.dma(o_pv[l, sq_i].rearrange("t k d -> t (k d)"), kvout["v"][:, 0:128], eng="pool", out_dma=True)
                wk.free(kvout["k"], kvout["v"])
            else:
                for kv in range(2):
                    P.copy(RND(KT[l][kv][0:64, 0:128]), KT[l][kv][0:64, N:N + 128], eng="pool")
                P.copy(RND(Vb[l][:, 0, :]), Vb[l][:, NB, :], eng="pool")
            OTv = lambda h: OTg[h // 4][0:64, :].re("p (q j t) -> p q j t", j=4, t=128)[:, :, h % 4, :]
            wk.free(cosv, sinv)
        else:
            Vn = wkr.alloc()

            def v_consume(bi, bw, g0, gw, ps):
                P.copy(RND(Vn[0:NST, 0:128]), ps[0:NST, 0:128])
                P.copy(kvout["v"][0:NST, 0:128], ps[0:NST, 0:128])
                pfree(ps)
            proj_tm(w_in[:, O_VA:O_VA + 128], D, HTn, blocks, v_consume)
            NC16 = NSq * 16
            pssc = [psum(), psum()]
            for s0 in range(0, NSq, 4):
                Kc4 = wk5.alloc()
                P.dma(Kc4[:, :].re("t (s f) -> t s f", f=128), csk[l, s0:s0 + 4].rearrange("s t k d -> t s (k d)"))
                ps = psum()
                for q in range(4):
                    P.transpose(ps[:, q * 128:(q + 1) * 128], Kc4[:, q * 128:(q + 1) * 128], ident[:, :])
                wk5.free(Kc4)
                for kv in range(2):
                    KTc4 = wk5r.alloc()
                    P.copy(RND(KTc4[0:64, :]), ps[64 * kv:64 * kv + 64, :])
                    for q in range(4):
                        s_ = s0 + q
                        P.mm(pssc[kv][:, 16 * s_:16 * s_ + 16], KTc4[0:64, q * 128:(q + 1) * 128],
                             QTs[kv][0:64, 16 * s_:16 * s_ + 16])
                    wk5.free(KTc4)
                pfree(ps)
            for kv in range(2):
                pTc = wk5r.alloc()
                P.act(pTc[:, 0:NC16], pssc[kv][:, 0:NC16], AF.Exp, scale=0.125)
                pfree(pssc[kv])
                P.tt(RND(pTc[:, 0:NC16]), pTc[:, 0:NC16], Cs["msc"][:, 0:NC16], ALU.mult, eng="pool")
                pssn = psum()
                P.mm(pssn[0:NST, 0:NC16], KTn[kv][0:64, 0:NST], QTs[kv][0:64, 0:NC16])
                pTn = wk5r.alloc()
                P.act(pTn[0:NST, 0:NC16], pssn[0:NST, 0:NC16], AF.Exp, scale=0.125)
                pfree(pssn)
                P.tt(RND(pTn[0:NST, 0:NC16]), pTn[0:NST, 0:NC16], Cs["msn"][0:NST, 0:NC16], ALU.mult, eng="pool")
                pso, psd = psum(), psum()
                P.mm(pso[0:64, 0:NC16], Vn[0:NST, 64 * kv:64 * kv + 64], pTn[0:NST, 0:NC16], start=True, stop=False)
                for s0 in range(0, NSq, 4):
                    Vc4 = wk5.alloc()
                    P.dma(Vc4[:, :].re("t (s f) -> t s f", f=128), csv[l, s0:s0 + 4].rearrange("s t k d -> t s (k d)"))
                    for q in range(4):
                        s_ = s0 + q
                        P.mm(pso[0:64, 16 * s_:16 * s_ + 16], Vc4[:, q * 128 + 64 * kv:q * 128 + 64 * kv + 64],
                             pTc[:, 16 * s_:16 * s_ + 16], start=False, stop=(s_ == NSq - 1))
                    wk5.free(Vc4)
                P.mm(psd[0:64, 0:NC16], ones[0:NST, 0:64], pTn[0:NST, 0:NC16], start=True, stop=False)
                P.mm(psd[0:64, 0:NC16], ones[:, 0:64], pTc[:, 0:NC16], start=False, stop=True)
                den = wk5.alloc()
                for j in range(4):
                    P.ts(den[0:64, 0:NC16].re("p (s j t) -> p s j t", j=4, t=4)[:, :, j, :],
                         psd[0:64, 0:NC16].re("p (s j t) -> p s j t", j=4, t=4)[:, :, j, :],
                         sx[:, l * 8 + kv * 4 + j:l * 8 + kv * 4 + j + 1], ALU.add)
                P.recip(den[0:64, 0:NC16], den[0:64, 0:NC16])
                P.tt(RND(OTs[kv][0:64, 0:NC16]), pso[0:64, 0:NC16], den[0:64, 0:NC16], ALU.mult)
                wk5.free(den, pTc, pTn)
                pfree(pso, psd)
            P.dma(o_sk[l, :, 0:124], csk[l, :, 4:128], eng="pool", out_dma=True)
            P.dma(o_sv[l, :, 0:124], csv[l, :, 4:128], eng="pool", out_dma=True)
            st = wk5.alloc()
            ps = psum()
            for kv in range(2):
                P.transpose(ps[0:NST, 64 * kv:64 * kv + 64], kvout["k"][0:64, kv * 128:kv * 128 + NST], ident[0:64, 0:64])
            P.copy(st[0:NST, 0:128], ps[0:NST, 0:128])
            pfree(ps)
            for s in range(NSq):
                P.dma(o_sk[l, s, 124:128].rearrange("t k d -> t (k d)"), st[4 * s:4 * s + 4, 0:128], eng="pool", out_dma=True)
                P.dma(o_sv[l, s, 124:128].rearrange("t k d -> t (k d)"), kvout["v"][4 * s:4 * s + 4, 0:128], eng="pool", out_dma=True)
            wk5.free(st)
            wk.free(Vn, *KTn)
            wk.free(kvout["k"], kvout["v"])
            wk5.free(*QTs)
            OTv = lambda h: OTs[h // 4][0:64, 0:NC16].re("p (s j t) -> p s j t", j=4, t=4)[:, :, h % 4, :]

        P.tag = "lru"
        OB = [wkr.alloc() for _ in range(4)]
        Hh = [wk.alloc() for _ in range(4)]
        if prompt:
            xb3 = lambda c: XB[l][c][:, 0:3 + N].re("p (s w) -> p s w", s=1)
        else:
            XBs = [wk.alloc() for _ in range(4)]
            xb3 = lambda c: XBs[c][:, 0:NSq * 7].re("p (s w) -> p s w", w=7)
            st = wk5.alloc()
            R = NSq * 3
            P.dma(st[0:R, :], slc[l].rearrange("s j f -> (s j) f"))
            ps = psum()
            for c in range(4):
                P.transpose(ps[:, c * 128:c * 128 + R], st[0:R, c * 128:(c + 1) * 128], ident[0:R, 0:R])
            for c in range(4):
                P.copy(xb3(c)[:, :, 0:3], ps[:, c * 128:c * 128 + R].re("p (s j) -> p s j", j=3))
            pfree(ps)
            P.dma(st[0:NSq, :], slh[l])
            ps = psum()
            for c in range(4):
                P.transpose(ps[:, c * 16:c * 16 + NSq], st[0:NSq, c * 128:(c + 1) * 128], ident[0:NSq, 0:NSq])
            for c in range(4):
                P.copy(Hst[l][c][:, 0:NSq], ps[:, c * 16:c * 16 + NSq])
            pfree(ps)
            wk5.free(st)
        if prompt and first:
            for c in range(4):
                P.memset(XB[l][c][:, 0:3], 0.0)
                P.memset(Hst[l][c][:, 0:1], 0.0)

        def xb_consume(c, mw, ps):
            X3 = xb3(c)
            P.copy(X3[:, :, 3:3 + Lg], seg3(ps[:, 0:N], Lg))
            pfree(ps)
            xc = wkr.alloc()
            xc3 = seg3(xc[:, 0:N], Lg)
            cwc = lambda j: cw[:, l * 16 + j * 4 + c:l * 16 + j * 4 + c + 1]
            P.ts(xc3, X3[:, :, 0:Lg], cwc(0), ALU.mult, cb[:, l * 4 + c:l * 4 + c + 1], ALU.add)
            for j in range(1, 4):
                P.stt(xc3, X3[:, :, j:j + Lg], cwc(j), xc3, ALU.mult, ALU.add)
            return lambda: (xb_postA(c, xc) if prompt else xb_post(c, xc))

        lruA = {}

        def xb_postA(c, xc):
            tr = wk.alloc()
            ti = wk.alloc()
            col = slice(l * 4 + c, l * 4 + c + 1)
            ps1 = psum()
            P.mm(ps1[:, 0:N], BDa[l][c].a, xc[:, 0:N])
            P.act(tr[:, 0:N], ps1[:, 0:N], AF.Tanh, bias=lbah[:, col], scale=0.5)
            pfree(ps1)
            ps2 = psum()
            P.mm(ps2[:, 0:N], BDx[l][c].a, xc[:, 0:N])
            P.act(ti[:, 0:N], ps2[:, 0:N], AF.Tanh, bias=lbxh[:, col], scale=0.5)
            pfree(ps2)
            a = wk.alloc()
            P.act(a[:, 0:N], tr[:, 0:N], AF.Exp, scale=m8sph[:, col], bias=m8sph[:, col])
            P.stt(ti[:, 0:N], ti[:, 0:N], 1.0, xc[:, 0:N], ALU.add, ALU.mult)
            wk.free(xc, tr)
            lruA[c] = (a, ti)

        def xb_postB(c):
            a, gi = lruA.pop(c)
            sq = wk.alloc()
            P.tt(sq[:, 0:N], a[:, 0:N], a[:, 0:N], ALU.mult, eng="pool")
            P.act(sq[:, 0:N], sq[:, 0:N], AF.Sqrt, scale=-0.25, bias=qtr[:, 0:1])
            P.tt(gi[:, 0:N], gi[:, 0:N], sq[:, 0:N], ALU.mult)
            a3, b3 = seg3(a[:, 0:N], Lg), seg3(gi[:, 0:N], Lg)
            tmp = wk.alloc()
            t3 = tmp[:, 0:nseg].re("p (s o) -> p s o", o=1)
            h0 = Hst[l][c][:, 0:nseg].re("p (s o) -> p s o", o=1)
            P.tt(t3, a3[:, :, 0:1], h0, ALU.mult)
            P.tt(b3[:, :, 0:1], b3[:, :, 0:1], t3, ALU.add)
            P.memset(a3[:, :, 0:1], 0.0, eng="dve")
            P.scan(Hh[c][:, 0:N], a[:, 0:N], gi[:, 0:N], 0.0)
            P.copy(h0, seg3(Hh[c][:, 0:N], Lg)[:, :, Lg - 1:Lg], eng="pool")
            wk.free(sq, gi, a, tmp)

        def xb_post(c, xc):
            r = wk.alloc()
            ig = wk.alloc()
            ps1 = psum()
            P.mm(ps1[:, 0:N], BDa[l][c].a, xc[:, 0:N])
            P.act(r[:, 0:N], ps1[:, 0:N], AF.Sigmoid, bias=lba[:, l * 4 + c:l * 4 + c + 1])
            pfree(ps1)
            ps2 = psum()
            P.mm(ps2[:, 0:N], BDx[l][c].a, xc[:, 0:N])
            P.act(ig[:, 0:N], ps2[:, 0:N], AF.Sigmoid, bias=lbx[:, l * 4 + c:l * 4 + c + 1])
            pfree(ps2)
            a = wk.alloc()
            P.act(a[:, 0:N], r[:, 0:N], AF.Exp, scale=m8sp[:, l * 4 + c:l * 4 + c + 1])
            sq = r
            P.tt(sq[:, 0:N], a[:, 0:N], a[:, 0:N], ALU.mult, eng="pool")
            P.act(sq[:, 0:N], sq[:, 0:N], AF.Sqrt, scale=-1.0, bias=oneb[:, 0:1])
            P.tt(ig[:, 0:N], ig[:, 0:N], xc[:, 0:N], ALU.mult)
            P.tt(ig[:, 0:N], ig[:, 0:N], sq[:, 0:N], ALU.mult)
            a3, b3 = seg3(a[:, 0:N], Lg), seg3(ig[:, 0:N], Lg)
            tmp = wk.alloc()
            t3 = tmp[:, 0:nseg].re("p (s o) -> p s o", o=1)
            h0 = Hst[l][c][:, 0:nseg].re("p (s o) -> p s o", o=1)
            P.tt(t3, a3[:, :, 0:1], h0, ALU.mult)
            P.tt(b3[:, :, 0:1], b3[:, :, 0:1], t3, ALU.add)
            P.memset(a3[:, :, 0:1], 0.0, eng="dve")
            P.scan(Hh[c][:, 0:N], a[:, 0:N], ig[:, 0:N], 0.0)
            P.copy(h0, seg3(Hh[c][:, 0:N], Lg)[:, :, Lg - 1:Lg], eng="pool")
            wk.free(xc, r, ig, a, tmp)

        def yb_consume(c, mw, ps):
            g = wk.alloc()
            P.act(g[:, 0:N], ps[:, 0:N], AF.Gelu_apprx_tanh)
            pfree(ps)
            P.tt(RND(OB[c][:, 0:N]), g[:, 0:N], Hh[c][:, 0:N], ALU.mult)
            wk.free(g)
        if prompt:
            proj_fm(w_in[:, O_XB:O_XB + 512], D, HTn, N, xb_consume)
            for c in range(4):
                xb_postB(c)
            proj_fm(w_in[:, O_YB:O_YB + 512], D, HTn, N, yb_consume)
        else:
            proj_fm(w_in[:, O_XB:O_XB + 1024], D, HTn, N,
                    lambda c, mw, ps: xb_consume(c, mw, ps) if c < 4 else yb_consume(c - 4, mw, ps))
        wk.free(*Hh)
        if prompt:
            if last:
                store_fm([Hst[l][c] for c in range(4)], 1, 0, o_ph[l, sq_i:sq_i + 1, :])
                store_fm([XB[l][c] for c in range(4)], 3, N, o_pc[l, sq_i])
            else:
                for c in range(4):
                    P.copy(XB[l][c][:, 0:3], XB[l][c][:, N:N + 3], eng="pool")
        else:
            store_fm([Hst[l][c] for c in range(4)], NSq, 0, o_sh[l])
            cst = [wk.alloc() for _ in range(4)]
            for c in range(4):
                P.copy(cst[c][:, 0:NSq * 3].re("p (s j) -> p s j", j=3), xb3(c)[:, :, 4:7], eng="pool")
            store_fm(cst, NSq * 3, 0, o_sc[l].rearrange("s j f -> (s j) f"))
            wk.free(*cst)
            wk.free(*XBs)

        P.tag = "gla"
        qcT = [wk.alloc() for _ in range(2)]
        kcT = [wk.alloc() for _ in range(2)]

        def qc_consume(c, mw, ps):
            P.act(qcT[c][:, 0:N], ps[:, 0:N], AF.Copy, scale=0.125)
            pfree(ps)

        def kc_consume(c, mw, ps):
            P.copy(kcT[c][:, 0:N], ps[:, 0:N])
            pfree(ps)
        proj_fm(w_in[:, O_QC:O_QC + 512], D, HTn, N,
                lambda c, mw, ps: qc_consume(c, mw, ps) if c < 2 else kc_consume(c - 2, mw, ps))

        def kct_consume(bi, bw, g0, gw, ps):
            P.copy(KCt[0:bw, bi, g0:g0 + gw], ps[0:bw, 0:gw])
            pfree(ps)
        proj_tm(w_in[:, O_KC:O_KC + 256], D, HTn, blocks, kct_consume)

        def vct_consume(bi, bw, g0, gw, ps):
            P.copy(RND(VCt[0:bw, bi, g0:g0 + gw]), ps[0:bw, 0:gw])
            pfree(ps)
        proj_tm(w_in[:, O_VC:O_VC + 512], D, HTn, blocks, vct_consume)
        act_ = wkr.alloc()

        def ac_consume(c, mw, ps):
            P.copy(RND(act_[0:16, 0:N]), ps[0:16, 0:N])
            pfree(ps)
        SR = [wk.alloc() for _ in range(4)]

        def rc_consume(c, mw, ps):
            P.act(SR[c][:, 0:N], ps[:, 0:N], AF.Silu)
            pfree(ps)
        proj_fm(w_in[:, O_AC:O_AC + 16], D, HTn, N, ac_consume)
        for bi, (c0, bw) in enumerate(blocks):
            ps = psum()
            P.mm(ps[0:bw, 0:256], act_[0:16, c0:c0 + bw], wa2[l].a, start=True, stop=False)
            P.mm(ps[0:bw, 0:256], ones[0:1, 0:bw], gba[l].a, start=False, stop=True)
            e = wk5.alloc()
            P.act(e[0:bw, 0:256], ps[0:bw, 0:256], AF.Exp, scale=-1.0)
            pfree(ps)
            P.act(e[0:bw, 0:256], e[0:bw, 0:256], AF.Ln, bias=oneb[0:bw, 0:1])
            P.ts(RND(Gt[0:bw, bi, :]), e[0:bw, 0:256], -1.0 / 16.0, ALU.mult)
            wk5.free(e)
        wk.free(act_)
        proj_fm(w_in[:, O_RC:O_RC + 512], D, HTn, N, rc_consume)
        oT = [wk.alloc() for _ in range(4)]
        qt = [wkr.alloc() for _ in range(2)]
        kt = [wkr.alloc() for _ in range(2)]
        Mm = Cs["mtri"] if prompt else Cs["ms"]
        Um = Cs["umat"] if prompt else Cs["us"]
        if prompt and first:
            for j in range(2):
                P.memset(Sst[l][j].a, 0.0)
        P.tag = "gla_blk"
        for bi, (c0, bw) in enumerate(blocks):
            psU = psum()
            P.mm(psU[0:bw, 0:256], Um[0:bw, 0:bw], Gt[0:bw, bi, :])
            kp = wk5r.alloc()
            P.act(kp[0:bw, 0:256], psU[0:bw, 0:256], AF.Exp)
            pfree(psU)
            P.tt(RND(kp[0:bw, 0:256]), kp[0:bw, 0:256], KCt[0:bw, bi, :], ALU.mult)
            eP = [wk.alloc() for _ in range(2)]
            for j in range(2):
                psC = psum()
                P.mm(psC[:, 0:bw], Gt[0:bw, bi, 128 * j:128 * (j + 1)], Mm[0:bw, 0:bw])
                eN = wk.alloc()
                P.act(eP[j][:, 0:bw], psC[:, 0:bw], AF.Exp)
                P.act(eN[:, 0:bw], psC[:, 0:bw], AF.Exp, scale=-1.0)
                pfree(psC)
                P.tt(RND(qt[j][:, c0:c0 + bw]), qcT[j][:, c0:c0 + bw], eP[j][:, 0:bw], ALU.mult)
                P.tt(RND(kt[j][:, c0:c0 + bw]), kcT[j][:, c0:c0 + bw], eN[:, 0:bw], ALU.mult, eng="pool")
                wk.free(eN)
            if prompt:
                psO = psum()
                psOh = [psO[:, h * 128:h * 128 + bw] for h in range(4)]
            else:
                psOb = [psum() for _ in range(4)]
                psOh = [psOb[h][:, 0:bw] for h in range(4)]
            for h in range(4):
                j, r0 = h // 2, 64 * (h % 2)
                psA = psum()
                P.mm(psA[0:bw, 0:bw], kt[j][r0:r0 + 64, c0:c0 + bw], qt[j][r0:r0 + 64, c0:c0 + bw])
                att = wkr.alloc()
                P.tt(RND(att[0:bw, 0:bw]), psA[0:bw, 0:bw], Mm[0:bw, 0:bw], ALU.mult)
                pfree(psA)
                P.mm(psOh[h], VCt[0:bw, bi, 128 * h:128 * (h + 1)], att[0:bw, 0:bw], start=True, stop=False)
                if prompt:
                    P.mm(psOh[h], Sst[l][j][r0:r0 + 64, :], qt[j][r0:r0 + 64, c0:c0 + bw], start=False, stop=True, fast=False)
                    P.copy(oT[h][:, c0:c0 + bw], psOh[h])
                wk.free(att)
            if prompt:
                pfree(psO)
                psS = psum()
                for h in range(4):
                    j, r0 = h // 2, 64 * (h % 2)
                    P.mm(psS[r0:r0 + 64, j * 128:(j + 1) * 128], kp[0:bw, 64 * h:64 * h + 64],
                         VCt[0:bw, bi, 128 * h:128 * (h + 1)], fast=False)
                for j in range(2):
                    P.stt(Sst[l][j].a, Sst[l][j].a, eP[j][:, bw - 1:bw], psS[:, j * 128:(j + 1) * 128], ALU.mult, ALU.add)
                pfree(psS)
            else:
                for s in range(NSq):
                    Ss = wk5.alloc()
                    Ss3 = Ss[:, 0:256].re("p (j v) -> p j v", j=2)
                    P.dma(Ss3, sgs[l, s].rearrange("(j h) d v -> (h d) j v", h=2))
                    for h in range(4):
                        j, r0 = h // 2, 64 * (h % 2)
                        P.mm(psOb[h][:, 4 * s:4 * s + 4], Ss3[r0:r0 + 64, j, :], qt[j][r0:r0 + 64, 4 * s:4 * s + 4],
                             start=False, stop=(s == NSq - 1), fast=False)
                    kpm = wk5r.alloc()
                    P.ts(RND(kpm[0:bw, 0:256]), kp[0:bw, 0:256], Cs["rowmask"][0:bw, s:s + 1], ALU.mult, eng="pool")
                    psS = psum()
                    for h in range(4):
                        j, r0 = h // 2, 64 * (h % 2)
                        P.mm(psS[r0:r0 + 64, j * 128:(j + 1) * 128], kpm[0:bw, 64 * h:64 * h + 64],
                             VCt[0:bw, 0, 128 * h:128 * (h + 1)], fast=False)
                    so = wk5.alloc()
                    for j in range(2):
                        P.stt(so[:, j * 128:(j + 1) * 128], Ss3[:, j, :], eP[j][:, 4 * s + 3:4 * s + 4],
                              psS[:, j * 128:(j + 1) * 128], ALU.mult, ALU.add)
                    pfree(psS)
                    P.dma(o_ss[l, s].rearrange("(j h) d v -> (h d) j v", h=2),
                          so[:, 0:256].re("p (j v) -> p j v", j=2), eng="pool", out_dma=True)
                    wk5.free(kpm, so, Ss)
                for h in range(4):
                    P.copy(oT[h][:, c0:c0 + bw], psOh[h])
                pfree(*psOb)
            wk.free(*eP)
            wk5.free(kp)
        if prompt and last:
            for j in range(2):
                P.dma(o_ps[l, sq_i, 2 * j:2 * j + 2].rearrange("h d v -> (h d) v"), Sst[l][j].a, eng="pool", out_dma=True)
        wk.free(*qcT, *kcT, *qt, *kt)
        OC = [wkr.alloc() for _ in range(4)]
        sqs = [wkr.alloc() for _ in range(4)]
        for h in range(4):
            P.act(sqs[h][:, 0:N], oT[h][:, 0:N], AF.Square)
        pns = [psum() for _ in range(4)]
        for h in range(4):
            P.mm(pns[h][:, 0:N], ones[:, :], sqs[h][:, 0:N])
        wk.free(*sqs)
        rs = [wk.alloc() for _ in range(4)]
        for h in range(4):
            P.act(rs[h][:, 0:N], pns[h][:, 0:N], AF.Ln, scale=1.0 / 128.0, bias=epsb[:, 0:1])
        pfree(*pns)
        for h in range(4):
            P.act(rs[h][:, 0:N], rs[h][:, 0:N], AF.Exp, scale=-0.5)
        for h in range(4):
            P.stt(rs[h][:, 0:N], oT[h][:, 0:N], g_on[:, l:l + 1], rs[h][:, 0:N], ALU.mult, ALU.mult)
        for h in range(4):
            P.tt(RND(OC[h][:, 0:N]), rs[h][:, 0:N], SR[h][:, 0:N], ALU.mult, eng="pool")
        wk.free(*rs)
        wk.free(*oT, *SR)

        P.tag = "merge"
        MG = [wkr.alloc() for _ in range(8)]
        brs = (
            (O_GA, Wd["w_branch_a"][l], 64, [OTv(h) for h in range(8)]),
            (O_GB, Wd["w_branch_b"][l], 128, [OB[c][:, 0:N] for c in range(4)]),
            (O_GC, Wd["w_branch_c"][l], 128, [OC[c][:, 0:N] for c in range(4)]),
        )
        for bi_, (gofs, wbr, kp_, rl) in enumerate(brs):
            sig = [None] * 8

            def g_consume(c, mw, ps):
                s = wk.alloc()
                P.act(s[:, 0:N], ps[:, 0:N], AF.Sigmoid)
                pfree(ps)
                sig[c] = s
            proj_fm(w_in[:, gofs:gofs + 1024], D, HTn, N, g_consume)

            def b_consume(c, mw, ps):
                if bi_ == 0:
                    P.tt(RND(MG[c][:, 0:N]), ps[:, 0:N], sig[c][:, 0:N], ALU.mult)
                else:
                    P.tt(sig[c][:, 0:N], ps[:, 0:N], sig[c][:, 0:N], ALU.mult)
                    P.tt(RND(MG[c][:, 0:N]), MG[c][:, 0:N], sig[c][:, 0:N], ALU.add, eng="pool")
                pfree(ps)
                wk.free(sig[c])
            proj_fm(wbr, 512, rl, N, b_consume, kp=kp_)
        wk.free(*OB, *OC, *HT)
        if not prompt:
            wk5.free(*OTs)

        def res_consume(c, mw, ps):
            P.tt(XT[c][:, 0:N], XT[c][:, 0:N], ps[:, 0:N], ALU.add)
            pfree(ps)
        P.tag = "wout"
        proj_fm(Wd["w_out"][l], D, [MG[c][:, 0:N] for c in range(8)], N, res_consume)
        wk.free(*MG)

        P.tag = "xattn"
        if prompt and first:
            MT = [wk.alloc() for _ in range(8)]
            for mb in range(2):
                load_fm(memp[sq_i, mb * 128:(mb + 1) * 128, :], 128, MT, mb * 128)
            MN = [wkr.alloc() for _ in range(8)]
            rms_apply([MT[c][:, 0:256] for c in range(8)], 256, ones[:, :], D,
                      lambda c: g_mem[:, l * 8 + c:l * 8 + c + 1], [MN[c][:, 0:256] for c in range(8)])
            wk.free(*MT)
            MNn = [MN[c][:, 0:256] for c in range(8)]

            def mk_consume(c, mw, ps):
                rms_apply([ps[:, 0:256]], 256, ones[:, :], 128.0, lambda cc: g_xk[:, l:l + 1], [MKT[l][:, c, :]])
                pfree(ps)
            proj_fm(Wd["x_wk"][l], D, MNn, 256, mk_consume)

            def mv_consume(bi, bw, g0, gw, ps):
                P.copy(RND(MV[l][:, bi, g0:g0 + gw]), ps[:, 0:gw])
                pfree(ps)
            proj_tm(Wd["x_wv"][l], D, MNn, [(0, 128), (128, 128)], mv_consume)
            wk.free(*MN)
            for mb in range(2):
                st = wk5.alloc()
                P.copy(st.a, MV[l][:, mb, :], eng="pool")
                P.dma(o_pmv[l, sq_i, mb * 128:(mb + 1) * 128].rearrange("t h d -> t (h d)"), st.a, eng="pool", out_dma=True)
                wk5.free(st)
                ps = psum()
                for h in range(4):
                    P.transpose(ps[:, h * 128:(h + 1) * 128], MKT[l][:, h, mb * 128:(mb + 1) * 128], ident[:, :])
                st = wk5.alloc()
                P.copy(st.a, ps.a)
                pfree(ps)
                P.dma(o_pmk[l, sq_i, mb * 128:(mb + 1) * 128].rearrange("t h d -> t (h d)"), st.a, eng="pool", out_dma=True)
                wk5.free(st)
        HT = [wkr.alloc() for _ in range(8)]
        rms_apply([XT[c][:, 0:N] for c in range(8)], N, ones[:, :], D,
                  lambda c: g_x[:, l * 8 + c:l * 8 + c + 1], [HT[c][:, 0:N] for c in range(8)])
        HTn = [HT[c][:, 0:N] for c in range(8)]
        QX = [wkr.alloc() for _ in range(4)]

        def qx_consume(c, mw, ps):
            rms_apply([ps[:, 0:N]], N, ones[:, :], 128.0, lambda cc: g_xq[:, l:l + 1], [QX[c][:, 0:N]])
            pfree(ps)
        proj_fm(Wd["x_wq"][l], D, HTn, N, qx_consume)
        wk.free(*HT)
        OX = [wkr.alloc() for _ in range(4)]
        xsc = 128.0 ** -0.5
        if prompt:
            assert N <= 256

            def xscores(h):
                pss = psum()
                for mb in range(2):
                    P.mm(pss[:, mb * 256:mb * 256 + N], MKT[l][:, h, mb * 128:(mb + 1) * 128], QX[h][:, 0:N])
                return pss
            def xexp(pss):
                pT_ = wk5r.alloc()
                P.act(pT_[:, :].re("p (m n) -> p m n", m=2)[:, :, 0:N], pss[:, :].re("p (m n) -> p m n", m=2)[:, :, 0:N],
                      AF.Exp, scale=xsc)
                pfree(pss)
                return pT_
            pT_n = xexp(xscores(0))
            pss_n = xscores(1)
            for h in range(4):
                pT = pT_n
                pso, psd = psum(), psum()
                for mb in range(2):
                    P.mm(pso[:, 0:N], MV[l][:, mb, 128 * h:128 * (h + 1)], pT[:, mb * 256:mb * 256 + N], start=(mb == 0), stop=(mb == 1))
                    P.mm(psd[:, 0:N], ones[:, :], pT[:, mb * 256:mb * 256 + N], start=(mb == 0), stop=(mb == 1))
                wk5.free(pT)
                if h < 3:
                    pT_n = xexp(pss_n)
                    pss_n = xscores(h + 2) if h + 2 < 4 else None
                rd = wk.alloc()
                P.act(rd[:, 0:N], psd[:, 0:N], AF.Ln)
                P.act(rd[:, 0:N], rd[:, 0:N], AF.Exp, scale=-1.0)
                P.tt(RND(OX[h][:, 0:N]), pso[:, 0:N], rd[:, 0:N], ALU.mult)
                wk.free(rd)
                pfree(pso, psd)
        else:
            oxP, dP = psum(), psum()
            for s in range(NSq):
                for mb in range(2):
                    Kc2, Vc2, MKs = wk5.alloc(), wk5.alloc(), wk5r.alloc()
                    P.dma(Kc2.a, cmk[l, s, mb * 128:(mb + 1) * 128].rearrange("t h d -> t (h d)"))
                    P.dma(Vc2.a, cmv[l, s, mb * 128:(mb + 1) * 128].rearrange("t h d -> t (h d)"))
                    ps = psum()
                    for h in range(4):
                        P.transpose(ps[:, h * 128:(h + 1) * 128], Kc2[:, h * 128:(h + 1) * 128], ident[:, :])
                    P.copy(RND(MKs.a), ps.a)
                    pfree(ps)
                    pss = psum()
                    for h in range(4):
                        P.mm(pss[:, h * 4:h * 4 + 4], MKs[:, h * 128:(h + 1) * 128], QX[h][:, 4 * s:4 * s + 4])
                    pT = wkr.alloc()
                    P.act(RND(pT[:, 0:16]), pss[:, 0:16], AF.Exp, scale=xsc)
                    pfree(pss)
                    for h in range(4):
                        P.mm(oxP[:, mb * 256 + 16 * s + 4 * h:mb * 256 + 16 * s + 4 * h + 4], Vc2[:, 128 * h:128 * (h + 1)],
                             pT[:, h * 4:h * 4 + 4])
                    P.mm(dP[:, mb * 256 + 16 * s:mb * 256 + 16 * s + 16], ones[:, :], pT[:, 0:16])
                    wk.free(pT)
                    wk5.free(Kc2, Vc2, MKs)
            NC16 = NSq * 16
            rd, oxs = wk5.alloc(), wk5.alloc()
            P.copy(rd[:, 0:NC16], dP[:, 0:NC16])
            P.tt(rd[:, 0:NC16], rd[:, 0:NC16], dP[:, 256:256 + NC16], ALU.add)
            P.recip(rd[:, 0:NC16], rd[:, 0:NC16])
            P.copy(oxs[:, 0:NC16], oxP[:, 0:NC16])
            P.tt(oxs[:, 0:NC16], oxs[:, 0:NC16], oxP[:, 256:256 + NC16], ALU.add)
            for h in range(4):
                P.tt(RND(OX[h][:, 0:N].re("p (s t) -> p s t", t=4)),
                     oxs[:, 0:NC16].re("p (s h t) -> p s h t", h=4, t=4)[:, :, h, :],
                     rd[:, 0:NC16].re("p (s h t) -> p s h t", h=4, t=4)[:, :, h, :], ALU.mult)
            wk5.free(rd, oxs)
            pfree(oxP, dP)
        wk.free(*QX)
        proj_fm(Wd["x_wo"][l], 512, [OX[h][:, 0:N] for h in range(4)], N, res_consume)
        wk.free(*OX)

        P.tag = "ffn"
        HT = [wkr.alloc() for _ in range(8)]
        rms_apply([XT[c][:, 0:N] for c in range(8)], N, ones[:, :], D,
                  lambda c: g_ffn[:, l * 8 + c:l * 8 + c + 1], [HT[c][:, 0:N] for c in range(8)])
        HTn = [HT[c][:, 0:N] for c in range(8)]
        if prompt and first:
            for ch in range(44):
                P.memset(FC[l][ch].a, 0.0)
        ACTT = [wkr.alloc() for _ in range(NCH_FF)]
        R2 = NSq * 2
        sfc2 = sfc[l].rearrange("s j f -> (s j) f")
        osf2 = o_sf[l].rearrange("s j f -> (s j) f")
        pair = {}
        WL = 2 + Lg
        gbuf = {}

        def up_consume_f(half):
            def up_consume(c, mw, ps):
                ch = half * NCH_FF + c
                if prompt:
                    y = wk.alloc()
                    fw = lambda j: fcw[:, l * 132 + j * 44 + ch:l * 132 + j * 44 + ch + 1]
                    P.act(y[:, 0:N], ps[:, 2:N + 2], AF.Identity, scale=fw(2), bias=fcb[:, l * 44 + ch:l * 44 + ch + 1])
                    P.copy(ps[:, 0:2], FC[l][ch][:, 0:2], eng="dve")
                    P.stt(y[:, 0:N], ps[:, 1:N + 1], fw(1), y[:, 0:N], ALU.mult, ALU.add)
                    P.stt(y[:, 0:N], ps[:, 0:N], fw(0), y[:, 0:N], ALU.mult, ALU.add)
                    P.copy(FC[l][ch][:, 0:2], ps[:, N:N + 2], eng="dve")
                    pfree(ps)

                    def post_p():
                        if half == 0:
                            P.act(ACTT[c][:, 0:N], y[:, 0:N], AF.Silu)
                        else:
                            P.tt(RND(ACTT[c][:, 0:N]), ACTT[c][:, 0:N], y[:, 0:N], ALU.mult, eng="pool")
                        wk.free(y)
                    return post_p
                ps3 = seg3(ps[:, 0:N], Lg)
                if prompt:
                    car3 = FC[l][ch][:, 0:2].re("p (s j) -> p s j", j=2)
                else:
                    if ch % 2 == 0:
                        st = wk5.alloc()
                        P.dma(st[0:R2, 0:256], sfc2[:, ch * 128:(ch + 2) * 128])
                        psx = psum()
                        for q in range(2):
                            P.transpose(psx[:, q * 32:q * 32 + R2], st[0:R2, q * 128:(q + 1) * 128], ident[0:R2, 0:R2])
                        fc2, fn2 = wk.alloc(), wk.alloc()
                        P.copy(fc2[:, 0:64], psx[:, 0:64])
                        pfree(psx)
                        wk5.free(st)
                        pair["fc"], pair["fn"] = fc2, fn2
                    q_ = ch % 2
                    car3 = pair["fc"][:, q_ * 32:q_ * 32 + R2].re("p (s j) -> p s j", j=2)
                y = wk.alloc()
                y3 = seg3(y[:, 0:N], Lg)
                fw = lambda j: fcw[:, l * 132 + j * 44 + ch:l * 132 + j * 44 + ch + 1]
                P.act(y3, ps3, AF.Identity, scale=fw(2), bias=fcb[:, l * 44 + ch:l * 44 + ch + 1])
                P.stt(y3[:, :, 1:Lg], ps3[:, :, 0:Lg - 1], fw(1), y3[:, :, 1:Lg], ALU.mult, ALU.add)
                P.stt(y3[:, :, 2:Lg], ps3[:, :, 0:Lg - 2], fw(0), y3[:, :, 2:Lg], ALU.mult, ALU.add)
                P.stt(y3[:, :, 0:1], car3[:, :, 1:2], fw(1), y3[:, :, 0:1], ALU.mult, ALU.add)
                P.stt(y3[:, :, 0:2], car3[:, :, 0:2], fw(0), y3[:, :, 0:2], ALU.mult, ALU.add)
                if prompt:
                    P.copy(car3, ps3[:, :, Lg - 2:Lg])
                else:
                    P.copy(pair["fn"][:, q_ * 32:q_ * 32 + R2].re("p (s j) -> p s j", j=2), ps3[:, :, Lg - 2:Lg])
                    if q_ == 1:
                        psx = psum()
                        for q in range(2):
                            P.transpose(psx[0:R2, q * 128:(q + 1) * 128], pair["fn"][:, q * 32:q * 32 + R2], ident[:, :])
                        st = wk5.alloc()
                        P.copy(st[0:R2, 0:256], psx[0:R2, 0:256])
                        pfree(psx)
                        P.dma(osf2[:, (ch - 1) * 128:(ch + 1) * 128], st[0:R2, 0:256], eng="pool", out_dma=True)
                        wk5.free(st)
                        wk.free(pair["fc"], pair["fn"])
                pfree(ps)

                def post():
                    if half == 0:
                        P.act(ACTT[c][:, 0:N], y[:, 0:N], AF.Silu)
                    else:
                        P.tt(RND(ACTT[c][:, 0:N]), ACTT[c][:, 0:N], y[:, 0:N], ALU.mult, eng="pool")
                    wk.free(y)
                return post
            return up_consume
        po = 2 if prompt else 0
        proj_fm(Wd["ffn_w_up"][l][:, 0:D_FF], D, HTn, N, up_consume_f(0), ps_off=po)
        proj_fm(Wd["ffn_w_up"][l][:, D_FF:2 * D_FF], D, HTn, N, up_consume_f(1), ps_off=po)
        wk.free(*HT)
        P.tag = "ffn_down"
        for hf in range(2):
            accs = [psum() for _ in range(4)]
            for jb in range(0, NCH_FF, 4):
                nj = min(4, NCH_FF - jb)
                wt = wpool.alloc()
                wv = wt[:, 0:nj * 512].re("p (j c) -> p j c", j=nj)
                P.dma(wv, Wd["ffn_w_down"][l][jb * 128:(jb + nj) * 128, hf * 512:(hf + 1) * 512].rearrange("(j p) c -> p j c", p=128))
                for jj in range(nj):
                    j = jb + jj
                    for n in range(4):
                        P.mm(accs[n][:, 0:N], wv[:, jj, n * 128:(n + 1) * 128], ACTT[j][:, 0:N],
                             start=(j == 0), stop=(j == NCH_FF - 1))
                wpool.free(wt)
            for n in range(4):
                res_consume(hf * 4 + n, 128, accs[n])
        if prompt and last:
            for g0 in range(0, 44, 4):
                ps = psum()
                for q in range(4):
                    P.transpose(ps[0:2, q * 128:(q + 1) * 128], FC[l][g0 + q].a, ident[:, :])
                st = wk5.alloc()
                P.copy(st[0:2, :], ps[0:2, :])
                pfree(ps)
                P.dma(o_pf[l, sq_i][:, g0 * 128:(g0 + 4) * 128], st[0:2, :], eng="pool", out_dma=True)
                wk5.free(st)
        wk.free(*ACTT)

    tiles = []
    for sq in range(NP):
        for ti in range(SEQ // T):
            tiles.append(dict(kind="p", N=T, seq=sq, pos0=ti * T, first=(ti == 0), last=(ti == SEQ // T - 1)))
    if NSq > 0:
        tiles.append(dict(kind="s", N=NST, seq=0, pos0=0, first=True, last=True))
    for tl in tiles:
        N = tl["N"]
        P.tag = "io"
        if tl["kind"] == "p":
            for b in range(NB):
                load_fm(xp[tl["seq"], tl["pos0"] + b * 128:tl["pos0"] + (b + 1) * 128, :], 128, XT, b * 128)
        else:
            load_fm(xs.rearrange("s t d -> (s t) d"), NST, XT, 0)
        for l in range(L):
            layer(l, tl)
        P.tag = "io"
        if tl["kind"] == "p":
            for b in range(NB):
                store_fm(XT, 128, b * 128, y_p[tl["seq"], tl["pos0"] + b * 128:tl["pos0"] + (b + 1) * 128, :])
        else:
            store_fm(XT, NST, 0, y_s.rearrange("s t d -> (s t) d"))

    P.emit()
    P.close()
    return nc, hc, P


OUT_NAMES = ["y_prompt", "y_sample", "p_swa_k", "p_swa_v", "p_lru_h", "p_lru_conv", "p_gla_s", "p_mem_k",
             "p_mem_v", "p_ffn_conv", "s_swa_k", "s_swa_v", "s_lru_h", "s_lru_conv", "s_gla_s", "s_ffn_conv"]
OUT_AXIS = [0, 0, 1, 1, 1, 1, 1, 1, 1, 1, 1, 1, 1, 1, 1, 1]
SAMPLE_IN = {"cache_swa_k": 1, "cache_swa_v": 1, "state_lru_h": 1, "state_lru_conv": 1, "state_gla_s": 1,
             "cache_mem_k": 1, "cache_mem_v": 1, "state_ffn_conv": 1}

CFG = dict(T=256, SEQ=2048, NP=2, NSq=16)
NCORES = 8


def make_in_maps(inputs, cfg, ncores, hc):
    NP, NSq = cfg["NP"], cfg["NSq"]
    maps = []
    for i in range(ncores):
        m = {}
        m["x_prompt"] = np.ascontiguousarray(inputs["x_prompt"][i * NP:(i + 1) * NP])
        m["mem_prompt"] = np.ascontiguousarray(inputs["mem_prompt"][i * NP:(i + 1) * NP])
        m["x_sample"] = np.ascontiguousarray(inputs["x_sample"][i * NSq:(i + 1) * NSq])
        for k in SAMPLE_IN:
            m[k] = np.ascontiguousarray(inputs[k][:, i * NSq:(i + 1) * NSq])
        for k in WEIGHT_SHAPES:
            m[k] = np.ascontiguousarray(inputs[k], dtype=np.float32)
        for k, v in hc.items():
            m["c_" + k] = v
        maps.append(m)
    return maps


def kernel(**inputs):
    inputs = {k: np.asarray(v) for k, v in inputs.items()}
    nc, hc, _ = build(CFG)
    in_maps = make_in_maps(inputs, CFG, NCORES, hc)
    res = run_bass_kernel_spmd(nc, in_maps, core_ids=list(range(NCORES)))
    outs = []
    for name, ax in zip(OUT_NAMES, OUT_AXIS):
        outs.append(np.concatenate([np.asarray(r[name]) for r in res.results], axis=ax).astype(np.float32))
    return tuple(outs)
```

```python
import numpy as np
import concourse.bass as bass
import concourse.mybir as mybir
from concourse.bass_utils import run_bass_kernel_spmd

F32 = mybir.dt.float32
F32R = mybir.dt.float32r
FAST_MM = True
ALU = mybir.AluOpType
AF = mybir.ActivationFunctionType

D = 1024
DEPTH = 2
PAST_LEN = 16384
SWA_W = 128
ROPE_DIM = 16
ROPE_THETA = 500000.0
LRU_W = 512
GLA_K = 256
GLA_V = 512
N_MEM = 256
D_FF = 2816
EPS = 1e-6
O_QA, O_KA, O_VA, O_XB, O_YB, O_QC, O_KC, O_VC, O_RC, O_AC, O_GA, O_GB, O_GC = (
    0, 512, 640, 768, 1280, 1792, 2048, 2304, 2816, 3328, 3344, 4368, 5392)
IN_COLS = 6416
NCH_FF = D_FF // 128

ENGS = ("pe", "act", "dve", "pool", "sp")
CENG = {"pe": 0, "act": 1, "dve": 2, "pool": 3, "sp": 4}


class Buf:
    __slots__ = ("t", "name", "lw", "rd", "rnd", "pool")

    def __init__(self, t, name, rnd=False):
        self.t = t
        self.name = name
        self.lw = None
        self.rd = []
        self.rnd = rnd
        self.pool = None

    def __getitem__(self, idx):
        return V(self, self.t[idx])

    @property
    def a(self):
        return V(self, self.t[:])


class V:
    __slots__ = ("b", "ap")

    def __init__(self, b, ap):
        self.b = b
        self.ap = ap

    def __getitem__(self, idx):
        return V(self.b, self.ap[idx])

    def re(self, pat, **kw):
        return V(self.b, self.ap.rearrange(pat, **kw))


def _ap(x):
    if isinstance(x, V):
        if FAST_MM and x.b is not None and x.b.rnd:
            return x.ap.bitcast(F32)
        return x.ap
    return x


def RND(v):
    return v


def _out(x):
    return x.ap if isinstance(x, V) else x


def _isr(x):
    return isinstance(x, V) and x.b is not None and x.b.rnd


def _bufs(xs):
    out = []
    for x in xs:
        if isinstance(x, V):
            if x.b is not None:
                out.append(x.b)
        elif isinstance(x, Buf):
            out.append(x)
    return out


class Op:
    __slots__ = ("eng", "fn", "deps", "is_dma", "idx", "pos", "waits", "dmaslot", "tag")

    def __init__(self, eng, fn, deps, is_dma):
        self.eng = eng
        self.fn = fn
        self.deps = deps
        self.is_dma = is_dma
        self.waits = []
        self.dmaslot = None


class Prog:
    def __init__(self, nc, dma_ring=24):
        self.nc = nc
        self.ops = []
        self.by_eng = {e: [] for e in ENGS}
        self.dma_ring = dma_ring
        self.final_wait = []
        self._stack = []
        self.tag = ""

    def sb(self, name, shape, dtype=F32, rnd=False):
        if rnd and FAST_MM:
            dtype = F32R
        cm = self.nc.sbuf_tensor(name, list(shape), dtype)
        t = cm.__enter__()
        self._stack.append(cm)
        return Buf(t, name, rnd)

    def ps(self, name, shape, dtype=F32):
        cm = self.nc.psum_tensor(name, list(shape), dtype)
        t = cm.__enter__()
        self._stack.append(cm)
        return Buf(t, name)

    def op(self, eng, fn, reads=(), writes=(), is_dma=False, out_dma=False):
        rb = _bufs(reads)
        wb = _bufs(writes)
        deps = set()
        for b in rb:
            if b.lw is not None:
                deps.add(b.lw)
        for b in wb:
            if b.lw is not None:
                deps.add(b.lw)
            deps.update(b.rd)
        o = Op(eng, fn, deps, is_dma)
        o.tag = self.tag
        o.idx = len(self.ops)
        self.ops.append(o)
        o.pos = len(self.by_eng[eng])
        self.by_eng[eng].append(o)
        for b in rb:
            b.rd.append(o.idx)
        for b in wb:
            b.lw = o.idx
            b.rd = []
        if out_dma:
            self.final_wait.append(o.idx)
        return o

    def mm(self, out, lhsT, rhs, start=True, stop=True, fast=None):
        o, l, r = _ap(out), _ap(lhsT), _ap(rhs)
        if fast is None:
            fast = FAST_MM
        if fast and _isr(lhsT) and _isr(rhs) and o.start_partition() == 0 and (r.shape[-1] % 2 == 0):
            l = lhsT.ap
            r = rhs.ap
        return self.op("pe", lambda e: e.matmul(o, l, r, start=start, stop=stop),
                       reads=[lhsT, rhs], writes=[out])

    def transpose(self, out, in_, ident):
        o, i, d = _ap(out), _ap(in_), _ap(ident)
        op = self.op("pe", lambda e: e.transpose(o, i, d), reads=[in_, ident], writes=[out])
        op.tag = op.tag + "|T"
        return op

    def act(self, out, in_, func, bias=None, scale=None):
        o, i = _out(out), _ap(in_)
        kw = {}
        rd = [in_]
        if bias is not None:
            kw["bias"] = _ap(bias)
            rd.append(bias)
        if scale is not None:
            kw["scale"] = _ap(scale)
            rd.append(scale)
        return self.op("act", lambda e: e.activation(o, i, func, **kw), reads=rd, writes=[out])

    def tt(self, out, in0, in1, op, eng="dve"):
        o, a, b = _out(out), _ap(in0), _ap(in1)
        return self.op(eng, lambda e: e.tensor_tensor(o, a, b, op), reads=[in0, in1], writes=[out])

    def ts(self, out, in0, s1, op0, s2=None, op1=None, eng="dve"):
        o, a = _out(out), _ap(in0)
        rd = [in0, s1, s2]
        s1a, s2a = _ap(s1), _ap(s2)
        if op1 is None:
            return self.op(eng, lambda e: e.tensor_scalar(o, a, s1a, None, op0), reads=rd, writes=[out])
        return self.op(eng, lambda e: e.tensor_scalar(o, a, s1a, s2a, op0, op1), reads=rd, writes=[out])

    def stt(self, out, in0, scalar, in1, op0, op1):
        o, a, b = _out(out), _ap(in0), _ap(in1)
        s = _ap(scalar)
        return self.op("dve", lambda e: e.scalar_tensor_tensor(o, a, s, b, op0, op1),
                       reads=[in0, in1, scalar], writes=[out])

    def scan(self, out, d0, d1, initial, op0=ALU.mult, op1=ALU.add):
        o, a, b = _ap(out), _ap(d0), _ap(d1)
        ini = _ap(initial)
        return self.op("dve", lambda e: e.tensor_tensor_scan(o, a, b, ini, op0, op1),
                       reads=[d0, d1, initial], writes=[out])

    def copy(self, out, in_, eng="act"):
        o, i = _out(out), _ap(in_)
        if eng == "act":
            return self.op("act", lambda e: e.copy(o, i), reads=[in_], writes=[out])
        return self.op(eng, lambda e: e.tensor_copy(o, i), reads=[in_], writes=[out])

    def recip(self, out, in_):
        o, i = _ap(out), _ap(in_)
        return self.op("dve", lambda e: e.reciprocal(o, i), reads=[in_], writes=[out])

    def memset(self, out, val, eng="pool"):
        o = _out(out)
        return self.op(eng, lambda e: e.memset(o, val), reads=[], writes=[out])

    def dma(self, out, in_, eng="sp", out_dma=False):
        o, i = _out(out), _out(in_)
        return self.op(eng, lambda e: e.dma_start(out=o, in_=i), reads=[in_], writes=[out],
                       is_dma=True, out_dma=out_dma)

    def emit(self):
        nc = self.nc
        ops = self.ops
        NE = len(ENGS)
        K = self.dma_ring
        needed = set()
        for o in ops:
            needed.update(o.deps)
        needed.update(self.final_wait)
        sems = {}
        for e in ENGS:
            cm = nc.semaphore("s_" + e)
            sems[e] = cm.__enter__()
            self._stack.append(cm)
        rings = {}
        for e in ENGS:
            if any(o.is_dma for o in self.by_eng[e]):
                r = []
                for k in range(K):
                    cm = nc.semaphore("d_%s_%d" % (e, k))
                    r.append(cm.__enter__())
                    self._stack.append(cm)
                rings[e] = r
        cnt = {e: 0 for e in ENGS}
        dcnt = {e: 0 for e in ENGS}
        token = {}
        for e in ENGS:
            for o in self.by_eng[e]:
                if o.is_dma:
                    n = dcnt[e]
                    dcnt[e] += 1
                    o.dmaslot = n
                    token[o.idx] = (rings[e][n % K], 16 * (n // K + 1))
                elif o.idx in needed:
                    cnt[e] += 1
                    token[o.idx] = (sems[e], cnt[e])
        known = {e: [-1] * NE for e in ENGS}
        known_dma = {e: set() for e in ENGS}
        vcs = [None] * len(ops)
        for o in ops:
            e = o.eng
            kn = known[e]
            waits = {}
            for d in sorted(o.deps):
                p = ops[d]
                if p.is_dma:
                    if d in known_dma[e]:
                        continue
                    known_dma[e].add(d)
                    s, v = token[d]
                else:
                    f = CENG[p.eng]
                    if p.eng == "pe" and e == "pe":
                        continue
                    if kn[f] >= p.pos:
                        continue
                    s, v = token[d]
                    if kn[f] < p.pos:
                        kn[f] = p.pos
                pv = vcs[d]
                for i in range(NE):
                    if pv[i] > kn[i]:
                        kn[i] = pv[i]
                k = id(s)
                if k not in waits or waits[k][1] < v:
                    waits[k] = (s, v)
            o.waits = list(waits.values())
            v = list(kn)
            if not o.is_dma:
                v[CENG[e]] = o.pos
            vcs[o.idx] = v
        by_eng = self.by_eng
        fb = {}
        for i in self.final_wait:
            s, v = token[i]
            k = id(s)
            if k not in fb or fb[k][1] < v:
                fb[k] = (s, v)
        final = list(fb.values())
        self.n_waits = sum(len(o.waits) for o in ops)

        def run(eng_name):
            def body(e):
                for o in by_eng[eng_name]:
                    if o.is_dma and o.dmaslot >= K:
                        e.wait_ge(rings[eng_name][o.dmaslot % K], 16 * (o.dmaslot // K))
                    for (s, v) in o.waits:
                        e.wait_ge(s, v)
                    ins = o.fn(e)
                    if o.idx in token:
                        ins.then_inc(token[o.idx][0], 16 if o.is_dma else 1)
                if eng_name == "sp":
                    for (s, v) in final:
                        e.wait_ge(s, v)
            return body

        with nc.Block() as block:
            block.tensor(run("pe"))
            block.scalar(run("act"))
            block.vector(run("dve"))
            block.gpsimd(run("pool"))
            block.sync(run("sp"))

    def close(self):
        while self._stack:
            cm = self._stack.pop()
            cm.__exit__(None, None, None)


class Pool:
    def __init__(self, P, name, n, shape, rnd=False):
        self.free_list = [P.sb("%s%d" % (name, i), shape, rnd=rnd) for i in range(n)]
        for b in self.free_list:
            b.pool = self
        self.name = name
        self.n = n

    def alloc(self):
        assert self.free_list, "pool %s exhausted" % self.name
        return self.free_list.pop(0)

    def free(self, *bs):
        for b in bs:
            b.pool.free_list.append(b)


def host_consts(cfg):
    T, SEQ, NSq = cfg["T"], cfg["SEQ"], cfg["NSq"]
    NST = 4 * NSq
    c = {}
    c["ident"] = np.eye(128, dtype=np.float32)
    c["ones"] = np.ones((128, 128), np.float32)
    bd = np.zeros((128, 128), np.float32)
    bd[:64, :64] = 1
    bd[64:, 64:] = 1
    c["bd64"] = bd
    prot = np.zeros((128, 128), np.float32)
    for m in range(128):
        d = m % 64
        if d < 8:
            prot[m + 8, m] = 1
        elif d < 16:
            prot[m - 8, m] = 1
    c["prot"] = prot
    s = np.arange(128)[:, None]
    t = np.arange(128)[None, :]
    mtri = (s <= t).astype(np.float32)
    c["mtri"] = mtri
    c["umat"] = (s > t).astype(np.float32)
    c["mcur4"] = np.tile(mtri, (1, 4))
    c["mprev4"] = np.tile((s >= t).astype(np.float32), (1, 4))
    tok_seq = np.arange(NST) // 4
    tok_t = np.arange(NST) % 4
    same = tok_seq[:, None] == tok_seq[None, :]
    ms = np.zeros((128, 128), np.float32)
    ms[:NST, :NST] = (same & (tok_t[:, None] <= tok_t[None, :]))
    us = np.zeros((128, 128), np.float32)
    us[:NST, :NST] = (same & (tok_t[:, None] > tok_t[None, :]))
    c["ms"] = ms
    c["us"] = us
    col_seq = np.repeat(np.arange(NSq), 16)
    col_t = np.tile(np.arange(4), NSq * 4)
    msc = np.zeros((128, 256), np.float32)
    msc[:, :NSq * 16] = (np.arange(128)[:, None] >= col_t[None, :])
    c["msc"] = msc
    msn = np.zeros((128, 256), np.float32)
    msn[:NST, :NSq * 16] = ((tok_seq[:, None] == col_seq[None, :]) & (tok_t[:, None] <= col_t[None, :]))
    c["msn"] = msn
    rm = np.zeros((128, 16), np.float32)
    rm[np.arange(NST), tok_seq] = 1
    c["rowmask"] = rm
    half = ROPE_DIM // 2
    inv = (ROPE_THETA ** (-np.arange(half, dtype=np.float32) * np.float32(2.0) / np.float32(ROPE_DIM))).astype(np.float32)

    def tables(pos):
        ang = pos.astype(np.float32)[None, :] * inv[:, None]
        cs, sn = np.cos(ang).astype(np.float32), np.sin(ang).astype(np.float32)
        C = np.ones((128, pos.shape[0]), np.float32)
        S = np.zeros((128, pos.shape[0]), np.float32)
        for hh in range(2):
            C[64 * hh:64 * hh + 8] = cs
            C[64 * hh + 8:64 * hh + 16] = cs
            S[64 * hh:64 * hh + 8] = -sn
            S[64 * hh + 8:64 * hh + 16] = sn
        return C, S
    c["cosp"], c["sinp"] = tables(np.arange(SEQ))
    cs_, ss_ = tables(PAST_LEN + np.arange(4))
    c["coss"] = np.tile(cs_, (1, NSq))
    c["sins"] = np.tile(ss_, (1, NSq))
    return c


RND_C = {"ones", "bd64", "prot", "mtri", "umat", "ms", "us"}
RND_W = {"lru_wa", "lru_wx", "w_in", "gla_wa2", "gla_ba", "w_branch_a", "w_branch_b", "w_branch_c", "w_out", "x_wq", "x_wk", "x_wv",
         "x_wo", "ffn_w_up", "ffn_w_down"}
WEIGHT_SHAPES = {
    "norm_mix_g": (DEPTH, D), "w_in": (DEPTH, D, IN_COLS), "swa_qn_g": (DEPTH, 64), "swa_kn_g": (DEPTH, 64),
    "swa_sink": (DEPTH, 8), "lru_conv_w": (DEPTH, 4, 512), "lru_conv_b": (DEPTH, 512),
    "lru_wa": (DEPTH, 8, 64, 64), "lru_ba": (DEPTH, 512), "lru_wx": (DEPTH, 8, 64, 64), "lru_bx": (DEPTH, 512),
    "lru_lambda": (DEPTH, 512), "gla_wa2": (DEPTH, 16, 256), "gla_ba": (DEPTH, 256), "gla_on_g": (DEPTH, 128),
    "w_branch_a": (DEPTH, 512, D), "w_branch_b": (DEPTH, 512, D), "w_branch_c": (DEPTH, 512, D),
    "w_out": (DEPTH, D, D), "norm_x_g": (DEPTH, D), "norm_mem_g": (DEPTH, D), "x_wq": (DEPTH, D, 512),
    "x_wk": (DEPTH, D, 512), "x_wv": (DEPTH, D, 512), "x_qn_g": (DEPTH, 128), "x_kn_g": (DEPTH, 128),
    "x_wo": (DEPTH, 512, D), "norm_ffn_g": (DEPTH, D), "ffn_w_up": (DEPTH, D, 2 * D_FF),
    "ffn_conv_w": (DEPTH, 3, 2 * D_FF), "ffn_conv_b": (DEPTH, 2 * D_FF), "ffn_w_down": (DEPTH, D_FF, D),
}


def build(cfg):
    T, SEQ, NP, NSq = cfg["T"], cfg["SEQ"], cfg["NP"], cfg["NSq"]
    NST = 4 * NSq
    NB = T // 128
    assert SEQ % T == 0 and T % 128 == 0
    nc = bass.Bass("TRN2", target_bir_lowering=False)
    if FAST_MM:
        nc.dge_precook = False
    P = Prog(nc)
    L = DEPTH

    def din(name, shape):
        return nc.dram_tensor(name, list(shape), F32, kind="ExternalInput").ap()

    def dout(name, shape):
        return nc.dram_tensor(name, list(shape), F32, kind="ExternalOutput").ap()

    xp = din("x_prompt", [NP, SEQ, D])
    xs = din("x_sample", [NSq, 4, D])
    csk = din("cache_swa_k", [L, NSq, 128, 2, 64])
    csv = din("cache_swa_v", [L, NSq, 128, 2, 64])
    slh = din("state_lru_h", [L, NSq, 512])
    slc = din("state_lru_conv", [L, NSq, 3, 512])
    sgs = din("state_gla_s", [L, NSq, 4, 64, 128])
    cmk = din("cache_mem_k", [L, NSq, 256, 4, 128])
    cmv = din("cache_mem_v", [L, NSq, 256, 4, 128])
    sfc = din("state_ffn_conv", [L, NSq, 2, 2 * D_FF])
    memp = din("mem_prompt", [NP, N_MEM, D])
    def dinr(name, shape):
        return nc.dram_tensor(name, list(shape), F32R if FAST_MM else F32, kind="ExternalInput").ap()

    Wd = {k: (dinr(k, s) if k in RND_W else din(k, s)) for k, s in WEIGHT_SHAPES.items()}
    hc = host_consts(cfg)
    Cd = {k: (dinr("c_" + k, v.shape) if k in RND_C else din("c_" + k, v.shape)) for k, v in hc.items()}

    y_p = dout("y_prompt", [NP, SEQ, D])
    y_s = dout("y_sample", [NSq, 4, D])
    o_pk = dout("p_swa_k", [L, NP, 128, 2, 64])
    o_pv = dout("p_swa_v", [L, NP, 128, 2, 64])
    o_ph = dout("p_lru_h", [L, NP, 512])
    o_pc = dout("p_lru_conv", [L, NP, 3, 512])
    o_ps = dout("p_gla_s", [L, NP, 4, 64, 128])
    o_pmk = dout("p_mem_k", [L, NP, 256, 4, 128])
    o_pmv = dout("p_mem_v", [L, NP, 256, 4, 128])
    o_pf = dout("p_ffn_conv", [L, NP, 2, 2 * D_FF])
    o_sk = dout("s_swa_k", [L, NSq, 128, 2, 64])
    o_sv = dout("s_swa_v", [L, NSq, 128, 2, 64])
    o_sh = dout("s_lru_h", [L, NSq, 512])
    o_sc = dout("s_lru_conv", [L, NSq, 3, 512])
    o_ss = dout("s_gla_s", [L, NSq, 4, 64, 128])
    o_sf = dout("s_ffn_conv", [L, NSq, 2, 2 * D_FF])

    wk = Pool(P, "wk", cfg.get("n_wk", 18), [128, T])
    wkr = Pool(P, "wkr", cfg.get("n_wkr", 31), [128, T], rnd=True)
    wk5 = Pool(P, "wk5", cfg.get("n_wk5", 4), [128, 512])
    wk5r = Pool(P, "wk5r", cfg.get("n_wk5r", 7), [128, 512], rnd=True)
    wpool = Pool(P, "wt", cfg.get("n_wt", 5), [128, 2048], rnd=True)
    psb = [P.ps("psb%d" % i, [128, 512]) for i in range(8)]
    ps_free = list(psb)

    def psum():
        assert ps_free, "psum exhausted"
        return ps_free.pop(0)

    def pfree(*bs):
        ps_free.extend(bs)

    Cs = {}
    for k, v in hc.items():
        if k in ("cosp", "sinp"):
            continue
        Cs[k] = P.sb("sc_" + k, list(v.shape), rnd=(k in RND_C))
        P.dma(Cs[k].a, Cd[k])
    ident, ones, bd64, prot = Cs["ident"], Cs["ones"], Cs["bd64"], Cs["prot"]
    epsb = P.sb("epsb", [128, 1])
    P.memset(epsb.a, EPS)
    oneb = P.sb("oneb", [128, 1])
    P.memset(oneb.a, 1.0)

    def load_cols(name, view, R, dup64=False):
        st = wk5.alloc()
        if dup64:
            P.dma(st[0:R, 0:64], view)
            P.dma(st[0:R, 64:128], view)
        else:
            P.dma(st[0:R, 0:128], view)
        ps = psum()
        P.transpose(ps[:, 0:R], st[0:R, 0:128], ident[0:R, 0:R])
        out = P.sb("pc_" + name, [128, R])
        P.copy(out.a, ps[:, 0:R])
        pfree(ps)
        wk5.free(st)
        return out

    g_mix = load_cols("gmix", Wd["norm_mix_g"].rearrange("l (c p) -> (l c) p", p=128), 16)
    g_x = load_cols("gx", Wd["norm_x_g"].rearrange("l (c p) -> (l c) p", p=128), 16)
    g_mem = load_cols("gmem", Wd["norm_mem_g"].rearrange("l (c p) -> (l c) p", p=128), 16)
    g_ffn = load_cols("gffn", Wd["norm_ffn_g"].rearrange("l (c p) -> (l c) p", p=128), 16)
    g_qn = load_cols("gqn", Wd["swa_qn_g"], 2, dup64=True)
    g_kn = load_cols("gkn", Wd["swa_kn_g"], 2, dup64=True)
    cw = load_cols("cw", Wd["lru_conv_w"].rearrange("l j (c p) -> (l j c) p", p=128), 32)
    cb = load_cols("cb", Wd["lru_conv_b"].rearrange("l (c p) -> (l c) p", p=128), 8)
    lba = load_cols("lba", Wd["lru_ba"].rearrange("l (c p) -> (l c) p", p=128), 8)
    lbx = load_cols("lbx", Wd["lru_bx"].rearrange("l (c p) -> (l c) p", p=128), 8)
    lam = load_cols("lam", Wd["lru_lambda"].rearrange("l (c p) -> (l c) p", p=128), 8)
    g_on = load_cols("gon", Wd["gla_on_g"], 2)
    g_xq = load_cols("gxq", Wd["x_qn_g"], 2)
    g_xk = load_cols("gxk", Wd["x_kn_g"], 2)
    fcb = load_cols("fcb", Wd["ffn_conv_b"].rearrange("l (c p) -> (l c) p", p=128), 88)
    fcw = P.sb("pc_fcw", [128, 264])
    fcw_view = Wd["ffn_conv_w"].rearrange("l j (c p) -> (l j c) p", p=128)
    for i in range(3):
        tmp = load_cols("fcw%d" % i, fcw_view[88 * i:88 * (i + 1), :], 88)
        P.copy(fcw[:, 88 * i:88 * (i + 1)], tmp.a, eng="pool")
    lbah = P.sb("lbah", [128, 8])
    lbxh = P.sb("lbxh", [128, 8])
    P.ts(lbah.a, lba.a, 0.5, ALU.mult)
    P.ts(lbxh.a, lbx.a, 0.5, ALU.mult)
    qtr = P.sb("qtr", [128, 1])
    P.memset(qtr.a, 0.25)
    m8sph = P.sb("m8sph", [128, 8])
    m8sp = P.sb("m8sp", [128, 8])
    P.act(m8sp.a, lam.a, AF.Exp, scale=-1.0)
    P.act(m8sp.a, m8sp.a, AF.Ln, bias=oneb[:, 0:1])
    P.ts(m8sp.a, m8sp.a, -8.0, ALU.mult)
    P.ts(m8sph.a, m8sp.a, 0.5, ALU.mult)
    sx = P.sb("sinkx", [64, 16])
    P.dma(sx.a, Wd["swa_sink"].rearrange("l h -> (l h)").partition_broadcast(64))
    P.act(sx.a, sx.a, AF.Exp)
    BDa = [[None] * 4 for _ in range(L)]
    BDx = [[None] * 4 for _ in range(L)]
    for l in range(L):
        for c in range(4):
            for nm, dst, src in (("a", BDa, Wd["lru_wa"]), ("x", BDx, Wd["lru_wx"])):
                b = P.sb("bd%s%d%d" % (nm, l, c), [128, 128], rnd=True)
                P.ts(b.a, ones[:, :], 0.0, ALU.mult)
                P.dma(b[0:64, 0:64], src[l, 2 * c])
                P.dma(b[64:128, 64:128], src[l, 2 * c + 1])
                dst[l][c] = b
    wa2 = [P.sb("wa2_%d" % l, [16, 256], rnd=True) for l in range(L)]
    gba = [P.sb("gba_%d" % l, [1, 256], rnd=True) for l in range(L)]
    for l in range(L):
        P.dma(wa2[l].a, Wd["gla_wa2"][l])
        P.dma(gba[l].a, Wd["gla_ba"][l:l + 1, :])

    XT = [P.sb("XT%d" % c, [128, T]) for c in range(8)]
    KT = [[P.sb("KT%d%d" % (l, kv), [64, 128 + T], rnd=True) for kv in range(2)] for l in range(L)]
    Vb = [P.sb("Vb%d" % l, [128, NB + 1, 128], rnd=True) for l in range(L)]
    XB = [[P.sb("XB%d%d" % (l, c), [128, 3 + T]) for c in range(4)] for l in range(L)]
    Hst = [[P.sb("Hst%d%d" % (l, c), [128, 16]) for c in range(4)] for l in range(L)]
    Sst = [[P.sb("Sst%d%d" % (l, j), [128, 128]) for j in range(2)] for l in range(L)]
    FC = [[P.sb("FC%d_%d" % (l, ch), [128, 2]) for ch in range(44)] for l in range(L)]
    MKT = [P.sb("MKT%d" % l, [128, 4, 256], rnd=True) for l in range(L)]
    MV = [P.sb("MV%d" % l, [128, 2, 512], rnd=True) for l in range(L)]
    QTg = [P.sb("QTg%d" % kv, [64, NB * 512], rnd=True) for kv in range(2)]
    OTg = [P.sb("OTg%d" % kv, [64, NB * 512], rnd=True) for kv in range(2)]
    KCt = P.sb("KCt", [128, NB, 256])
    VCt = P.sb("VCt", [128, NB, 512], rnd=True)
    Gt = P.sb("Gt", [128, NB, 256], rnd=True)

    def load_w(src, K, kp=128):
        nk = K // kp
        ncols = src.shape[1]
        assert nk * ncols <= 2048
        wt = wpool.alloc()
        wv = wt[0:kp, 0:nk * ncols].re("p (k c) -> p k c", k=nk)
        P.dma(wv, src.rearrange("(k p) c -> p k c", p=kp))
        return wt, wv

    def proj_fm(src2d, K, rhs_list, N, consume, GW=None, kp=128, Mch=128, ps_off=0):
        ncols = src2d.shape[1]
        nk = K // kp
        if GW is None:
            GW = max(128, (2048 // nk) // 128 * 128)
        ci = 0
        pend = None
        post = None
        for g0 in range(0, ncols, GW):
            gw = min(GW, ncols - g0)
            wt, wv = load_w(src2d[:, g0:g0 + gw], K, kp)
            for m0 in range(0, gw, Mch):
                mw = min(Mch, gw - m0)
                ps = psum()
                for k in range(nk):
                    P.mm(ps[0:mw, ps_off:ps_off + N], wv[:, k, m0:m0 + mw], rhs_list[k], start=(k == 0), stop=(k == nk - 1))
                if pend is not None:
                    r = consume(*pend)
                    if post is not None:
                        post()
                    post = r
                pend = (ci, mw, ps)
                ci += 1
            wpool.free(wt)
        if pend is not None:
            r = consume(*pend)
            if post is not None:
                post()
            if r is not None:
                r()

    def proj_tm(src2d, K, lhs_list, blocks, consume, GW=256):
        ncols = src2d.shape[1]
        nk = K // 128
        for g0 in range(0, ncols, GW):
            gw = min(GW, ncols - g0)
            wt, wv = load_w(src2d[:, g0:g0 + gw], K)
            for bi, (c0, bw) in enumerate(blocks):
                ps = psum()
                for k in range(nk):
                    P.mm(ps[0:bw, 0:gw], lhs_list[k][:, c0:c0 + bw], wv[:, k, :], start=(k == 0), stop=(k == nk - 1))
                consume(bi, bw, g0, gw, ps)
            wpool.free(wt)

    def rms_apply(srcs, N, lhs_ones, Dn, gcol, outs):
        ps = psum()
        n = len(srcs)
        for c, x in enumerate(srcs):
            sq = wkr.alloc()
            P.act(RND(sq[:, 0:N]), x, AF.Square)
            P.mm(ps[:, 0:N], lhs_ones, sq[:, 0:N], start=(c == 0), stop=(c == n - 1))
            wk.free(sq)
        rstd = wk.alloc()
        P.act(rstd[:, 0:N], ps[:, 0:N], AF.Ln, scale=1.0 / Dn, bias=epsb[:, 0:1])
        pfree(ps)
        P.act(rstd[:, 0:N], rstd[:, 0:N], AF.Exp, scale=-0.5)
        for c, x in enumerate(srcs):
            P.stt(RND(outs[c]), x, gcol(c), rstd[:, 0:N], ALU.mult, ALU.mult)
        wk.free(rstd)

    def load_fm(rows_ap, R, outs, col0):
        st0, st1 = wk5.alloc(), wk5.alloc()
        P.dma(st0[0:R, :], rows_ap[:, 0:512])
        P.dma(st1[0:R, :], rows_ap[:, 512:1024])
        for half, st in enumerate((st0, st1)):
            ps = psum()
            for q in range(4):
                P.transpose(ps[:, q * 128:q * 128 + R], st[0:R, q * 128:(q + 1) * 128], ident[0:R, 0:R])
            for q in range(4):
                P.copy(outs[half * 4 + q][:, col0:col0 + R], ps[:, q * 128:q * 128 + R])
            pfree(ps)
        wk5.free(st0, st1)

    def store_fm(srcs, R, col0, rows_ap, out_dma=True):
        n = len(srcs)
        for g0 in range(0, n, 4):
            gn = min(4, n - g0)
            ps = psum()
            for q in range(gn):
                P.transpose(ps[0:R, q * 128:(q + 1) * 128], srcs[g0 + q][:, col0:col0 + R], ident[:, :])
            st = wk5.alloc()
            P.copy(st[0:R, 0:gn * 128], ps[0:R, 0:gn * 128])
            pfree(ps)
            P.dma(rows_ap[:, g0 * 128:(g0 + gn) * 128], st[0:R, 0:gn * 128], eng="pool", out_dma=out_dma)
            wk5.free(st)

    def layer(l, tl):
        kind, N = tl["kind"], tl["N"]
        prompt = kind == "p"
        first, last, sq_i = tl["first"], tl["last"], tl["seq"]
        nseg = 1 if prompt else NSq
        Lg = N // nseg
        nb = NB if prompt else 1
        blocks = [(b * 128, 128) for b in range(NB)] if prompt else [(0, NST)]
        w_in = Wd["w_in"][l]

        def seg3(v, w):
            return v.re("p (s w) -> p s w", w=w)

        P.tag = "norm_mix"
        HT = [wkr.alloc() for _ in range(8)]
        rms_apply([XT[c][:, 0:N] for c in range(8)], N, ones[:, :], D,
                  lambda c: g_mix[:, l * 8 + c:l * 8 + c + 1], [HT[c][:, 0:N] for c in range(8)])
        HTn = [HT[c][:, 0:N] for c in range(8)]

        P.tag = "swa"
        if prompt:
            cosv = wk.alloc()
            sinv = wk.alloc()
            P.dma(cosv[:, 0:N], Cd["cosp"][:, tl["pos0"]:tl["pos0"] + N])
            P.dma(sinv[:, 0:N], Cd["sinp"][:, tl["pos0"]:tl["pos0"] + N])
            cosa, sina = cosv[:, 0:N], sinv[:, 0:N]
        else:
            cosa, sina = Cs["coss"][:, 0:N], Cs["sins"][:, 0:N]
        if not prompt:
            QTs = [wk5r.alloc() for _ in range(2)]
            OTs = [wk5r.alloc() for _ in range(2)]
            KTn = [wkr.alloc() for _ in range(2)]

        def normrope(ps, gcolv, writer):
            xn = wkr.alloc()
            rms_apply([ps[:, 0:N]], N, bd64[:, :], 64.0, lambda c: gcolv, [xn[:, 0:N]])
            pfree(ps)

            def post():
                ps2 = psum()
                P.mm(ps2[:, 0:N], prot[:, :], xn[:, 0:N])
                t1 = wk.alloc()
                P.tt(t1[:, 0:N], xn[:, 0:N], cosa, ALU.mult)
                t2 = wk.alloc()
                P.tt(t2[:, 0:N], ps2[:, 0:N], sina, ALU.mult)
                pfree(ps2)
                wk.free(xn)
                writer(t1, t2)
                wk.free(t1, t2)
            return post

        def q_consume(c, mw, ps):
            def writer(t1, t2):
                kv = c // 2
                for e in range(2):
                    j = 2 * (c % 2) + e
                    if prompt:
                        dst = QTg[kv][0:64, :].re("p (q j t) -> p q j t", j=4, t=128)[:, :, j, :]
                        a = t1[64 * e:64 * e + 64, 0:N].re("p (q t) -> p q t", t=128)
                        b = t2[64 * e:64 * e + 64, 0:N].re("p (q t) -> p q t", t=128)
                    else:
                        dst = QTs[kv][0:64, 0:NSq * 16].re("p (s j t) -> p s j t", j=4, t=4)[:, :, j, :]
                        a = t1[64 * e:64 * e + 64, 0:N].re("p (s t) -> p s t", t=4)
                        b = t2[64 * e:64 * e + 64, 0:N].re("p (s t) -> p s t", t=4)
                    P.tt(RND(dst), a, b, ALU.add)
            return normrope(ps, g_qn[:, l:l + 1], writer)

        kvout = {}
        if (not prompt) or last:
            kvout["k"] = wk.alloc()
            kvout["v"] = wk.alloc()
        NO = min(N, 128)

        def k_consume(c, mw, ps):
            def writer(t1, t2):
                for kv in range(2):
                    dst = KT[l][kv][0:64, 128:128 + N] if prompt else KTn[kv][0:64, 0:N]
                    P.tt(RND(dst), t1[64 * kv:64 * kv + 64, 0:N], t2[64 * kv:64 * kv + 64, 0:N], ALU.add)
                    if "k" in kvout:
                        P.tt(kvout["k"][0:64, kv * 128:kv * 128 + NO], t1[64 * kv:64 * kv + 64, N - NO:N],
                             t2[64 * kv:64 * kv + 64, N - NO:N], ALU.add, eng="pool")
            return normrope(ps, g_kn[:, l:l + 1], writer)
        proj_fm(w_in[:, O_QA:O_QA + 640], D, HTn, N,
                lambda c, mw, ps: q_consume(c, mw, ps) if c < 4 else k_consume(c - 4, mw, ps))

        if prompt:
            def v_consume(bi, bw, g0, gw, ps):
                P.copy(RND(Vb[l][:, 1 + bi, :]), ps[:, 0:128])
                if last and bi == NB - 1:
                    P.copy(kvout["v"][:, 0:128], ps[:, 0:128])
                pfree(ps)
            proj_tm(w_in[:, O_VA:O_VA + 128], D, HTn, blocks, v_consume)
            P.tag = "swa_attn"
            units = []
            for kv in range(2):
                for qb in range(NB):
                    kbs = []
                    if not (first and qb == 0):
                        kbs.append((qb * 128, qb, Cs["mprev4"]))
                    kbs.append((128 + qb * 128, qb + 1, Cs["mcur4"]))
                    for i, (kc0, vblk, mask) in enumerate(kbs):
                        units.append((kv, qb, kc0, vblk, mask, i == 0, i == len(kbs) - 1))

            def sscores(u):
                kv, qb, kc0 = u[0], u[1], u[2]
                pss = psum()
                P.mm(pss[:, 0:512], KT[l][kv][0:64, kc0:kc0 + 128], QTg[kv][0:64, qb * 512:(qb + 1) * 512])
                return pss
            def expmask(u, pss):
                pT_ = wk5r.alloc()
                P.act(pT_.a, pss.a, AF.Exp, scale=0.125)
                pfree(pss)
                P.tt(RND(pT_.a), pT_.a, u[4].a, ALU.mult)
                return pT_
            pT_n = expmask(units[0], sscores(units[0]))
            pss_n = sscores(units[1]) if len(units) > 1 else None
            pso = psd = None
            for ui, (kv, qb, kc0, vblk, mask, is_first, is_last) in enumerate(units):
                pT = pT_n
                if is_first:
                    pso, psd = psum(), psum()
                P.mm(pso[0:64, 0:512], Vb[l][:, vblk, 64 * kv:64 * kv + 64], pT.a, start=is_first, stop=is_last)
                P.mm(psd[0:64, 0:512], ones[:, 0:64], pT.a, start=is_first, stop=is_last)
                wk5.free(pT)
                if ui + 1 < len(units):
                    pT_n = expmask(units[ui + 1], pss_n)
                    pss_n = sscores(units[ui + 2]) if ui + 2 < len(units) else None
                if is_last:
                    den = wk5.alloc()
                    for j in range(4):
                        P.act(den[0:64, j * 128:(j + 1) * 128], psd[0:64, j * 128:(j + 1) * 128], AF.Ln,
                              bias=sx[:, l * 8 + kv * 4 + j:l * 8 + kv * 4 + j + 1])
                    P.act(den[0:64, :], den[0:64, :], AF.Exp, scale=-1.0)
                    P.tt(RND(OTg[kv][0:64, qb * 512:(qb + 1) * 512]), pso[0:64, :], den[0:64, :], ALU.mult)
                    wk5.free(den)
                    pfree(pso, psd)
            if last:
                st = wk5.alloc()
                ps = psum()
                for kv in range(2):
                    P.transpose(ps[:, 64 * kv:64 * kv + 64], kvout["k"][0:64, kv * 128:(kv + 1) * 128], ident[0:64, 0:64])
                P.copy(st[:, 0:128], ps[:, 0:128])
                pfree(ps)
                P.dma(o_pk[l, sq_i].rearrange("t k d -> t (k d)"), st[:, 0:128], eng="pool", out_dma=True)
                wk5.free(st)
                P.dma(o_pv[l, sq_i].rearrange("t k d -> t (k d)"), kvout["v"][:, 0:128], eng="pool", out_dma=True)
                wk.free(kvout["k"], kvout["v"])
            else:
                for kv in range(2):
                    P.copy(RND(KT[l][kv][0:64, 0:128]), KT[l][kv][0:64, N:N + 128], eng="pool")
                P.copy(RND(Vb[l][:, 0, :]), Vb[l][:, NB, :], eng="pool")
            OTv = lambda h: OTg[h // 4][0:64, :].re("p (q j t) -> p q j t", j=4, t=128)[:, :, h % 4, :]
            wk.free(cosv, sinv)
        else:
            Vn = wkr.alloc()

            def v_consume(bi, bw, g0, gw, ps):
                P.copy(RND(Vn[0:NST, 0:128]), ps[0:NST, 0:128])
                P.copy(kvout["v"][0:NST, 0:128], ps[0:NST, 0:128])
                pfree(ps)
            proj_tm(w_in[:, O_VA:O_VA + 128], D, HTn, blocks, v_consume)
            NC16 = NSq * 16
            pssc = [psum(), psum()]
            for s0 in range(0, NSq, 4):
                Kc4 = wk5.alloc()
                P.dma(Kc4[:, :].re("t (s f) -> t s f", f=128), csk[l, s0:s0 + 4].rearrange("s t k d -> t s (k d)"))
                ps = psum()
                for q in range(4):
                    P.transpose(ps[:, q * 128:(q + 1) * 128], Kc4[:, q * 128:(q + 1) * 128], ident[:, :])
                wk5.free(Kc4)
                for kv in range(2):
                    KTc4 = wk5r.alloc()
                    P.copy(RND(KTc4[0:64, :]), ps[64 * kv:64 * kv + 64, :])
                    for q in range(4):
                        s_ = s0 + q
                        P.mm(pssc[kv][:, 16 * s_:16 * s_ + 16], KTc4[0:64, q * 128:(q + 1) * 128],
                             QTs[kv][0:64, 16 * s_:16 * s_ + 16])
                    wk5.free(KTc4)
                pfree(ps)
            for kv in range(2):
                pTc = wk5r.alloc()
                P.act(pTc[:, 0:NC16], pssc[kv][:, 0:NC16], AF.Exp, scale=0.125)
                pfree(pssc[kv])
                P.tt(RND(pTc[:, 0:NC16]), pTc[:, 0:NC16], Cs["msc"][:, 0:NC16], ALU.mult, eng="pool")
                pssn = psum()
                P.mm(pssn[0:NST, 0:NC16], KTn[kv][0:64, 0:NST], QTs[kv][0:64, 0:NC16])
                pTn = wk5r.alloc()
                P.act(pTn[0:NST, 0:NC16], pssn[0:NST, 0:NC16], AF.Exp, scale=0.125)
                pfree(pssn)
                P.tt(RND(pTn[0:NST, 0:NC16]), pTn[0:NST, 0:NC16], Cs["msn"][0:NST, 0:NC16], ALU.mult, eng="pool")
                pso, psd = psum(), psum()
                P.mm(pso[0:64, 0:NC16], Vn[0:NST, 64 * kv:64 * kv + 64], pTn[0:NST, 0:NC16], start=True, stop=False)
                for s0 in range(0, NSq, 4):
                    Vc4 = wk5.alloc()
                    P.dma(Vc4[:, :].re("t (s f) -> t s f", f=128), csv[l, s0:s0 + 4].rearrange("s t k d -> t s (k d)"))
                    for q in range(4):
                        s_ = s0 + q
                        P.mm(pso[0:64, 16 * s_:16 * s_ + 16], Vc4[:, q * 128 + 64 * kv:q * 128 + 64 * kv + 64],
                             pTc[:, 16 * s_:16 * s_ + 16], start=False, stop=(s_ == NSq - 1))
                    wk5.free(Vc4)
                P.mm(psd[0:64, 0:NC16], ones[0:NST, 0:64], pTn[0:NST, 0:NC16], start=True, stop=False)
                P.mm(psd[0:64, 0:NC16], ones[:, 0:64], pTc[:, 0:NC16], start=False, stop=True)
                den = wk5.alloc()
                for j in range(4):
                    P.ts(den[0:64, 0:NC16].re("p (s j t) -> p s j t", j=4, t=4)[:, :, j, :],
                         psd[0:64, 0:NC16].re("p (s j t) -> p s j t", j=4, t=4)[:, :, j, :],
                         sx[:, l * 8 + kv * 4 + j:l * 8 + kv * 4 + j + 1], ALU.add)
                P.recip(den[0:64, 0:NC16], den[0:64, 0:NC16])
                P.tt(RND(OTs[kv][0:64, 0:NC16]), pso[0:64, 0:NC16], den[0:64, 0:NC16], ALU.mult)
                wk5.free(den, pTc, pTn)
                pfree(pso, psd)
            P.dma(o_sk[l, :, 0:124], csk[l, :, 4:128], eng="pool", out_dma=True)
            P.dma(o_sv[l, :, 0:124], csv[l, :, 4:128], eng="pool", out_dma=True)
            st = wk5.alloc()
            ps = psum()
            for kv in range(2):
                P.transpose(ps[0:NST, 64 * kv:64 * kv + 64], kvout["k"][0:64, kv * 128:kv * 128 + NST], ident[0:64, 0:64])
            P.copy(st[0:NST, 0:128], ps[0:NST, 0:128])
            pfree(ps)
            for s in range(NSq):
                P.dma(o_sk[l, s, 124:128].rearrange("t k d -> t (k d)"), st[4 * s:4 * s + 4, 0:128], eng="pool", out_dma=True)
                P.dma(o_sv[l, s, 124:128].rearrange("t k d -> t (k d)"), kvout["v"][4 * s:4 * s + 4, 0:128], eng="pool", out_dma=True)
            wk5.free(st)
            wk.free(Vn, *KTn)
            wk.free(kvout["k"], kvout["v"])
            wk5.free(*QTs)
            OTv = lambda h: OTs[h // 4][0:64, 0:NC16].re("p (s j t) -> p s j t", j=4, t=4)[:, :, h % 4, :]

        P.tag = "lru"
        OB = [wkr.alloc() for _ in range(4)]
        Hh = [wk.alloc() for _ in range(4)]
        if prompt:
            xb3 = lambda c: XB[l][c][:, 0:3 + N].re("p (s w) -> p s w", s=1)
        else:
            XBs = [wk.alloc() for _ in range(4)]
            xb3 = lambda c: XBs[c][:, 0:NSq * 7].re("p (s w) -> p s w", w=7)
            st = wk5.alloc()
            R = NSq * 3
            P.dma(st[0:R, :], slc[l].rearrange("s j f -> (s j) f"))
            ps = psum()
            for c in range(4):
                P.transpose(ps[:, c * 128:c * 128 + R], st[0:R, c * 128:(c + 1) * 128], ident[0:R, 0:R])
            for c in range(4):
                P.copy(xb3(c)[:, :, 0:3], ps[:, c * 128:c * 128 + R].re("p (s j) -> p s j", j=3))
            pfree(ps)
            P.dma(st[0:NSq, :], slh[l])
            ps = psum()
            for c in range(4):
                P.transpose(ps[:, c * 16:c * 16 + NSq], st[0:NSq, c * 128:(c + 1) * 128], ident[0:NSq, 0:NSq])
            for c in range(4):
                P.copy(Hst[l][c][:, 0:NSq], ps[:, c * 16:c * 16 + NSq])
            pfree(ps)
            wk5.free(st)
        if prompt and first:
            for c in range(4):
                P.memset(XB[l][c][:, 0:3], 0.0)
                P.memset(Hst[l][c][:, 0:1], 0.0)

        def xb_consume(c, mw, ps):
            X3 = xb3(c)
            P.copy(X3[:, :, 3:3 + Lg], seg3(ps[:, 0:N], Lg))
            pfree(ps)
            xc = wkr.alloc()
            xc3 = seg3(xc[:, 0:N], Lg)
            cwc = lambda j: cw[:, l * 16 + j * 4 + c:l * 16 + j * 4 + c + 1]
            P.ts(xc3, X3[:, :, 0:Lg], cwc(0), ALU.mult, cb[:, l * 4 + c:l * 4 + c + 1], ALU.add)
            for j in range(1, 4):
                P.stt(xc3, X3[:, :, j:j + Lg], cwc(j), xc3, ALU.mult, ALU.add)
            return lambda: (xb_postA(c, xc) if prompt else xb_post(c, xc))

        lruA = {}

        def xb_postA(c, xc):
            tr = wk.alloc()
            ti = wk.alloc()
            col = slice(l * 4 + c, l * 4 + c + 1)
            ps1 = psum()
            P.mm(ps1[:, 0:N], BDa[l][c].a, xc[:, 0:N])
            P.act(tr[:, 0:N], ps1[:, 0:N], AF.Tanh, bias=lbah[:, col], scale=0.5)
            pfree(ps1)
            ps2 = psum()
            P.mm(ps2[:, 0:N], BDx[l][c].a, xc[:, 0:N])
            P.act(ti[:, 0:N], ps2[:, 0:N], AF.Tanh, bias=lbxh[:, col], scale=0.5)
            pfree(ps2)
            a = wk.alloc()
            P.act(a[:, 0:N], tr[:, 0:N], AF.Exp, scale=m8sph[:, col], bias=m8sph[:, col])
            P.stt(ti[:, 0:N], ti[:, 0:N], 1.0, xc[:, 0:N], ALU.add, ALU.mult)
            wk.free(xc, tr)
            lruA[c] = (a, ti)

        def xb_postB(c):
            a, gi = lruA.pop(c)
            sq = wk.alloc()
            P.act(sq[:, 0:N], a[:, 0:N], AF.Square)
            P.act(sq[:, 0:N], sq[:, 0:N], AF.Sqrt, scale=-0.25, bias=qtr[:, 0:1])
            P.tt(gi[:, 0:N], gi[:, 0:N], sq[:, 0:N], ALU.mult)
            a3, b3 = seg3(a[:, 0:N], Lg), seg3(gi[:, 0:N], Lg)
            tmp = wk.alloc()
            t3 = tmp[:, 0:nseg].re("p (s o) -> p s o", o=1)
            h0 = Hst[l][c][:, 0:nseg].re("p (s o) -> p s o", o=1)
            P.tt(t3, a3[:, :, 0:1], h0, ALU.mult)
            P.tt(b3[:, :, 0:1], b3[:, :, 0:1], t3, ALU.add)
            P.memset(a3[:, :, 0:1], 0.0, eng="dve")
            P.scan(Hh[c][:, 0:N], a[:, 0:N], gi[:, 0:N], 0.0)
            P.copy(h0, seg3(Hh[c][:, 0:N], Lg)[:, :, Lg - 1:Lg], eng="pool")
            wk.free(sq, gi, a, tmp)

        def xb_post(c, xc):
            r = wk.alloc()
            ig = wk.alloc()
            ps1 = psum()
            P.mm(ps1[:, 0:N], BDa[l][c].a, xc[:, 0:N])
            P.act(r[:, 0:N], ps1[:, 0:N], AF.Sigmoid, bias=lba[:, l * 4 + c:l * 4 + c + 1])
            pfree(ps1)
            ps2 = psum()
            P.mm(ps2[:, 0:N], BDx[l][c].a, xc[:, 0:N])
            P.act(ig[:, 0:N], ps2[:, 0:N], AF.Sigmoid, bias=lbx[:, l * 4 + c:l * 4 + c + 1])
            pfree(ps2)
            a = wk.alloc()
            P.act(a[:, 0:N], r[:, 0:N], AF.Exp, scale=m8sp[:, l * 4 + c:l * 4 + c + 1])
            sq = r
            P.tt(sq[:, 0:N], a[:, 0:N], a[:, 0:N], ALU.mult, eng="pool")
            P.act(sq[:, 0:N], sq[:, 0:N], AF.Sqrt, scale=-1.0, bias=oneb[:, 0:1])
            P.tt(ig[:, 0:N], ig[:, 0:N], xc[:, 0:N], ALU.mult)
            P.tt(ig[:, 0:N], ig[:, 0:N], sq[:, 0:N], ALU.mult)
            a3, b3 = seg3(a[:, 0:N], Lg), seg3(ig[:, 0:N], Lg)
            tmp = wk.alloc()
            t3 = tmp[:, 0:nseg].re("p (s o) -> p s o", o=1)
            h0 = Hst[l][c][:, 0:nseg].re("p (s o) -> p s o", o=1)
            P.tt(t3, a3[:, :, 0:1], h0, ALU.mult)
            P.tt(b3[:, :, 0:1], b3[:, :, 0:1], t3, ALU.add)
            P.memset(a3[:, :, 0:1], 0.0, eng="dve")
            P.scan(Hh[c][:, 0:N], a[:, 0:N], ig[:, 0:N], 0.0)
            P.copy(h0, seg3(Hh[c][:, 0:N], Lg)[:, :, Lg - 1:Lg], eng="pool")
            wk.free(xc, r, ig, a, tmp)

        def yb_consume(c, mw, ps):
            g = wk.alloc()
            P.act(g[:, 0:N], ps[:, 0:N], AF.Gelu_apprx_tanh)
            pfree(ps)
            P.tt(RND(OB[c][:, 0:N]), g[:, 0:N], Hh[c][:, 0:N], ALU.mult)
            wk.free(g)
        if prompt:
            proj_fm(w_in[:, O_XB:O_XB + 512], D, HTn, N, xb_consume)
            for c in range(4):
                xb_postB(c)
            proj_fm(w_in[:, O_YB:O_YB + 512], D, HTn, N, yb_consume)
        else:
            proj_fm(w_in[:, O_XB:O_XB + 1024], D, HTn, N,
                    lambda c, mw, ps: xb_consume(c, mw, ps) if c < 4 else yb_consume(c - 4, mw, ps))
        wk.free(*Hh)
        if prompt:
            if last:
                store_fm([Hst[l][c] for c in range(4)], 1, 0, o_ph[l, sq_i:sq_i + 1, :])
                store_fm([XB[l][c] for c in range(4)], 3, N, o_pc[l, sq_i])
            else:
                for c in range(4):
                    P.copy(XB[l][c][:, 0:3], XB[l][c][:, N:N + 3], eng="pool")
        else:
            store_fm([Hst[l][c] for c in range(4)], NSq, 0, o_sh[l])
            cst = [wk.alloc() for _ in range(4)]
            for c in range(4):
                P.copy(cst[c][:, 0:NSq * 3].re("p (s j) -> p s j", j=3), xb3(c)[:, :, 4:7], eng="pool")
            store_fm(cst, NSq * 3, 0, o_sc[l].rearrange("s j f -> (s j) f"))
            wk.free(*cst)
            wk.free(*XBs)

        P.tag = "gla"
        qcT = [wk.alloc() for _ in range(2)]
        kcT = [wk.alloc() for _ in range(2)]

        def qc_consume(c, mw, ps):
            P.act(qcT[c][:, 0:N], ps[:, 0:N], AF.Copy, scale=0.125)
            pfree(ps)

        def kc_consume(c, mw, ps):
            P.copy(kcT[c][:, 0:N], ps[:, 0:N])
            pfree(ps)
        proj_fm(w_in[:, O_QC:O_QC + 512], D, HTn, N,
                lambda c, mw, ps: qc_consume(c, mw, ps) if c < 2 else kc_consume(c - 2, mw, ps))

        def kct_consume(bi, bw, g0, gw, ps):
            P.copy(KCt[0:bw, bi, g0:g0 + gw], ps[0:bw, 0:gw])
            pfree(ps)
        proj_tm(w_in[:, O_KC:O_KC + 256], D, HTn, blocks, kct_consume)

        def vct_consume(bi, bw, g0, gw, ps):
            P.copy(RND(VCt[0:bw, bi, g0:g0 + gw]), ps[0:bw, 0:gw])
            pfree(ps)
        proj_tm(w_in[:, O_VC:O_VC + 512], D, HTn, blocks, vct_consume)
        act_ = wkr.alloc()

        def ac_consume(c, mw, ps):
            P.copy(RND(act_[0:16, 0:N]), ps[0:16, 0:N])
            pfree(ps)
        SR = [wk.alloc() for _ in range(4)]

        def rc_consume(c, mw, ps):
            P.act(SR[c][:, 0:N], ps[:, 0:N], AF.Silu)
            pfree(ps)
        proj_fm(w_in[:, O_AC:O_AC + 16], D, HTn, N, ac_consume)
        for bi, (c0, bw) in enumerate(blocks):
            ps = psum()
            P.mm(ps[0:bw, 0:256], act_[0:16, c0:c0 + bw], wa2[l].a, start=True, stop=False)
            P.mm(ps[0:bw, 0:256], ones[0:1, 0:bw], gba[l].a, start=False, stop=True)
            e = wk5.alloc()
            P.act(e[0:bw, 0:256], ps[0:bw, 0:256], AF.Exp, scale=-1.0)
            pfree(ps)
            P.act(e[0:bw, 0:256], e[0:bw, 0:256], AF.Ln, bias=oneb[0:bw, 0:1])
            P.ts(RND(Gt[0:bw, bi, :]), e[0:bw, 0:256], -1.0 / 16.0, ALU.mult)
            wk5.free(e)
        wk.free(act_)
        proj_fm(w_in[:, O_RC:O_RC + 512], D, HTn, N, rc_consume)
        oT = [wk.alloc() for _ in range(4)]
        qt = [wkr.alloc() for _ in range(2)]
        kt = [wkr.alloc() for _ in range(2)]
        Mm = Cs["mtri"] if prompt else Cs["ms"]
        Um = Cs["umat"] if prompt else Cs["us"]
        if prompt and first:
            for j in range(2):
                P.memset(Sst[l][j].a, 0.0)
        P.tag = "gla_blk"
        for bi, (c0, bw) in enumerate(blocks):
            psU = psum()
            P.mm(psU[0:bw, 0:256], Um[0:bw, 0:bw], Gt[0:bw, bi, :])
            kp = wk5r.alloc()
            P.act(kp[0:bw, 0:256], psU[0:bw, 0:256], AF.Exp)
            pfree(psU)
            P.tt(RND(kp[0:bw, 0:256]), kp[0:bw, 0:256], KCt[0:bw, bi, :], ALU.mult)
            eP = [wk.alloc() for _ in range(2)]
            for j in range(2):
                psC = psum()
                P.mm(psC[:, 0:bw], Gt[0:bw, bi, 128 * j:128 * (j + 1)], Mm[0:bw, 0:bw])
                eN = wk.alloc()
                P.act(eP[j][:, 0:bw], psC[:, 0:bw], AF.Exp)
                P.act(eN[:, 0:bw], psC[:, 0:bw], AF.Exp, scale=-1.0)
                pfree(psC)
                P.tt(RND(qt[j][:, c0:c0 + bw]), qcT[j][:, c0:c0 + bw], eP[j][:, 0:bw], ALU.mult)
                P.tt(RND(kt[j][:, c0:c0 + bw]), kcT[j][:, c0:c0 + bw], eN[:, 0:bw], ALU.mult)
                wk.free(eN)
            if prompt:
                psO = psum()
                psOh = [psO[:, h * 128:h * 128 + bw] for h in range(4)]
            else:
                psOb = [psum() for _ in range(4)]
                psOh = [psOb[h][:, 0:bw] for h in range(4)]
            for h in range(4):
                j, r0 = h // 2, 64 * (h % 2)
                psA = psum()
                P.mm(psA[0:bw, 0:bw], kt[j][r0:r0 + 64, c0:c0 + bw], qt[j][r0:r0 + 64, c0:c0 + bw])
                att = wkr.alloc()
                P.tt(RND(att[0:bw, 0:bw]), psA[0:bw, 0:bw], Mm[0:bw, 0:bw], ALU.mult)
                pfree(psA)
                P.mm(psOh[h], VCt[0:bw, bi, 128 * h:128 * (h + 1)], att[0:bw, 0:bw], start=True, stop=False)
                if prompt:
                    P.mm(psOh[h], Sst[l][j][r0:r0 + 64, :], qt[j][r0:r0 + 64, c0:c0 + bw], start=False, stop=True, fast=False)
                    P.copy(oT[h][:, c0:c0 + bw], psOh[h])
                wk.free(att)
            if prompt:
                pfree(psO)
                psS = psum()
                for h in range(4):
                    j, r0 = h // 2, 64 * (h % 2)
                    P.mm(psS[r0:r0 + 64, j * 128:(j + 1) * 128], kp[0:bw, 64 * h:64 * h + 64],
                         VCt[0:bw, bi, 128 * h:128 * (h + 1)], fast=False)
                for j in range(2):
                    P.stt(Sst[l][j].a, Sst[l][j].a, eP[j][:, bw - 1:bw], psS[:, j * 128:(j + 1) * 128], ALU.mult, ALU.add)
                pfree(psS)
            else:
                for s in range(NSq):
                    Ss = wk5.alloc()
                    Ss3 = Ss[:, 0:256].re("p (j v) -> p j v", j=2)
                    P.dma(Ss3, sgs[l, s].rearrange("(j h) d v -> (h d) j v", h=2))
                    for h in range(4):
                        j, r0 = h // 2, 64 * (h % 2)
                        P.mm(psOb[h][:, 4 * s:4 * s + 4], Ss3[r0:r0 + 64, j, :], qt[j][r0:r0 + 64, 4 * s:4 * s + 4],
                             start=False, stop=(s == NSq - 1), fast=False)
                    kpm = wk5r.alloc()
                    P.ts(RND(kpm[0:bw, 0:256]), kp[0:bw, 0:256], Cs["rowmask"][0:bw, s:s + 1], ALU.mult, eng="pool")
                    psS = psum()
                    for h in range(4):
                        j, r0 = h // 2, 64 * (h % 2)
                        P.mm(psS[r0:r0 + 64, j * 128:(j + 1) * 128], kpm[0:bw, 64 * h:64 * h + 64],
                             VCt[0:bw, 0, 128 * h:128 * (h + 1)], fast=False)
                    so = wk5.alloc()
                    for j in range(2):
                        P.stt(so[:, j * 128:(j + 1) * 128], Ss3[:, j, :], eP[j][:, 4 * s + 3:4 * s + 4],
                              psS[:, j * 128:(j + 1) * 128], ALU.mult, ALU.add)
                    pfree(psS)
                    P.dma(o_ss[l, s].rearrange("(j h) d v -> (h d) j v", h=2),
                          so[:, 0:256].re("p (j v) -> p j v", j=2), eng="pool", out_dma=True)
                    wk5.free(kpm, so, Ss)
                for h in range(4):
                    P.copy(oT[h][:, c0:c0 + bw], psOh[h])
                pfree(*psOb)
            wk.free(*eP)
            wk5.free(kp)
        if prompt and last:
            for j in range(2):
                P.dma(o_ps[l, sq_i, 2 * j:2 * j + 2].rearrange("h d v -> (h d) v"), Sst[l][j].a, eng="pool", out_dma=True)
        wk.free(*qcT, *kcT, *qt, *kt)
        OC = [wkr.alloc() for _ in range(4)]
        sqs = [wkr.alloc() for _ in range(4)]
        for h in range(4):
            P.act(sqs[h][:, 0:N], oT[h][:, 0:N], AF.Square)
        pns = [psum() for _ in range(4)]
        for h in range(4):
            P.mm(pns[h][:, 0:N], ones[:, :], sqs[h][:, 0:N])
        wk.free(*sqs)
        rs = [wk.alloc() for _ in range(4)]
        for h in range(4):
            P.act(rs[h][:, 0:N], pns[h][:, 0:N], AF.Ln, scale=1.0 / 128.0, bias=epsb[:, 0:1])
        pfree(*pns)
        for h in range(4):
            P.act(rs[h][:, 0:N], rs[h][:, 0:N], AF.Exp, scale=-0.5)
        for h in range(4):
            P.stt(rs[h][:, 0:N], oT[h][:, 0:N], g_on[:, l:l + 1], rs[h][:, 0:N], ALU.mult, ALU.mult)
        for h in range(4):
            P.tt(RND(OC[h][:, 0:N]), rs[h][:, 0:N], SR[h][:, 0:N], ALU.mult)
        wk.free(*rs)
        wk.free(*oT, *SR)

        P.tag = "merge"
        MG = [wkr.alloc() for _ in range(8)]
        brs = (
            (O_GA, Wd["w_branch_a"][l], 64, [OTv(h) for h in range(8)]),
            (O_GB, Wd["w_branch_b"][l], 128, [OB[c][:, 0:N] for c in range(4)]),
            (O_GC, Wd["w_branch_c"][l], 128, [OC[c][:, 0:N] for c in range(4)]),
        )
        for bi_, (gofs, wbr, kp_, rl) in enumerate(brs):
            sig = [None] * 8

            def g_consume(c, mw, ps):
                s = wk.alloc()
                P.act(s[:, 0:N], ps[:, 0:N], AF.Sigmoid)
                pfree(ps)
                sig[c] = s
            proj_fm(w_in[:, gofs:gofs + 1024], D, HTn, N, g_consume)

            def b_consume(c, mw, ps):
                if bi_ == 0:
                    P.tt(RND(MG[c][:, 0:N]), ps[:, 0:N], sig[c][:, 0:N], ALU.mult)
                else:
                    P.tt(sig[c][:, 0:N], ps[:, 0:N], sig[c][:, 0:N], ALU.mult)
                    P.tt(RND(MG[c][:, 0:N]), MG[c][:, 0:N], sig[c][:, 0:N], ALU.add)
                pfree(ps)
                wk.free(sig[c])
            proj_fm(wbr, 512, rl, N, b_consume, kp=kp_)
        wk.free(*OB, *OC, *HT)
        if not prompt:
            wk5.free(*OTs)

        def res_consume(c, mw, ps):
            P.tt(XT[c][:, 0:N], XT[c][:, 0:N], ps[:, 0:N], ALU.add)
            pfree(ps)
        P.tag = "wout"
        proj_fm(Wd["w_out"][l], D, [MG[c][:, 0:N] for c in range(8)], N, res_consume)
        wk.free(*MG)

        P.tag = "xattn"
        if prompt and first:
            MT = [wk.alloc() for _ in range(8)]
            for mb in range(2):
                load_fm(memp[sq_i, mb * 128:(mb + 1) * 128, :], 128, MT, mb * 128)
            MN = [wkr.alloc() for _ in range(8)]
            rms_apply([MT[c][:, 0:256] for c in range(8)], 256, ones[:, :], D,
                      lambda c: g_mem[:, l * 8 + c:l * 8 + c + 1], [MN[c][:, 0:256] for c in range(8)])
            wk.free(*MT)
            MNn = [MN[c][:, 0:256] for c in range(8)]

            def mk_consume(c, mw, ps):
                rms_apply([ps[:, 0:256]], 256, ones[:, :], 128.0, lambda cc: g_xk[:, l:l + 1], [MKT[l][:, c, :]])
                pfree(ps)
            proj_fm(Wd["x_wk"][l], D, MNn, 256, mk_consume)

            def mv_consume(bi, bw, g0, gw, ps):
                P.copy(RND(MV[l][:, bi, g0:g0 + gw]), ps[:, 0:gw])
                pfree(ps)
            proj_tm(Wd["x_wv"][l], D, MNn, [(0, 128), (128, 128)], mv_consume)
            wk.free(*MN)
            for mb in range(2):
                st = wk5.alloc()
                P.copy(st.a, MV[l][:, mb, :], eng="pool")
                P.dma(o_pmv[l, sq_i, mb * 128:(mb + 1) * 128].rearrange("t h d -> t (h d)"), st.a, eng="pool", out_dma=True)
                wk5.free(st)
                ps = psum()
                for h in range(4):
                    P.transpose(ps[:, h * 128:(h + 1) * 128], MKT[l][:, h, mb * 128:(mb + 1) * 128], ident[:, :])
                st = wk5.alloc()
                P.copy(st.a, ps.a)
                pfree(ps)
                P.dma(o_pmk[l, sq_i, mb * 128:(mb + 1) * 128].rearrange("t h d -> t (h d)"), st.a, eng="pool", out_dma=True)
                wk5.free(st)
        HT = [wkr.alloc() for _ in range(8)]
        rms_apply([XT[c][:, 0:N] for c in range(8)], N, ones[:, :], D,
                  lambda c: g_x[:, l * 8 + c:l * 8 + c + 1], [HT[c][:, 0:N] for c in range(8)])
        HTn = [HT[c][:, 0:N] for c in range(8)]
        QX = [wkr.alloc() for _ in range(4)]

        def qx_consume(c, mw, ps):
            rms_apply([ps[:, 0:N]], N, ones[:, :], 128.0, lambda cc: g_xq[:, l:l + 1], [QX[c][:, 0:N]])
            pfree(ps)
        proj_fm(Wd["x_wq"][l], D, HTn, N, qx_consume)
        wk.free(*HT)
        OX = [wkr.alloc() for _ in range(4)]
        xsc = 128.0 ** -0.5
        if prompt:
            assert N <= 256

            def xscores(h):
                pss = psum()
                for mb in range(2):
                    P.mm(pss[:, mb * 256:mb * 256 + N], MKT[l][:, h, mb * 128:(mb + 1) * 128], QX[h][:, 0:N])
                return pss
            def xexp(pss):
                pT_ = wk5r.alloc()
                P.act(pT_[:, :].re("p (m n) -> p m n", m=2)[:, :, 0:N], pss[:, :].re("p (m n) -> p m n", m=2)[:, :, 0:N],
                      AF.Exp, scale=xsc)
                pfree(pss)
                return pT_
            pT_n = xexp(xscores(0))
            pss_n = xscores(1)
            for h in range(4):
                pT = pT_n
                pso, psd = psum(), psum()
                for mb in range(2):
                    P.mm(pso[:, 0:N], MV[l][:, mb, 128 * h:128 * (h + 1)], pT[:, mb * 256:mb * 256 + N], start=(mb == 0), stop=(mb == 1))
                    P.mm(psd[:, 0:N], ones[:, :], pT[:, mb * 256:mb * 256 + N], start=(mb == 0), stop=(mb == 1))
                wk5.free(pT)
                if h < 3:
                    pT_n = xexp(pss_n)
                    pss_n = xscores(h + 2) if h + 2 < 4 else None
                rd = wk.alloc()
                P.act(rd[:, 0:N], psd[:, 0:N], AF.Ln)
                P.act(rd[:, 0:N], rd[:, 0:N], AF.Exp, scale=-1.0)
                P.tt(RND(OX[h][:, 0:N]), pso[:, 0:N], rd[:, 0:N], ALU.mult)
                wk.free(rd)
                pfree(pso, psd)
        else:
            oxP, dP = psum(), psum()
            for s in range(NSq):
                for mb in range(2):
                    Kc2, Vc2, MKs = wk5.alloc(), wk5.alloc(), wk5r.alloc()
                    P.dma(Kc2.a, cmk[l, s, mb * 128:(mb + 1) * 128].rearrange("t h d -> t (h d)"))
                    P.dma(Vc2.a, cmv[l, s, mb * 128:(mb + 1) * 128].rearrange("t h d -> t (h d)"))
                    ps = psum()
                    for h in range(4):
                        P.transpose(ps[:, h * 128:(h + 1) * 128], Kc2[:, h * 128:(h + 1) * 128], ident[:, :])
                    P.copy(RND(MKs.a), ps.a)
                    pfree(ps)
                    pss = psum()
                    for h in range(4):
                        P.mm(pss[:, h * 4:h * 4 + 4], MKs[:, h * 128:(h + 1) * 128], QX[h][:, 4 * s:4 * s + 4])
                    pT = wkr.alloc()
                    P.act(RND(pT[:, 0:16]), pss[:, 0:16], AF.Exp, scale=xsc)
                    pfree(pss)
                    for h in range(4):
                        P.mm(oxP[:, mb * 256 + 16 * s + 4 * h:mb * 256 + 16 * s + 4 * h + 4], Vc2[:, 128 * h:128 * (h + 1)],
                             pT[:, h * 4:h * 4 + 4])
                    P.mm(dP[:, mb * 256 + 16 * s:mb * 256 + 16 * s + 16], ones[:, :], pT[:, 0:16])
                    wk.free(pT)
                    wk5.free(Kc2, Vc2, MKs)
            NC16 = NSq * 16
            rd, oxs = wk5.alloc(), wk5.alloc()
            P.copy(rd[:, 0:NC16], dP[:, 0:NC16])
            P.tt(rd[:, 0:NC16], rd[:, 0:NC16], dP[:, 256:256 + NC16], ALU.add)
            P.recip(rd[:, 0:NC16], rd[:, 0:NC16])
            P.copy(oxs[:, 0:NC16], oxP[:, 0:NC16])
            P.tt(oxs[:, 0:NC16], oxs[:, 0:NC16], oxP[:, 256:256 + NC16], ALU.add)
            for h in range(4):
                P.tt(RND(OX[h][:, 0:N].re("p (s t) -> p s t", t=4)),
                     oxs[:, 0:NC16].re("p (s h t) -> p s h t", h=4, t=4)[:, :, h, :],
                     rd[:, 0:NC16].re("p (s h t) -> p s h t", h=4, t=4)[:, :, h, :], ALU.mult)
            wk5.free(rd, oxs)
            pfree(oxP, dP)
        wk.free(*QX)
        proj_fm(Wd["x_wo"][l], 512, [OX[h][:, 0:N] for h in range(4)], N, res_consume)
        wk.free(*OX)

        P.tag = "ffn"
        HT = [wkr.alloc() for _ in range(8)]
        rms_apply([XT[c][:, 0:N] for c in range(8)], N, ones[:, :], D,
                  lambda c: g_ffn[:, l * 8 + c:l * 8 + c + 1], [HT[c][:, 0:N] for c in range(8)])
        HTn = [HT[c][:, 0:N] for c in range(8)]
        if prompt and first:
            for ch in range(44):
                P.memset(FC[l][ch].a, 0.0)
        ACTT = [wkr.alloc() for _ in range(NCH_FF)]
        R2 = NSq * 2
        sfc2 = sfc[l].rearrange("s j f -> (s j) f")
        osf2 = o_sf[l].rearrange("s j f -> (s j) f")
        pair = {}
        WL = 2 + Lg
        gbuf = {}

        def up_consume_f(half):
            def up_consume(c, mw, ps):
                ch = half * NCH_FF + c
                if prompt:
                    y = wk.alloc()
                    fw = lambda j: fcw[:, l * 132 + j * 44 + ch:l * 132 + j * 44 + ch + 1]
                    P.act(y[:, 0:N], ps[:, 2:N + 2], AF.Identity, scale=fw(2), bias=fcb[:, l * 44 + ch:l * 44 + ch + 1])
                    P.copy(ps[:, 0:2], FC[l][ch][:, 0:2], eng="dve")
                    P.stt(y[:, 0:N], ps[:, 1:N + 1], fw(1), y[:, 0:N], ALU.mult, ALU.add)
                    P.stt(y[:, 0:N], ps[:, 0:N], fw(0), y[:, 0:N], ALU.mult, ALU.add)
                    P.copy(FC[l][ch][:, 0:2], ps[:, N:N + 2], eng="dve")
                    pfree(ps)

                    def post_p():
                        if half == 0:
                            P.act(ACTT[c][:, 0:N], y[:, 0:N], AF.Silu)
                        else:
                            P.tt(RND(ACTT[c][:, 0:N]), ACTT[c][:, 0:N], y[:, 0:N], ALU.mult)
                        wk.free(y)
                    return post_p
                ps3 = seg3(ps[:, 0:N], Lg)
                if prompt:
                    car3 = FC[l][ch][:, 0:2].re("p (s j) -> p s j", j=2)
                else:
                    if ch % 2 == 0:
                        st = wk5.alloc()
                        P.dma(st[0:R2, 0:256], sfc2[:, ch * 128:(ch + 2) * 128])
                        psx = psum()
                        for q in range(2):
                            P.transpose(psx[:, q * 32:q * 32 + R2], st[0:R2, q * 128:(q + 1) * 128], ident[0:R2, 0:R2])
                        fc2, fn2 = wk.alloc(), wk.alloc()
                        P.copy(fc2[:, 0:64], psx[:, 0:64])
                        pfree(psx)
                        wk5.free(st)
                        pair["fc"], pair["fn"] = fc2, fn2
                    q_ = ch % 2
                    car3 = pair["fc"][:, q_ * 32:q_ * 32 + R2].re("p (s j) -> p s j", j=2)
                y = wk.alloc()
                y3 = seg3(y[:, 0:N], Lg)
                fw = lambda j: fcw[:, l * 132 + j * 44 + ch:l * 132 + j * 44 + ch + 1]
                P.act(y3, ps3, AF.Identity, scale=fw(2), bias=fcb[:, l * 44 + ch:l * 44 + ch + 1])
                P.stt(y3[:, :, 1:Lg], ps3[:, :, 0:Lg - 1], fw(1), y3[:, :, 1:Lg], ALU.mult, ALU.add)
                P.stt(y3[:, :, 2:Lg], ps3[:, :, 0:Lg - 2], fw(0), y3[:, :, 2:Lg], ALU.mult, ALU.add)
                P.stt(y3[:, :, 0:1], car3[:, :, 1:2], fw(1), y3[:, :, 0:1], ALU.mult, ALU.add)
                P.stt(y3[:, :, 0:2], car3[:, :, 0:2], fw(0), y3[:, :, 0:2], ALU.mult, ALU.add)
                if prompt:
                    P.copy(car3, ps3[:, :, Lg - 2:Lg])
                else:
                    P.copy(pair["fn"][:, q_ * 32:q_ * 32 + R2].re("p (s j) -> p s j", j=2), ps3[:, :, Lg - 2:Lg])
                    if q_ == 1:
                        psx = psum()
                        for q in range(2):
                            P.transpose(psx[0:R2, q * 128:(q + 1) * 128], pair["fn"][:, q * 32:q * 32 + R2], ident[:, :])
                        st = wk5.alloc()
                        P.copy(st[0:R2, 0:256], psx[0:R2, 0:256])
                        pfree(psx)
                        P.dma(osf2[:, (ch - 1) * 128:(ch + 1) * 128], st[0:R2, 0:256], eng="pool", out_dma=True)
                        wk5.free(st)
                        wk.free(pair["fc"], pair["fn"])
                pfree(ps)

                def post():
                    if half == 0:
                        P.act(ACTT[c][:, 0:N], y[:, 0:N], AF.Silu)
                    else:
                        P.tt(RND(ACTT[c][:, 0:N]), ACTT[c][:, 0:N], y[:, 0:N], ALU.mult)
                    wk.free(y)
                return post
            return up_consume
        po = 2 if prompt else 0
        proj_fm(Wd["ffn_w_up"][l][:, 0:D_FF], D, HTn, N, up_consume_f(0), ps_off=po)
        proj_fm(Wd["ffn_w_up"][l][:, D_FF:2 * D_FF], D, HTn, N, up_consume_f(1), ps_off=po)
        wk.free(*HT)
        P.tag = "ffn_down"
        for hf in range(2):
            accs = [psum() for _ in range(4)]
            for jb in range(0, NCH_FF, 4):
                nj = min(4, NCH_FF - jb)
                wt = wpool.alloc()
                wv = wt[:, 0:nj * 512].re("p (j c) -> p j c", j=nj)
                P.dma(wv, Wd["ffn_w_down"][l][jb * 128:(jb + nj) * 128, hf * 512:(hf + 1) * 512].rearrange("(j p) c -> p j c", p=128))
                for jj in range(nj):
                    j = jb + jj
                    for n in range(4):
                        P.mm(accs[n][:, 0:N], wv[:, jj, n * 128:(n + 1) * 128], ACTT[j][:, 0:N],
                             start=(j == 0), stop=(j == NCH_FF - 1))
                wpool.free(wt)
            for n in range(4):
                res_consume(hf * 4 + n, 128, accs[n])
        if prompt and last:
            for g0 in range(0, 44, 4):
                ps = psum()
                for q in range(4):
                    P.transpose(ps[0:2, q * 128:(q + 1) * 128], FC[l][g0 + q].a, ident[:, :])
                st = wk5.alloc()
                P.copy(st[0:2, :], ps[0:2, :])
                pfree(ps)
                P.dma(o_pf[l, sq_i][:, g0 * 128:(g0 + 4) * 128], st[0:2, :], eng="pool", out_dma=True)
                wk5.free(st)
        wk.free(*ACTT)

    tiles = []
    for sq in range(NP):
        for ti in range(SEQ // T):
            tiles.append(dict(kind="p", N=T, seq=sq, pos0=ti * T, first=(ti == 0), last=(ti == SEQ // T - 1)))
    if NSq > 0:
        tiles.append(dict(kind="s", N=NST, seq=0, pos0=0, first=True, last=True))
    for tl in tiles:
        N = tl["N"]
        P.tag = "io"
        if tl["kind"] == "p":
            for b in range(NB):
                load_fm(xp[tl["seq"], tl["pos0"] + b * 128:tl["pos0"] + (b + 1) * 128, :], 128, XT, b * 128)
        else:
            load_fm(xs.rearrange("s t d -> (s t) d"), NST, XT, 0)
        for l in range(L):
            layer(l, tl)
        P.tag = "io"
        if tl["kind"] == "p":
            for b in range(NB):
                store_fm(XT, 128, b * 128, y_p[tl["seq"], tl["pos0"] + b * 128:tl["pos0"] + (b + 1) * 128, :])
        else:
            store_fm(XT, NST, 0, y_s.rearrange("s t d -> (s t) d"))

    P.emit()
    P.close()
    return nc, hc, P


OUT_NAMES = ["y_prompt", "y_sample", "p_swa_k", "p_swa_v", "p_lru_h", "p_lru_conv", "p_gla_s", "p_mem_k",
             "p_mem_v", "p_ffn_conv", "s_swa_k", "s_swa_v", "s_lru_h", "s_lru_conv", "s_gla_s", "s_ffn_conv"]
OUT_AXIS = [0, 0, 1, 1, 1, 1, 1, 1, 1, 1, 1, 1, 1, 1, 1, 1]
SAMPLE_IN = {"cache_swa_k": 1, "cache_swa_v": 1, "state_lru_h": 1, "state_lru_conv": 1, "state_gla_s": 1,
             "cache_mem_k": 1, "cache_mem_v": 1, "state_ffn_conv": 1}

CFG = dict(T=256, SEQ=2048, NP=2, NSq=16)
NCORES = 8


def make_in_maps(inputs, cfg, ncores, hc):
    NP, NSq = cfg["NP"], cfg["NSq"]
    maps = []
    for i in range(ncores):
        m = {}
        m["x_prompt"] = np.ascontiguousarray(inputs["x_prompt"][i * NP:(i + 1) * NP])
        m["mem_prompt"] = np.ascontiguousarray(inputs["mem_prompt"][i * NP:(i + 1) * NP])
        m["x_sample"] = np.ascontiguousarray(inputs["x_sample"][i * NSq:(i + 1) * NSq])
        for k in SAMPLE_IN:
            m[k] = np.ascontiguousarray(inputs[k][:, i * NSq:(i + 1) * NSq])
        for k in WEIGHT_SHAPES:
            m[k] = np.ascontiguousarray(inputs[k], dtype=np.float32)
        for k, v in hc.items():
            m["c_" + k] = v
        maps.append(m)
    return maps


def kernel(**inputs):
    inputs = {k: np.asarray(v) for k, v in inputs.items()}
    nc, hc, _ = build(CFG)
    in_maps = make_in_maps(inputs, CFG, NCORES, hc)
    res = run_bass_kernel_spmd(nc, in_maps, core_ids=list(range(NCORES)))
    outs = []
    for name, ax in zip(OUT_NAMES, OUT_AXIS):
        outs.append(np.concatenate([np.asarray(r[name]) for r in res.results], axis=ax).astype(np.float32))
    return tuple(outs)
```

```python
import numpy as np
import concourse.bass as bass
import concourse.mybir as mybir
from concourse.bass_utils import run_bass_kernel_spmd

F32 = mybir.dt.float32
F32R = mybir.dt.float32r
FAST_MM = True
ALU = mybir.AluOpType
AF = mybir.ActivationFunctionType

D = 1024
DEPTH = 2
PAST_LEN = 16384
SWA_W = 128
ROPE_DIM = 16
ROPE_THETA = 500000.0
LRU_W = 512
GLA_K = 256
GLA_V = 512
N_MEM = 256
D_FF = 2816
EPS = 1e-6
O_QA, O_KA, O_VA, O_XB, O_YB, O_QC, O_KC, O_VC, O_RC, O_AC, O_GA, O_GB, O_GC = (
    0, 512, 640, 768, 1280, 1792, 2048, 2304, 2816, 3328, 3344, 4368, 5392)
IN_COLS = 6416
NCH_FF = D_FF // 128

ENGS = ("pe", "act", "dve", "pool", "sp")
CENG = {"pe": 0, "act": 1, "dve": 2, "pool": 3, "sp": 4}


class Buf:
    __slots__ = ("t", "name", "lw", "rd", "rnd", "pool")

    def __init__(self, t, name, rnd=False):
        self.t = t
        self.name = name
        self.lw = None
        self.rd = []
        self.rnd = rnd
        self.pool = None

    def __getitem__(self, idx):
        return V(self, self.t[idx])

    @property
    def a(self):
        return V(self, self.t[:])


class V:
    __slots__ = ("b", "ap")

    def __init__(self, b, ap):
        self.b = b
        self.ap = ap

    def __getitem__(self, idx):
        return V(self.b, self.ap[idx])

    def re(self, pat, **kw):
        return V(self.b, self.ap.rearrange(pat, **kw))


def _ap(x):
    if isinstance(x, V):
        if FAST_MM and x.b is not None and x.b.rnd:
            return x.ap.bitcast(F32)
        return x.ap
    return x


def RND(v):
    return v


def _out(x):
    return x.ap if isinstance(x, V) else x


def _isr(x):
    return isinstance(x, V) and x.b is not None and x.b.rnd


def _bufs(xs):
    out = []
    for x in xs:
        if isinstance(x, V):
            if x.b is not None:
                out.append(x.b)
        elif isinstance(x, Buf):
            out.append(x)
    return out


class Op:
    __slots__ = ("eng", "fn", "deps", "is_dma", "idx", "pos", "waits", "dmaslot", "tag")

    def __init__(self, eng, fn, deps, is_dma):
        self.eng = eng
        self.fn = fn
        self.deps = deps
        self.is_dma = is_dma
        self.waits = []
        self.dmaslot = None


class Prog:
    def __init__(self, nc, dma_ring=24):
        self.nc = nc
        self.ops = []
        self.by_eng = {e: [] for e in ENGS}
        self.dma_ring = dma_ring
        self.final_wait = []
        self._stack = []
        self.tag = ""

    def sb(self, name, shape, dtype=F32, rnd=False):
        if rnd and FAST_MM:
            dtype = F32R
        cm = self.nc.sbuf_tensor(name, list(shape), dtype)
        t = cm.__enter__()
        self._stack.append(cm)
        return Buf(t, name, rnd)

    def ps(self, name, shape, dtype=F32):
        cm = self.nc.psum_tensor(name, list(shape), dtype)
        t = cm.__enter__()
        self._stack.append(cm)
        return Buf(t, name)

    def op(self, eng, fn, reads=(), writes=(), is_dma=False, out_dma=False):
        rb = _bufs(reads)
        wb = _bufs(writes)
        deps = set()
        for b in rb:
            if b.lw is not None:
                deps.add(b.lw)
        for b in wb:
            if b.lw is not None:
                deps.add(b.lw)
            deps.update(b.rd)
        o = Op(eng, fn, deps, is_dma)
        o.tag = self.tag
        o.idx = len(self.ops)
        self.ops.append(o)
        o.pos = len(self.by_eng[eng])
        self.by_eng[eng].append(o)
        for b in rb:
            b.rd.append(o.idx)
        for b in wb:
            b.lw = o.idx
            b.rd = []
        if out_dma:
            self.final_wait.append(o.idx)
        return o

    def mm(self, out, lhsT, rhs, start=True, stop=True, fast=None):
        o, l, r = _ap(out), _ap(lhsT), _ap(rhs)
        if fast is None:
            fast = FAST_MM
        if fast and _isr(lhsT) and _isr(rhs) and o.start_partition() == 0 and (r.shape[-1] % 2 == 0):
            l = lhsT.ap
            r = rhs.ap
        return self.op("pe", lambda e: e.matmul(o, l, r, start=start, stop=stop),
                       reads=[lhsT, rhs], writes=[out])

    def transpose(self, out, in_, ident):
        o, i, d = _ap(out), _ap(in_), _ap(ident)
        op = self.op("pe", lambda e: e.transpose(o, i, d), reads=[in_, ident], writes=[out])
        op.tag = op.tag + "|T"
        return op

    def act(self, out, in_, func, bias=None, scale=None):
        o, i = _out(out), _ap(in_)
        kw = {}
        rd = [in_]
        if bias is not None:
            kw["bias"] = _ap(bias)
            rd.append(bias)
        if scale is not None:
            kw["scale"] = _ap(scale)
            rd.append(scale)
        return self.op("act", lambda e: e.activation(o, i, func, **kw), reads=rd, writes=[out])

    def tt(self, out, in0, in1, op, eng="dve"):
        o, a, b = _out(out), _ap(in0), _ap(in1)
        return self.op(eng, lambda e: e.tensor_tensor(o, a, b, op), reads=[in0, in1], writes=[out])

    def ts(self, out, in0, s1, op0, s2=None, op1=None, eng="dve"):
        o, a = _out(out), _ap(in0)
        rd = [in0, s1, s2]
        s1a, s2a = _ap(s1), _ap(s2)
        if op1 is None:
            return self.op(eng, lambda e: e.tensor_scalar(o, a, s1a, None, op0), reads=rd, writes=[out])
        return self.op(eng, lambda e: e.tensor_scalar(o, a, s1a, s2a, op0, op1), reads=rd, writes=[out])

    def stt(self, out, in0, scalar, in1, op0, op1):
        o, a, b = _out(out), _ap(in0), _ap(in1)
        s = _ap(scalar)
        return self.op("dve", lambda e: e.scalar_tensor_tensor(o, a, s, b, op0, op1),
                       reads=[in0, in1, scalar], writes=[out])

    def scan(self, out, d0, d1, initial, op0=ALU.mult, op1=ALU.add):
        o, a, b = _ap(out), _ap(d0), _ap(d1)
        ini = _ap(initial)
        return self.op("dve", lambda e: e.tensor_tensor_scan(o, a, b, ini, op0, op1),
                       reads=[d0, d1, initial], writes=[out])

    def copy(self, out, in_, eng="act"):
        o, i = _out(out), _ap(in_)
        if eng == "act":
            return self.op("act", lambda e: e.copy(o, i), reads=[in_], writes=[out])
        return self.op(eng, lambda e: e.tensor_copy(o, i), reads=[in_], writes=[out])

    def recip(self, out, in_):
        o, i = _ap(out), _ap(in_)
        return self.op("dve", lambda e: e.reciprocal(o, i), reads=[in_], writes=[out])

    def memset(self, out, val, eng="pool"):
        o = _out(out)
        return self.op(eng, lambda e: e.memset(o, val), reads=[], writes=[out])

    def dma(self, out, in_, eng="sp", out_dma=False):
        o, i = _out(out), _out(in_)
        return self.op(eng, lambda e: e.dma_start(out=o, in_=i), reads=[in_], writes=[out],
                       is_dma=True, out_dma=out_dma)

    def emit(self):
        nc = self.nc
        ops = self.ops
        NE = len(ENGS)
        K = self.dma_ring
        needed = set()
        for o in ops:
            needed.update(o.deps)
        needed.update(self.final_wait)
        sems = {}
        for e in ENGS:
            cm = nc.semaphore("s_" + e)
            sems[e] = cm.__enter__()
            self._stack.append(cm)
        rings = {}
        for e in ENGS:
            if any(o.is_dma for o in self.by_eng[e]):
                r = []
                for k in range(K):
                    cm = nc.semaphore("d_%s_%d" % (e, k))
                    r.append(cm.__enter__())
                    self._stack.append(cm)
                rings[e] = r
        cnt = {e: 0 for e in ENGS}
        dcnt = {e: 0 for e in ENGS}
        token = {}
        for e in ENGS:
            for o in self.by_eng[e]:
                if o.is_dma:
                    n = dcnt[e]
                    dcnt[e] += 1
                    o.dmaslot = n
                    token[o.idx] = (rings[e][n % K], 16 * (n // K + 1))
                elif o.idx in needed:
                    cnt[e] += 1
                    token[o.idx] = (sems[e], cnt[e])
        known = {e: [-1] * NE for e in ENGS}
        known_dma = {e: set() for e in ENGS}
        vcs = [None] * len(ops)
        for o in ops:
            e = o.eng
            kn = known[e]
            waits = {}
            for d in sorted(o.deps):
                p = ops[d]
                if p.is_dma:
                    if d in known_dma[e]:
                        continue
                    known_dma[e].add(d)
                    s, v = token[d]
                else:
                    f = CENG[p.eng]
                    if p.eng == "pe" and e == "pe":
                        continue
                    if kn[f] >= p.pos:
                        continue
                    s, v = token[d]
                    if kn[f] < p.pos:
                        kn[f] = p.pos
                pv = vcs[d]
                for i in range(NE):
                    if pv[i] > kn[i]:
                        kn[i] = pv[i]
                k = id(s)
                if k not in waits or waits[k][1] < v:
                    waits[k] = (s, v)
            o.waits = list(waits.values())
            v = list(kn)
            if not o.is_dma:
                v[CENG[e]] = o.pos
            vcs[o.idx] = v
        by_eng = self.by_eng
        fb = {}
        for i in self.final_wait:
            s, v = token[i]
            k = id(s)
            if k not in fb or fb[k][1] < v:
                fb[k] = (s, v)
        final = list(fb.values())
        self.n_waits = sum(len(o.waits) for o in ops)

        def run(eng_name):
            def body(e):
                for o in by_eng[eng_name]:
                    if o.is_dma and o.dmaslot >= K:
                        e.wait_ge(rings[eng_name][o.dmaslot % K], 16 * (o.dmaslot // K))
                    for (s, v) in o.waits:
                        e.wait_ge(s, v)
                    ins = o.fn(e)
                    if o.idx in token:
                        ins.then_inc(token[o.idx][0], 16 if o.is_dma else 1)
                if eng_name == "sp":
                    for (s, v) in final:
                        e.wait_ge(s, v)
            return body

        with nc.Block() as block:
            block.tensor(run("pe"))
            block.scalar(run("act"))
            block.vector(run("dve"))
            block.gpsimd(run("pool"))
            block.sync(run("sp"))

    def close(self):
        while self._stack:
            cm = self._stack.pop()
            cm.__exit__(None, None, None)


class Pool:
    def __init__(self, P, name, n, shape, rnd=False):
        self.free_list = [P.sb("%s%d" % (name, i), shape, rnd=rnd) for i in range(n)]
        for b in self.free_list:
            b.pool = self
        self.name = name
        self.n = n

    def alloc(self):
        assert self.free_list, "pool %s exhausted" % self.name
        return self.free_list.pop(0)

    def free(self, *bs):
        for b in bs:
            b.pool.free_list.append(b)


def host_consts(cfg):
    T, SEQ, NSq = cfg["T"], cfg["SEQ"], cfg["NSq"]
    NST = 4 * NSq
    c = {}
    c["ident"] = np.eye(128, dtype=np.float32)
    c["ones"] = np.ones((128, 128), np.float32)
    bd = np.zeros((128, 128), np.float32)
    bd[:64, :64] = 1
    bd[64:, 64:] = 1
    c["bd64"] = bd
    prot = np.zeros((128, 128), np.float32)
    for m in range(128):
        d = m % 64
        if d < 8:
            prot[m + 8, m] = 1
        elif d < 16:
            prot[m - 8, m] = 1
    c["prot"] = prot
    s = np.arange(128)[:, None]
    t = np.arange(128)[None, :]
    mtri = (s <= t).astype(np.float32)
    c["mtri"] = mtri
    c["umat"] = (s > t).astype(np.float32)
    c["mcur4"] = np.tile(mtri, (1, 4))
    c["mprev4"] = np.tile((s >= t).astype(np.float32), (1, 4))
    tok_seq = np.arange(NST) // 4
    tok_t = np.arange(NST) % 4
    same = tok_seq[:, None] == tok_seq[None, :]
    ms = np.zeros((128, 128), np.float32)
    ms[:NST, :NST] = (same & (tok_t[:, None] <= tok_t[None, :]))
    us = np.zeros((128, 128), np.float32)
    us[:NST, :NST] = (same & (tok_t[:, None] > tok_t[None, :]))
    c["ms"] = ms
    c["us"] = us
    col_seq = np.repeat(np.arange(NSq), 16)
    col_t = np.tile(np.arange(4), NSq * 4)
    msc = np.zeros((128, 256), np.float32)
    msc[:, :NSq * 16] = (np.arange(128)[:, None] >= col_t[None, :])
    c["msc"] = msc
    msn = np.zeros((128, 256), np.float32)
    msn[:NST, :NSq * 16] = ((tok_seq[:, None] == col_seq[None, :]) & (tok_t[:, None] <= col_t[None, :]))
    c["msn"] = msn
    rm = np.zeros((128, 16), np.float32)
    rm[np.arange(NST), tok_seq] = 1
    c["rowmask"] = rm
    half = ROPE_DIM // 2
    inv = (ROPE_THETA ** (-np.arange(half, dtype=np.float32) * np.float32(2.0) / np.float32(ROPE_DIM))).astype(np.float32)

    def tables(pos):
        ang = pos.astype(np.float32)[None, :] * inv[:, None]
        cs, sn = np.cos(ang).astype(np.float32), np.sin(ang).astype(np.float32)
        C = np.ones((128, pos.shape[0]), np.float32)
        S = np.zeros((128, pos.shape[0]), np.float32)
        for hh in range(2):
            C[64 * hh:64 * hh + 8] = cs
            C[64 * hh + 8:64 * hh + 16] = cs
            S[64 * hh:64 * hh + 8] = -sn
            S[64 * hh + 8:64 * hh + 16] = sn
        return C, S
    c["cosp"], c["sinp"] = tables(np.arange(SEQ))
    cs_, ss_ = tables(PAST_LEN + np.arange(4))
    c["coss"] = np.tile(cs_, (1, NSq))
    c["sins"] = np.tile(ss_, (1, NSq))
    return c


RND_C = {"ones", "bd64", "prot", "mtri", "umat", "ms", "us"}
RND_W = {"lru_wa", "lru_wx", "w_in", "gla_wa2", "gla_ba", "w_branch_a", "w_branch_b", "w_branch_c", "w_out", "x_wq", "x_wk", "x_wv",
         "x_wo", "ffn_w_up", "ffn_w_down"}
WEIGHT_SHAPES = {
    "norm_mix_g": (DEPTH, D), "w_in": (DEPTH, D, IN_COLS), "swa_qn_g": (DEPTH, 64), "swa_kn_g": (DEPTH, 64),
    "swa_sink": (DEPTH, 8), "lru_conv_w": (DEPTH, 4, 512), "lru_conv_b": (DEPTH, 512),
    "lru_wa": (DEPTH, 8, 64, 64), "lru_ba": (DEPTH, 512), "lru_wx": (DEPTH, 8, 64, 64), "lru_bx": (DEPTH, 512),
    "lru_lambda": (DEPTH, 512), "gla_wa2": (DEPTH, 16, 256), "gla_ba": (DEPTH, 256), "gla_on_g": (DEPTH, 128),
    "w_branch_a": (DEPTH, 512, D), "w_branch_b": (DEPTH, 512, D), "w_branch_c": (DEPTH, 512, D),
    "w_out": (DEPTH, D, D), "norm_x_g": (DEPTH, D), "norm_mem_g": (DEPTH, D), "x_wq": (DEPTH, D, 512),
    "x_wk": (DEPTH, D, 512), "x_wv": (DEPTH, D, 512), "x_qn_g": (DEPTH, 128), "x_kn_g": (DEPTH, 128),
    "x_wo": (DEPTH, 512, D), "norm_ffn_g": (DEPTH, D), "ffn_w_up": (DEPTH, D, 2 * D_FF),
    "ffn_conv_w": (DEPTH, 3, 2 * D_FF), "ffn_conv_b": (DEPTH, 2 * D_FF), "ffn_w_down": (DEPTH, D_FF, D),
}


def build(cfg):
    T, SEQ, NP, NSq = cfg["T"], cfg["SEQ"], cfg["NP"], cfg["NSq"]
    NST = 4 * NSq
    NB = T // 128
    assert SEQ % T == 0 and T % 128 == 0
    nc = bass.Bass("TRN2", target_bir_lowering=False)
    if FAST_MM:
        nc.dge_precook = False
    P = Prog(nc)
    L = DEPTH

    def din(name, shape):
        return nc.dram_tensor(name, list(shape), F32, kind="ExternalInput").ap()

    def dout(name, shape):
        return nc.dram_tensor(name, list(shape), F32, kind="ExternalOutput").ap()

    xp = din("x_prompt", [NP, SEQ, D])
    xs = din("x_sample", [NSq, 4, D])
    csk = din("cache_swa_k", [L, NSq, 128, 2, 64])
    csv = din("cache_swa_v", [L, NSq, 128, 2, 64])
    slh = din("state_lru_h", [L, NSq, 512])
    slc = din("state_lru_conv", [L, NSq, 3, 512])
    sgs = din("state_gla_s", [L, NSq, 4, 64, 128])
    cmk = din("cache_mem_k", [L, NSq, 256, 4, 128])
    cmv = din("cache_mem_v", [L, NSq, 256, 4, 128])
    sfc = din("state_ffn_conv", [L, NSq, 2, 2 * D_FF])
    memp = din("mem_prompt", [NP, N_MEM, D])
    def dinr(name, shape):
        return nc.dram_tensor(name, list(shape), F32R if FAST_MM else F32, kind="ExternalInput").ap()

    Wd = {k: (dinr(k, s) if k in RND_W else din(k, s)) for k, s in WEIGHT_SHAPES.items()}
    hc = host_consts(cfg)
    Cd = {k: (dinr("c_" + k, v.shape) if k in RND_C else din("c_" + k, v.shape)) for k, v in hc.items()}

    y_p = dout("y_prompt", [NP, SEQ, D])
    y_s = dout("y_sample", [NSq, 4, D])
    o_pk = dout("p_swa_k", [L, NP, 128, 2, 64])
    o_pv = dout("p_swa_v", [L, NP, 128, 2, 64])
    o_ph = dout("p_lru_h", [L, NP, 512])
    o_pc = dout("p_lru_conv", [L, NP, 3, 512])
    o_ps = dout("p_gla_s", [L, NP, 4, 64, 128])
    o_pmk = dout("p_mem_k", [L, NP, 256, 4, 128])
    o_pmv = dout("p_mem_v", [L, NP, 256, 4, 128])
    o_pf = dout("p_ffn_conv", [L, NP, 2, 2 * D_FF])
    o_sk = dout("s_swa_k", [L, NSq, 128, 2, 64])
    o_sv = dout("s_swa_v", [L, NSq, 128, 2, 64])
    o_sh = dout("s_lru_h", [L, NSq, 512])
    o_sc = dout("s_lru_conv", [L, NSq, 3, 512])
    o_ss = dout("s_gla_s", [L, NSq, 4, 64, 128])
    o_sf = dout("s_ffn_conv", [L, NSq, 2, 2 * D_FF])

    wk = Pool(P, "wk", cfg.get("n_wk", 18), [128, T])
    wkr = Pool(P, "wkr", cfg.get("n_wkr", 31), [128, T], rnd=True)
    wk5 = Pool(P, "wk5", cfg.get("n_wk5", 4), [128, 512])
    wk5r = Pool(P, "wk5r", cfg.get("n_wk5r", 7), [128, 512], rnd=True)
    wpool = Pool(P, "wt", cfg.get("n_wt", 5), [128, 2048], rnd=True)
    psb = [P.ps("psb%d" % i, [128, 512]) for i in range(8)]
    ps_free = list(psb)

    def psum():
        assert ps_free, "psum exhausted"
        return ps_free.pop(0)

    def pfree(*bs):
        ps_free.extend(bs)

    Cs = {}
    for k, v in hc.items():
        if k in ("cosp", "sinp"):
            continue
        Cs[k] = P.sb("sc_" + k, list(v.shape), rnd=(k in RND_C))
        P.dma(Cs[k].a, Cd[k])
    ident, ones, bd64, prot = Cs["ident"], Cs["ones"], Cs["bd64"], Cs["prot"]
    epsb = P.sb("epsb", [128, 1])
    P.memset(epsb.a, EPS)
    oneb = P.sb("oneb", [128, 1])
    P.memset(oneb.a, 1.0)

    def load_cols(name, view, R, dup64=False):
        st = wk5.alloc()
        if dup64:
            P.dma(st[0:R, 0:64], view)
            P.dma(st[0:R, 64:128], view)
        else:
            P.dma(st[0:R, 0:128], view)
        ps = psum()
        P.transpose(ps[:, 0:R], st[0:R, 0:128], ident[0:R, 0:R])
        out = P.sb("pc_" + name, [128, R])
        P.copy(out.a, ps[:, 0:R])
        pfree(ps)
        wk5.free(st)
        return out

    g_mix = load_cols("gmix", Wd["norm_mix_g"].rearrange("l (c p) -> (l c) p", p=128), 16)
    g_x = load_cols("gx", Wd["norm_x_g"].rearrange("l (c p) -> (l c) p", p=128), 16)
    g_mem = load_cols("gmem", Wd["norm_mem_g"].rearrange("l (c p) -> (l c) p", p=128), 16)
    g_ffn = load_cols("gffn", Wd["norm_ffn_g"].rearrange("l (c p) -> (l c) p", p=128), 16)
    g_qn = load_cols("gqn", Wd["swa_qn_g"], 2, dup64=True)
    g_kn = load_cols("gkn", Wd["swa_kn_g"], 2, dup64=True)
    cw = load_cols("cw", Wd["lru_conv_w"].rearrange("l j (c p) -> (l j c) p", p=128), 32)
    cb = load_cols("cb", Wd["lru_conv_b"].rearrange("l (c p) -> (l c) p", p=128), 8)
    lba = load_cols("lba", Wd["lru_ba"].rearrange("l (c p) -> (l c) p", p=128), 8)
    lbx = load_cols("lbx", Wd["lru_bx"].rearrange("l (c p) -> (l c) p", p=128), 8)
    lam = load_cols("lam", Wd["lru_lambda"].rearrange("l (c p) -> (l c) p", p=128), 8)
    g_on = load_cols("gon", Wd["gla_on_g"], 2)
    g_xq = load_cols("gxq", Wd["x_qn_g"], 2)
    g_xk = load_cols("gxk", Wd["x_kn_g"], 2)
    fcb = load_cols("fcb", Wd["ffn_conv_b"].rearrange("l (c p) -> (l c) p", p=128), 88)
    fcw = P.sb("pc_fcw", [128, 264])
    fcw_view = Wd["ffn_conv_w"].rearrange("l j (c p) -> (l j c) p", p=128)
    for i in range(3):
        tmp = load_cols("fcw%d" % i, fcw_view[88 * i:88 * (i + 1), :], 88)
        P.copy(fcw[:, 88 * i:88 * (i + 1)], tmp.a, eng="pool")
    lbah = P.sb("lbah", [128, 8])
    lbxh = P.sb("lbxh", [128, 8])
    P.ts(lbah.a, lba.a, 0.5, ALU.mult)
    P.ts(lbxh.a, lbx.a, 0.5, ALU.mult)
    qtr = P.sb("qtr", [128, 1])
    P.memset(qtr.a, 0.25)
    m8sph = P.sb("m8sph", [128, 8])
    m8sp = P.sb("m8sp", [128, 8])
    P.act(m8sp.a, lam.a, AF.Exp, scale=-1.0)
    P.act(m8sp.a, m8sp.a, AF.Ln, bias=oneb[:, 0:1])
    P.ts(m8sp.a, m8sp.a, -8.0, ALU.mult)
    P.ts(m8sph.a, m8sp.a, 0.5, ALU.mult)
    sx = P.sb("sinkx", [64, 16])
    P.dma(sx.a, Wd["swa_sink"].rearrange("l h -> (l h)").partition_broadcast(64))
    P.act(sx.a, sx.a, AF.Exp)
    BDa = [[None] * 4 for _ in range(L)]
    BDx = [[None] * 4 for _ in range(L)]
    for l in range(L):
        for c in range(4):
            for nm, dst, src in (("a", BDa, Wd["lru_wa"]), ("x", BDx, Wd["lru_wx"])):
                b = P.sb("bd%s%d%d" % (nm, l, c), [128, 128], rnd=True)
                P.ts(b.a, ones[:, :], 0.0, ALU.mult)
                P.dma(b[0:64, 0:64], src[l, 2 * c])
                P.dma(b[64:128, 64:128], src[l, 2 * c + 1])
                dst[l][c] = b
    wa2 = [P.sb("wa2_%d" % l, [16, 256], rnd=True) for l in range(L)]
    gba = [P.sb("gba_%d" % l, [1, 256], rnd=True) for l in range(L)]
    for l in range(L):
        P.dma(wa2[l].a, Wd["gla_wa2"][l])
        P.dma(gba[l].a, Wd["gla_ba"][l:l + 1, :])

    XT = [P.sb("XT%d" % c, [128, T]) for c in range(8)]
    KT = [[P.sb("KT%d%d" % (l, kv), [64, 128 + T], rnd=True) for kv in range(2)] for l in range(L)]
    Vb = [P.sb("Vb%d" % l, [128, NB + 1, 128], rnd=True) for l in range(L)]
    XB = [[P.sb("XB%d%d" % (l, c), [128, 3 + T]) for c in range(4)] for l in range(L)]
    Hst = [[P.sb("Hst%d%d" % (l, c), [128, 16]) for c in range(4)] for l in range(L)]
    Sst = [[P.sb("Sst%d%d" % (l, j), [128, 128]) for j in range(2)] for l in range(L)]
    FC = [[P.sb("FC%d_%d" % (l, ch), [128, 2]) for ch in range(44)] for l in range(L)]
    MKT = [P.sb("MKT%d" % l, [128, 4, 256], rnd=True) for l in range(L)]
    MV = [P.sb("MV%d" % l, [128, 2, 512], rnd=True) for l in range(L)]
    QTg = [P.sb("QTg%d" % kv, [64, NB * 512], rnd=True) for kv in range(2)]
    OTg = [P.sb("OTg%d" % kv, [64, NB * 512], rnd=True) for kv in range(2)]
    KCt = P.sb("KCt", [128, NB, 256])
    VCt = P.sb("VCt", [128, NB, 512], rnd=True)
    Gt = P.sb("Gt", [128, NB, 256], rnd=True)

    def load_w(src, K, kp=128):
        nk = K // kp
        ncols = src.shape[1]
        assert nk * ncols <= 2048
        wt = wpool.alloc()
        wv = wt[0:kp, 0:nk * ncols].re("p (k c) -> p k c", k=nk)
        P.dma(wv, src.rearrange("(k p) c -> p k c", p=kp))
        return wt, wv

    def proj_fm(src2d, K, rhs_list, N, consume, GW=None, kp=128, Mch=128, ps_off=0):
        ncols = src2d.shape[1]
        nk = K // kp
        if GW is None:
            GW = max(128, (2048 // nk) // 128 * 128)
        ci = 0
        pend = None
        post = None
        for g0 in range(0, ncols, GW):
            gw = min(GW, ncols - g0)
            wt, wv = load_w(src2d[:, g0:g0 + gw], K, kp)
            for m0 in range(0, gw, Mch):
                mw = min(Mch, gw - m0)
                ps = psum()
                for k in range(nk):
                    P.mm(ps[0:mw, ps_off:ps_off + N], wv[:, k, m0:m0 + mw], rhs_list[k], start=(k == 0), stop=(k == nk - 1))
                if pend is not None:
                    r = consume(*pend)
                    if post is not None:
                        post()
                    post = r
                pend = (ci, mw, ps)
                ci += 1
            wpool.free(wt)
        if pend is not None:
            r = consume(*pend)
            if post is not None:
                post()
            if r is not None:
                r()

    def proj_tm(src2d, K, lhs_list, blocks, consume, GW=256):
        ncols = src2d.shape[1]
        nk = K // 128
        for g0 in range(0, ncols, GW):
            gw = min(GW, ncols - g0)
            wt, wv = load_w(src2d[:, g0:g0 + gw], K)
            for bi, (c0, bw) in enumerate(blocks):
                ps = psum()
                for k in range(nk):
                    P.mm(ps[0:bw, 0:gw], lhs_list[k][:, c0:c0 + bw], wv[:, k, :], start=(k == 0), stop=(k == nk - 1))
                consume(bi, bw, g0, gw, ps)
            wpool.free(wt)

    def rms_apply(srcs, N, lhs_ones, Dn, gcol, outs):
        ps = psum()
        n = len(srcs)
        for c, x in enumerate(srcs):
            sq = wkr.alloc()
            P.act(RND(sq[:, 0:N]), x, AF.Square)
            P.mm(ps[:, 0:N], lhs_ones, sq[:, 0:N], start=(c == 0), stop=(c == n - 1))
            wk.free(sq)
        rstd = wk.alloc()
        P.act(rstd[:, 0:N], ps[:, 0:N], AF.Ln, scale=1.0 / Dn, bias=epsb[:, 0:1])
        pfree(ps)
        P.act(rstd[:, 0:N], rstd[:, 0:N], AF.Exp, scale=-0.5)
        for c, x in enumerate(srcs):
            P.stt(RND(outs[c]), x, gcol(c), rstd[:, 0:N], ALU.mult, ALU.mult)
        wk.free(rstd)

    def load_fm(rows_ap, R, outs, col0):
        st0, st1 = wk5.alloc(), wk5.alloc()
        P.dma(st0[0:R, :], rows_ap[:, 0:512])
        P.dma(st1[0:R, :], rows_ap[:, 512:1024])
        for half, st in enumerate((st0, st1)):
            ps = psum()
            for q in range(4):
                P.transpose(ps[:, q * 128:q * 128 + R], st[0:R, q * 128:(q + 1) * 128], ident[0:R, 0:R])
            for q in range(4):
                P.copy(outs[half * 4 + q][:, col0:col0 + R], ps[:, q * 128:q * 128 + R])
            pfree(ps)
        wk5.free(st0, st1)

    def store_fm(srcs, R, col0, rows_ap, out_dma=True):
        n = len(srcs)
        for g0 in range(0, n, 4):
            gn = min(4, n - g0)
            ps = psum()
            for q in range(gn):
                P.transpose(ps[0:R, q * 128:(q + 1) * 128], srcs[g0 + q][:, col0:col0 + R], ident[:, :])
            st = wk5.alloc()
            P.copy(st[0:R, 0:gn * 128], ps[0:R, 0:gn * 128])
            pfree(ps)
            P.dma(rows_ap[:, g0 * 128:(g0 + gn) * 128], st[0:R, 0:gn * 128], eng="pool", out_dma=out_dma)
            wk5.free(st)

    def layer(l, tl):
        kind, N = tl["kind"], tl["N"]
        prompt = kind == "p"
        first, last, sq_i = tl["first"], tl["last"], tl["seq"]
        nseg = 1 if prompt else NSq
        Lg = N // nseg
        nb = NB if prompt else 1
        blocks = [(b * 128, 128) for b in range(NB)] if prompt else [(0, NST)]
        w_in = Wd["w_in"][l]

        def seg3(v, w):
            return v.re("p (s w) -> p s w", w=w)

        P.tag = "norm_mix"
        HT = [wkr.alloc() for _ in range(8)]
        rms_apply([XT[c][:, 0:N] for c in range(8)], N, ones[:, :], D,
                  lambda c: g_mix[:, l * 8 + c:l * 8 + c + 1], [HT[c][:, 0:N] for c in range(8)])
        HTn = [HT[c][:, 0:N] for c in range(8)]

        P.tag = "swa"
        if prompt:
            cosv = wk.alloc()
            sinv = wk.alloc()
            P.dma(cosv[:, 0:N], Cd["cosp"][:, tl["pos0"]:tl["pos0"] + N])
            P.dma(sinv[:, 0:N], Cd["sinp"][:, tl["pos0"]:tl["pos0"] + N])
            cosa, sina = cosv[:, 0:N], sinv[:, 0:N]
        else:
            cosa, sina = Cs["coss"][:, 0:N], Cs["sins"][:, 0:N]
        if not prompt:
            QTs = [wk5r.alloc() for _ in range(2)]
            OTs = [wk5r.alloc() for _ in range(2)]
            KTn = [wkr.alloc() for _ in range(2)]

        def normrope(ps, gcolv, writer):
            xn = wkr.alloc()
            rms_apply([ps[:, 0:N]], N, bd64[:, :], 64.0, lambda c: gcolv, [xn[:, 0:N]])
            pfree(ps)

            def post():
                ps2 = psum()
                P.mm(ps2[:, 0:N], prot[:, :], xn[:, 0:N])
                t1 = wk.alloc()
                P.tt(t1[:, 0:N], xn[:, 0:N], cosa, ALU.mult)
                t2 = wk.alloc()
                P.tt(t2[:, 0:N], ps2[:, 0:N], sina, ALU.mult)
                pfree(ps2)
                wk.free(xn)
                writer(t1, t2)
                wk.free(t1, t2)
            return post

        def q_consume(c, mw, ps):
            def writer(t1, t2):
                kv = c // 2
                for e in range(2):
                    j = 2 * (c % 2) + e
                    if prompt:
                        dst = QTg[kv][0:64, :].re("p (q j t) -> p q j t", j=4, t=128)[:, :, j, :]
                        a = t1[64 * e:64 * e + 64, 0:N].re("p (q t) -> p q t", t=128)
                        b = t2[64 * e:64 * e + 64, 0:N].re("p (q t) -> p q t", t=128)
                    else:
                        dst = QTs[kv][0:64, 0:NSq * 16].re("p (s j t) -> p s j t", j=4, t=4)[:, :, j, :]
                        a = t1[64 * e:64 * e + 64, 0:N].re("p (s t) -> p s t", t=4)
                        b = t2[64 * e:64 * e + 64, 0:N].re("p (s t) -> p s t", t=4)
                    P.tt(RND(dst), a, b, ALU.add)
            return normrope(ps, g_qn[:, l:l + 1], writer)

        kvout = {}
        if (not prompt) or last:
            kvout["k"] = wk.alloc()
            kvout["v"] = wk.alloc()
        NO = min(N, 128)

        def k_consume(c, mw, ps):
            def writer(t1, t2):
                for kv in range(2):
                    dst = KT[l][kv][0:64, 128:128 + N] if prompt else KTn[kv][0:64, 0:N]
                    P.tt(RND(dst), t1[64 * kv:64 * kv + 64, 0:N], t2[64 * kv:64 * kv + 64, 0:N], ALU.add)
                    if "k" in kvout:
                        P.tt(kvout["k"][0:64, kv * 128:kv * 128 + NO], t1[64 * kv:64 * kv + 64, N - NO:N],
                             t2[64 * kv:64 * kv + 64, N - NO:N], ALU.add, eng="pool")
            return normrope(ps, g_kn[:, l:l + 1], writer)
        proj_fm(w_in[:, O_QA:O_QA + 640], D, HTn, N,
                lambda c, mw, ps: q_consume(c, mw, ps) if c < 4 else k_consume(c - 4, mw, ps))

        if prompt:
            def v_consume(bi, bw, g0, gw, ps):
                P.copy(RND(Vb[l][:, 1 + bi, :]), ps[:, 0:128])
                if last and bi == NB - 1:
                    P.copy(kvout["v"][:, 0:128], ps[:, 0:128])
                pfree(ps)
            proj_tm(w_in[:, O_VA:O_VA + 128], D, HTn, blocks, v_consume)
            P.tag = "swa_attn"
            units = []
            for kv in range(2):
                for qb in range(NB):
                    kbs = []
                    if not (first and qb == 0):
                        kbs.append((qb * 128, qb, Cs["mprev4"]))
                    kbs.append((128 + qb * 128, qb + 1, Cs["mcur4"]))
                    for i, (kc0, vblk, mask) in enumerate(kbs):
                        units.append((kv, qb, kc0, vblk, mask, i == 0, i == len(kbs) - 1))

            def sscores(u):
                kv, qb, kc0 = u[0], u[1], u[2]
                pss = psum()
                P.mm(pss[:, 0:512], KT[l][kv][0:64, kc0:kc0 + 128], QTg[kv][0:64, qb * 512:(qb + 1) * 512])
                return pss
            def expmask(u, pss):
                pT_ = wk5r.alloc()
                P.act(pT_.a, pss.a, AF.Exp, scale=0.125)
                pfree(pss)
                P.tt(RND(pT_.a), pT_.a, u[4].a, ALU.mult)
                return pT_
            pT_n = expmask(units[0], sscores(units[0]))
            pss_n = sscores(units[1]) if len(units) > 1 else None
            pso = psd = None
            for ui, (kv, qb, kc0, vblk, mask, is_first, is_last) in enumerate(units):
                pT = pT_n
                if is_first:
                    pso, psd = psum(), psum()
                P.mm(pso[0:64, 0:512], Vb[l][:, vblk, 64 * kv:64 * kv + 64], pT.a, start=is_first, stop=is_last)
                P.mm(psd[0:64, 0:512], ones[:, 0:64], pT.a, start=is_first, stop=is_last)
                wk5.free(pT)
                if ui + 1 < len(units):
                    pT_n = expmask(units[ui + 1], pss_n)
                    pss_n = sscores(units[ui + 2]) if ui + 2 < len(units) else None
                if is_last:
                    den = wk5.alloc()
                    for j in range(4):
                        P.act(den[0:64, j * 128:(j + 1) * 128], psd[0:64, j * 128:(j + 1) * 128], AF.Ln,
                              bias=sx[:, l * 8 + kv * 4 + j:l * 8 + kv * 4 + j + 1])
                    P.act(den[0:64, :], den[0:64, :], AF.Exp, scale=-1.0)
                    P.tt(RND(OTg[kv][0:64, qb * 512:(qb + 1) * 512]), pso[0:64, :], den[0:64, :], ALU.mult)
                    wk5.free(den)
                    pfree(pso, psd)
            if last:
                st = wk5.alloc()
                ps = psum()
                for kv in range(2):
                    P.transpose(ps[:, 64 * kv:64 * kv + 64], kvout["k"][0:64, kv * 128:(kv + 1) * 128], ident[0:64, 0:64])
                P.copy(st[:, 0:128], ps[:, 0:128])
                pfree(ps)
                P.dma(o_pk[l, sq_i].rearrange("t k d -> t (k d)"), st[:, 0:128], eng="pool", out_dma=True)
                wk5.free(st)
                P.dma(o_pv[l, sq_i].rearrange("t k d -> t (k d)"), kvout["v"][:, 0:128], eng="pool", out_dma=True)
                wk.free(kvout["k"], kvout["v"])
            else:
                for kv in range(2):
                    P.copy(RND(KT[l][kv][0:64, 0:128]), KT[l][kv][0:64, N:N + 128], eng="pool")
                P.copy(RND(Vb[l][:, 0, :]), Vb[l][:, NB, :], eng="pool")
            OTv = lambda h: OTg[h // 4][0:64, :].re("p (q j t) -> p q j t", j=4, t=128)[:, :, h % 4, :]
            wk.free(cosv, sinv)
        else:
            Vn = wkr.alloc()

            def v_consume(bi, bw, g0, gw, ps):
                P.copy(RND(Vn[0:NST, 0:128]), ps[0:NST, 0:128])
                P.copy(kvout["v"][0:NST, 0:128], ps[0:NST, 0:128])
                pfree(ps)
            proj_tm(w_in[:, O_VA:O_VA + 128], D, HTn, blocks, v_consume)
            NC16 = NSq * 16
            pssc = [psum(), psum()]
            for s0 in range(0, NSq, 4):
                Kc4 = wk5.alloc()
                P.dma(Kc4[:, :].re("t (s f) -> t s f", f=128), csk[l, s0:s0 + 4].rearrange("s t k d -> t s (k d)"))
                ps = psum()
                for q in range(4):
                    P.transpose(ps[:, q * 128:(q + 1) * 128], Kc4[:, q * 128:(q + 1) * 128], ident[:, :])
                wk5.free(Kc4)
                for kv in range(2):
                    KTc4 = wk5r.alloc()
                    P.copy(RND(KTc4[0:64, :]), ps[64 * kv:64 * kv + 64, :])
                    for q in range(4):
                        s_ = s0 + q
                        P.mm(pssc[kv][:, 16 * s_:16 * s_ + 16], KTc4[0:64, q * 128:(q + 1) * 128],
                             QTs[kv][0:64, 16 * s_:16 * s_ + 16])
                    wk5.free(KTc4)
                pfree(ps)
            for kv in range(2):
                pTc = wk5r.alloc()
                P.act(pTc[:, 0:NC16], pssc[kv][:, 0:NC16], AF.Exp, scale=0.125)
                pfree(pssc[kv])
                P.tt(RND(pTc[:, 0:NC16]), pTc[:, 0:NC16], Cs["msc"][:, 0:NC16], ALU.mult)
                pssn = psum()
                P.mm(pssn[0:NST, 0:NC16], KTn[kv][0:64, 0:NST], QTs[kv][0:64, 0:NC16])
                pTn = wk5r.alloc()
                P.act(pTn[0:NST, 0:NC16], pssn[0:NST, 0:NC16], AF.Exp, scale=0.125)
                pfree(pssn)
                P.tt(RND(pTn[0:NST, 0:NC16]), pTn[0:NST, 0:NC16], Cs["msn"][0:NST, 0:NC16], ALU.mult)
                pso, psd = psum(), psum()
                P.mm(pso[0:64, 0:NC16], Vn[0:NST, 64 * kv:64 * kv + 64], pTn[0:NST, 0:NC16], start=True, stop=False)
                for s0 in range(0, NSq, 4):
                    Vc4 = wk5.alloc()
                    P.dma(Vc4[:, :].re("t (s f) -> t s f", f=128), csv[l, s0:s0 + 4].rearrange("s t k d -> t s (k d)"))
                    for q in range(4):
                        s_ = s0 + q
                        P.mm(pso[0:64, 16 * s_:16 * s_ + 16], Vc4[:, q * 128 + 64 * kv:q * 128 + 64 * kv + 64],
                             pTc[:, 16 * s_:16 * s_ + 16], start=False, stop=(s_ == NSq - 1))
                    wk5.free(Vc4)
                P.mm(psd[0:64, 0:NC16], ones[0:NST, 0:64], pTn[0:NST, 0:NC16], start=True, stop=False)
                P.mm(psd[0:64, 0:NC16], ones[:, 0:64], pTc[:, 0:NC16], start=False, stop=True)
                den = wk5.alloc()
                for j in range(4):
                    P.ts(den[0:64, 0:NC16].re("p (s j t) -> p s j t", j=4, t=4)[:, :, j, :],
                         psd[0:64, 0:NC16].re("p (s j t) -> p s j t", j=4, t=4)[:, :, j, :],
                         sx[:, l * 8 + kv * 4 + j:l * 8 + kv * 4 + j + 1], ALU.add)
                P.recip(den[0:64, 0:NC16], den[0:64, 0:NC16])
                P.tt(RND(OTs[kv][0:64, 0:NC16]), pso[0:64, 0:NC16], den[0:64, 0:NC16], ALU.mult)
                wk5.free(den, pTc, pTn)
                pfree(pso, psd)
            P.dma(o_sk[l, :, 0:124], csk[l, :, 4:128], eng="pool", out_dma=True)
            P.dma(o_sv[l, :, 0:124], csv[l, :, 4:128], eng="pool", out_dma=True)
            st = wk5.alloc()
            ps = psum()
            for kv in range(2):
                P.transpose(ps[0:NST, 64 * kv:64 * kv + 64], kvout["k"][0:64, kv * 128:kv * 128 + NST], ident[0:64, 0:64])
            P.copy(st[0:NST, 0:128], ps[0:NST, 0:128])
            pfree(ps)
            for s in range(NSq):
                P.dma(o_sk[l, s, 124:128].rearrange("t k d -> t (k d)"), st[4 * s:4 * s + 4, 0:128], eng="pool", out_dma=True)
                P.dma(o_sv[l, s, 124:128].rearrange("t k d -> t (k d)"), kvout["v"][4 * s:4 * s + 4, 0:128], eng="pool", out_dma=True)
            wk5.free(st)
            wk.free(Vn, *KTn)
            wk.free(kvout["k"], kvout["v"])
            wk5.free(*QTs)
            OTv = lambda h: OTs[h // 4][0:64, 0:NC16].re("p (s j t) -> p s j t", j=4, t=4)[:, :, h % 4, :]

        P.tag = "lru"
        OB = [wkr.alloc() for _ in range(4)]
        Hh = [wk.alloc() for _ in range(4)]
        if prompt:
            xb3 = lambda c: XB[l][c][:, 0:3 + N].re("p (s w) -> p s w", s=1)
        else:
            XBs = [wk.alloc() for _ in range(4)]
            xb3 = lambda c: XBs[c][:, 0:NSq * 7].re("p (s w) -> p s w", w=7)
            st = wk5.alloc()
            R = NSq * 3
            P.dma(st[0:R, :], slc[l].rearrange("s j f -> (s j) f"))
            ps = psum()
            for c in range(4):
                P.transpose(ps[:, c * 128:c * 128 + R], st[0:R, c * 128:(c + 1) * 128], ident[0:R, 0:R])
            for c in range(4):
                P.copy(xb3(c)[:, :, 0:3], ps[:, c * 128:c * 128 + R].re("p (s j) -> p s j", j=3))
            pfree(ps)
            P.dma(st[0:NSq, :], slh[l])
            ps = psum()
            for c in range(4):
                P.transpose(ps[:, c * 16:c * 16 + NSq], st[0:NSq, c * 128:(c + 1) * 128], ident[0:NSq, 0:NSq])
            for c in range(4):
                P.copy(Hst[l][c][:, 0:NSq], ps[:, c * 16:c * 16 + NSq])
            pfree(ps)
            wk5.free(st)
        if prompt and first:
            for c in range(4):
                P.memset(XB[l][c][:, 0:3], 0.0)
                P.memset(Hst[l][c][:, 0:1], 0.0)

        def xb_consume(c, mw, ps):
            X3 = xb3(c)
            P.copy(X3[:, :, 3:3 + Lg], seg3(ps[:, 0:N], Lg))
            pfree(ps)
            xc = wkr.alloc()
            xc3 = seg3(xc[:, 0:N], Lg)
            cwc = lambda j: cw[:, l * 16 + j * 4 + c:l * 16 + j * 4 + c + 1]
            P.ts(xc3, X3[:, :, 0:Lg], cwc(0), ALU.mult, cb[:, l * 4 + c:l * 4 + c + 1], ALU.add)
            for j in range(1, 4):
                P.stt(xc3, X3[:, :, j:j + Lg], cwc(j), xc3, ALU.mult, ALU.add)
            return lambda: (xb_postA(c, xc) if prompt else xb_post(c, xc))

        lruA = {}

        def xb_postA(c, xc):
            tr = wk.alloc()
            ti = wk.alloc()
            col = slice(l * 4 + c, l * 4 + c + 1)
            ps1 = psum()
            P.mm(ps1[:, 0:N], BDa[l][c].a, xc[:, 0:N])
            P.act(tr[:, 0:N], ps1[:, 0:N], AF.Tanh, bias=lbah[:, col], scale=0.5)
            pfree(ps1)
            ps2 = psum()
            P.mm(ps2[:, 0:N], BDx[l][c].a, xc[:, 0:N])
            P.act(ti[:, 0:N], ps2[:, 0:N], AF.Tanh, bias=lbxh[:, col], scale=0.5)
            pfree(ps2)
            a = wk.alloc()
            P.act(a[:, 0:N], tr[:, 0:N], AF.Exp, scale=m8sph[:, col], bias=m8sph[:, col])
            P.stt(ti[:, 0:N], ti[:, 0:N], 1.0, xc[:, 0:N], ALU.add, ALU.mult)
            wk.free(xc, tr)
            lruA[c] = (a, ti)

        def xb_postB(c):
            a, gi = lruA.pop(c)
            sq = wk.alloc()
            P.act(sq[:, 0:N], a[:, 0:N], AF.Square)
            P.act(sq[:, 0:N], sq[:, 0:N], AF.Sqrt, scale=-0.25, bias=qtr[:, 0:1])
            P.tt(gi[:, 0:N], gi[:, 0:N], sq[:, 0:N], ALU.mult)
            a3, b3 = seg3(a[:, 0:N], Lg), seg3(gi[:, 0:N], Lg)
            tmp = wk.alloc()
            t3 = tmp[:, 0:nseg].re("p (s o) -> p s o", o=1)
            h0 = Hst[l][c][:, 0:nseg].re("p (s o) -> p s o", o=1)
            P.tt(t3, a3[:, :, 0:1], h0, ALU.mult)
            P.tt(b3[:, :, 0:1], b3[:, :, 0:1], t3, ALU.add)
            P.memset(a3[:, :, 0:1], 0.0, eng="dve")
            P.scan(Hh[c][:, 0:N], a[:, 0:N], gi[:, 0:N], 0.0)
            P.copy(h0, seg3(Hh[c][:, 0:N], Lg)[:, :, Lg - 1:Lg], eng="pool")
            wk.free(sq, gi, a, tmp)

        def xb_post(c, xc):
            r = wk.alloc()
            ig = wk.alloc()
            ps1 = psum()
            P.mm(ps1[:, 0:N], BDa[l][c].a, xc[:, 0:N])
            P.act(r[:, 0:N], ps1[:, 0:N], AF.Sigmoid, bias=lba[:, l * 4 + c:l * 4 + c + 1])
            pfree(ps1)
            ps2 = psum()
            P.mm(ps2[:, 0:N], BDx[l][c].a, xc[:, 0:N])
            P.act(ig[:, 0:N], ps2[:, 0:N], AF.Sigmoid, bias=lbx[:, l * 4 + c:l * 4 + c + 1])
            pfree(ps2)
            a = wk.alloc()
            P.act(a[:, 0:N], r[:, 0:N], AF.Exp, scale=m8sp[:, l * 4 + c:l * 4 + c + 1])
            sq = r
            P.act(sq[:, 0:N], a[:, 0:N], AF.Square)
            P.act(sq[:, 0:N], sq[:, 0:N], AF.Sqrt, scale=-1.0, bias=oneb[:, 0:1])
            P.tt(ig[:, 0:N], ig[:, 0:N], xc[:, 0:N], ALU.mult)
            P.tt(ig[:, 0:N], ig[:, 0:N], sq[:, 0:N], ALU.mult)
            a3, b3 = seg3(a[:, 0:N], Lg), seg3(ig[:, 0:N], Lg)
            tmp = wk.alloc()
            t3 = tmp[:, 0:nseg].re("p (s o) -> p s o", o=1)
            h0 = Hst[l][c][:, 0:nseg].re("p (s o) -> p s o", o=1)
            P.tt(t3, a3[:, :, 0:1], h0, ALU.mult)
            P.tt(b3[:, :, 0:1], b3[:, :, 0:1], t3, ALU.add)
            P.memset(a3[:, :, 0:1], 0.0, eng="dve")
            P.scan(Hh[c][:, 0:N], a[:, 0:N], ig[:, 0:N], 0.0)
            P.copy(h0, seg3(Hh[c][:, 0:N], Lg)[:, :, Lg - 1:Lg], eng="pool")
            wk.free(xc, r, ig, a, tmp)

        def yb_consume(c, mw, ps):
            g = wk.alloc()
            P.act(g[:, 0:N], ps[:, 0:N], AF.Gelu_apprx_tanh)
            pfree(ps)
            P.tt(RND(OB[c][:, 0:N]), g[:, 0:N], Hh[c][:, 0:N], ALU.mult)
            wk.free(g)
        if prompt:
            proj_fm(w_in[:, O_XB:O_XB + 512], D, HTn, N, xb_consume)
            for c in range(4):
                xb_postB(c)
            proj_fm(w_in[:, O_YB:O_YB + 512], D, HTn, N, yb_consume)
        else:
            proj_fm(w_in[:, O_XB:O_XB + 1024], D, HTn, N,
                    lambda c, mw, ps: xb_consume(c, mw, ps) if c < 4 else yb_consume(c - 4, mw, ps))
        wk.free(*Hh)
        if prompt:
            if last:
                store_fm([Hst[l][c] for c in range(4)], 1, 0, o_ph[l, sq_i:sq_i + 1, :])
                store_fm([XB[l][c] for c in range(4)], 3, N, o_pc[l, sq_i])
            else:
                for c in range(4):
                    P.copy(XB[l][c][:, 0:3], XB[l][c][:, N:N + 3], eng="pool")
        else:
            store_fm([Hst[l][c] for c in range(4)], NSq, 0, o_sh[l])
            cst = [wk.alloc() for _ in range(4)]
            for c in range(4):
                P.copy(cst[c][:, 0:NSq * 3].re("p (s j) -> p s j", j=3), xb3(c)[:, :, 4:7], eng="pool")
            store_fm(cst, NSq * 3, 0, o_sc[l].rearrange("s j f -> (s j) f"))
            wk.free(*cst)
            wk.free(*XBs)

        P.tag = "gla"
        qcT = [wk.alloc() for _ in range(2)]
        kcT = [wk.alloc() for _ in range(2)]

        def qc_consume(c, mw, ps):
            P.act(qcT[c][:, 0:N], ps[:, 0:N], AF.Copy, scale=0.125)
            pfree(ps)

        def kc_consume(c, mw, ps):
            P.copy(kcT[c][:, 0:N], ps[:, 0:N])
            pfree(ps)
        proj_fm(w_in[:, O_QC:O_QC + 512], D, HTn, N,
                lambda c, mw, ps: qc_consume(c, mw, ps) if c < 2 else kc_consume(c - 2, mw, ps))

        def kct_consume(bi, bw, g0, gw, ps):
            P.copy(KCt[0:bw, bi, g0:g0 + gw], ps[0:bw, 0:gw])
            pfree(ps)
        proj_tm(w_in[:, O_KC:O_KC + 256], D, HTn, blocks, kct_consume)

        def vct_consume(bi, bw, g0, gw, ps):
            P.copy(RND(VCt[0:bw, bi, g0:g0 + gw]), ps[0:bw, 0:gw])
            pfree(ps)
        proj_tm(w_in[:, O_VC:O_VC + 512], D, HTn, blocks, vct_consume)
        act_ = wkr.alloc()

        def ac_consume(c, mw, ps):
            P.copy(RND(act_[0:16, 0:N]), ps[0:16, 0:N])
            pfree(ps)
        SR = [wk.alloc() for _ in range(4)]

        def rc_consume(c, mw, ps):
            P.act(SR[c][:, 0:N], ps[:, 0:N], AF.Silu)
            pfree(ps)
        proj_fm(w_in[:, O_AC:O_AC + 16], D, HTn, N, ac_consume)
        for bi, (c0, bw) in enumerate(blocks):
            ps = psum()
            P.mm(ps[0:bw, 0:256], act_[0:16, c0:c0 + bw], wa2[l].a, start=True, stop=False)
            P.mm(ps[0:bw, 0:256], ones[0:1, 0:bw], gba[l].a, start=False, stop=True)
            e = wk5.alloc()
            P.act(e[0:bw, 0:256], ps[0:bw, 0:256], AF.Exp, scale=-1.0)
            pfree(ps)
            P.act(e[0:bw, 0:256], e[0:bw, 0:256], AF.Ln, bias=oneb[0:bw, 0:1])
            P.ts(RND(Gt[0:bw, bi, :]), e[0:bw, 0:256], -1.0 / 16.0, ALU.mult)
            wk5.free(e)
        wk.free(act_)
        proj_fm(w_in[:, O_RC:O_RC + 512], D, HTn, N, rc_consume)
        oT = [wk.alloc() for _ in range(4)]
        qt = [wkr.alloc() for _ in range(2)]
        kt = [wkr.alloc() for _ in range(2)]
        Mm = Cs["mtri"] if prompt else Cs["ms"]
        Um = Cs["umat"] if prompt else Cs["us"]
        if prompt and first:
            for j in range(2):
                P.memset(Sst[l][j].a, 0.0)
        P.tag = "gla_blk"
        for bi, (c0, bw) in enumerate(blocks):
            psU = psum()
            P.mm(psU[0:bw, 0:256], Um[0:bw, 0:bw], Gt[0:bw, bi, :])
            kp = wk5r.alloc()
            P.act(kp[0:bw, 0:256], psU[0:bw, 0:256], AF.Exp)
            pfree(psU)
            P.tt(RND(kp[0:bw, 0:256]), kp[0:bw, 0:256], KCt[0:bw, bi, :], ALU.mult)
            eP = [wk.alloc() for _ in range(2)]
            for j in range(2):
                psC = psum()
                P.mm(psC[:, 0:bw], Gt[0:bw, bi, 128 * j:128 * (j + 1)], Mm[0:bw, 0:bw])
                eN = wk.alloc()
                P.act(eP[j][:, 0:bw], psC[:, 0:bw], AF.Exp)
                P.act(eN[:, 0:bw], psC[:, 0:bw], AF.Exp, scale=-1.0)
                pfree(psC)
                P.tt(RND(qt[j][:, c0:c0 + bw]), qcT[j][:, c0:c0 + bw], eP[j][:, 0:bw], ALU.mult)
                P.tt(RND(kt[j][:, c0:c0 + bw]), kcT[j][:, c0:c0 + bw], eN[:, 0:bw], ALU.mult)
                wk.free(eN)
            if prompt:
                psO = psum()
                psOh = [psO[:, h * 128:h * 128 + bw] for h in range(4)]
            else:
                psOb = [psum() for _ in range(4)]
                psOh = [psOb[h][:, 0:bw] for h in range(4)]
            for h in range(4):
                j, r0 = h // 2, 64 * (h % 2)
                psA = psum()
                P.mm(psA[0:bw, 0:bw], kt[j][r0:r0 + 64, c0:c0 + bw], qt[j][r0:r0 + 64, c0:c0 + bw])
                att = wkr.alloc()
                P.tt(RND(att[0:bw, 0:bw]), psA[0:bw, 0:bw], Mm[0:bw, 0:bw], ALU.mult)
                pfree(psA)
                P.mm(psOh[h], VCt[0:bw, bi, 128 * h:128 * (h + 1)], att[0:bw, 0:bw], start=True, stop=False)
                if prompt:
                    P.mm(psOh[h], Sst[l][j][r0:r0 + 64, :], qt[j][r0:r0 + 64, c0:c0 + bw], start=False, stop=True, fast=False)
                    P.copy(oT[h][:, c0:c0 + bw], psOh[h])
                wk.free(att)
            if prompt:
                pfree(psO)
                psS = psum()
                for h in range(4):
                    j, r0 = h // 2, 64 * (h % 2)
                    P.mm(psS[r0:r0 + 64, j * 128:(j + 1) * 128], kp[0:bw, 64 * h:64 * h + 64],
                         VCt[0:bw, bi, 128 * h:128 * (h + 1)], fast=False)
                for j in range(2):
                    P.stt(Sst[l][j].a, Sst[l][j].a, eP[j][:, bw - 1:bw], psS[:, j * 128:(j + 1) * 128], ALU.mult, ALU.add)
                pfree(psS)
            else:
                for s in range(NSq):
                    Ss = wk5.alloc()
                    Ss3 = Ss[:, 0:256].re("p (j v) -> p j v", j=2)
                    P.dma(Ss3, sgs[l, s].rearrange("(j h) d v -> (h d) j v", h=2))
                    for h in range(4):
                        j, r0 = h // 2, 64 * (h % 2)
                        P.mm(psOb[h][:, 4 * s:4 * s + 4], Ss3[r0:r0 + 64, j, :], qt[j][r0:r0 + 64, 4 * s:4 * s + 4],
                             start=False, stop=(s == NSq - 1), fast=False)
                    kpm = wk5r.alloc()
                    P.ts(RND(kpm[0:bw, 0:256]), kp[0:bw, 0:256], Cs["rowmask"][0:bw, s:s + 1], ALU.mult)
                    psS = psum()
                    for h in range(4):
                        j, r0 = h // 2, 64 * (h % 2)
                        P.mm(psS[r0:r0 + 64, j * 128:(j + 1) * 128], kpm[0:bw, 64 * h:64 * h + 64],
                             VCt[0:bw, 0, 128 * h:128 * (h + 1)], fast=False)
                    so = wk5.alloc()
                    for j in range(2):
                        P.stt(so[:, j * 128:(j + 1) * 128], Ss3[:, j, :], eP[j][:, 4 * s + 3:4 * s + 4],
                              psS[:, j * 128:(j + 1) * 128], ALU.mult, ALU.add)
                    pfree(psS)
                    P.dma(o_ss[l, s].rearrange("(j h) d v -> (h d) j v", h=2),
                          so[:, 0:256].re("p (j v) -> p j v", j=2), eng="pool", out_dma=True)
                    wk5.free(kpm, so, Ss)
                for h in range(4):
                    P.copy(oT[h][:, c0:c0 + bw], psOh[h])
                pfree(*psOb)
            wk.free(*eP)
            wk5.free(kp)
        if prompt and last:
            for j in range(2):
                P.dma(o_ps[l, sq_i, 2 * j:2 * j + 2].rearrange("h d v -> (h d) v"), Sst[l][j].a, eng="pool", out_dma=True)
        wk.free(*qcT, *kcT, *qt, *kt)
        OC = [wkr.alloc() for _ in range(4)]
        sqs = [wkr.alloc() for _ in range(4)]
        for h in range(4):
            P.act(sqs[h][:, 0:N], oT[h][:, 0:N], AF.Square)
        pns = [psum() for _ in range(4)]
        for h in range(4):
            P.mm(pns[h][:, 0:N], ones[:, :], sqs[h][:, 0:N])
        wk.free(*sqs)
        rs = [wk.alloc() for _ in range(4)]
        for h in range(4):
            P.act(rs[h][:, 0:N], pns[h][:, 0:N], AF.Ln, scale=1.0 / 128.0, bias=epsb[:, 0:1])
        pfree(*pns)
        for h in range(4):
            P.act(rs[h][:, 0:N], rs[h][:, 0:N], AF.Exp, scale=-0.5)
        for h in range(4):
            P.stt(rs[h][:, 0:N], oT[h][:, 0:N], g_on[:, l:l + 1], rs[h][:, 0:N], ALU.mult, ALU.mult)
        for h in range(4):
            P.tt(RND(OC[h][:, 0:N]), rs[h][:, 0:N], SR[h][:, 0:N], ALU.mult)
        wk.free(*rs)
        wk.free(*oT, *SR)

        P.tag = "merge"
        MG = [wkr.alloc() for _ in range(8)]
        brs = (
            (O_GA, Wd["w_branch_a"][l], 64, [OTv(h) for h in range(8)]),
            (O_GB, Wd["w_branch_b"][l], 128, [OB[c][:, 0:N] for c in range(4)]),
            (O_GC, Wd["w_branch_c"][l], 128, [OC[c][:, 0:N] for c in range(4)]),
        )
        for bi_, (gofs, wbr, kp_, rl) in enumerate(brs):
            sig = [None] * 8

            def g_consume(c, mw, ps):
                s = wk.alloc()
                P.act(s[:, 0:N], ps[:, 0:N], AF.Sigmoid)
                pfree(ps)
                sig[c] = s
            proj_fm(w_in[:, gofs:gofs + 1024], D, HTn, N, g_consume)

            def b_consume(c, mw, ps):
                if bi_ == 0:
                    P.tt(RND(MG[c][:, 0:N]), ps[:, 0:N], sig[c][:, 0:N], ALU.mult)
                else:
                    P.tt(sig[c][:, 0:N], ps[:, 0:N], sig[c][:, 0:N], ALU.mult)
                    P.tt(RND(MG[c][:, 0:N]), MG[c][:, 0:N], sig[c][:, 0:N], ALU.add)
                pfree(ps)
                wk.free(sig[c])
            proj_fm(wbr, 512, rl, N, b_consume, kp=kp_)
        wk.free(*OB, *OC, *HT)
        if not prompt:
            wk5.free(*OTs)

        def res_consume(c, mw, ps):
            P.tt(XT[c][:, 0:N], XT[c][:, 0:N], ps[:, 0:N], ALU.add)
            pfree(ps)
        P.tag = "wout"
        proj_fm(Wd["w_out"][l], D, [MG[c][:, 0:N] for c in range(8)], N, res_consume)
        wk.free(*MG)

        P.tag = "xattn"
        if prompt and first:
            MT = [wk.alloc() for _ in range(8)]
            for mb in range(2):
                load_fm(memp[sq_i, mb * 128:(mb + 1) * 128, :], 128, MT, mb * 128)
            MN = [wkr.alloc() for _ in range(8)]
            rms_apply([MT[c][:, 0:256] for c in range(8)], 256, ones[:, :], D,
                      lambda c: g_mem[:, l * 8 + c:l * 8 + c + 1], [MN[c][:, 0:256] for c in range(8)])
            wk.free(*MT)
            MNn = [MN[c][:, 0:256] for c in range(8)]

            def mk_consume(c, mw, ps):
                rms_apply([ps[:, 0:256]], 256, ones[:, :], 128.0, lambda cc: g_xk[:, l:l + 1], [MKT[l][:, c, :]])
                pfree(ps)
            proj_fm(Wd["x_wk"][l], D, MNn, 256, mk_consume)

            def mv_consume(bi, bw, g0, gw, ps):
                P.copy(RND(MV[l][:, bi, g0:g0 + gw]), ps[:, 0:gw])
                pfree(ps)
            proj_tm(Wd["x_wv"][l], D, MNn, [(0, 128), (128, 128)], mv_consume)
            wk.free(*MN)
            for mb in range(2):
                st = wk5.alloc()
                P.copy(st.a, MV[l][:, mb, :], eng="pool")
                P.dma(o_pmv[l, sq_i, mb * 128:(mb + 1) * 128].rearrange("t h d -> t (h d)"), st.a, eng="pool", out_dma=True)
                wk5.free(st)
                ps = psum()
                for h in range(4):
                    P.transpose(ps[:, h * 128:(h + 1) * 128], MKT[l][:, h, mb * 128:(mb + 1) * 128], ident[:, :])
                st = wk5.alloc()
                P.copy(st.a, ps.a)
                pfree(ps)
                P.dma(o_pmk[l, sq_i, mb * 128:(mb + 1) * 128].rearrange("t h d -> t (h d)"), st.a, eng="pool", out_dma=True)
                wk5.free(st)
        HT = [wkr.alloc() for _ in range(8)]
        rms_apply([XT[c][:, 0:N] for c in range(8)], N, ones[:, :], D,
                  lambda c: g_x[:, l * 8 + c:l * 8 + c + 1], [HT[c][:, 0:N] for c in range(8)])
        HTn = [HT[c][:, 0:N] for c in range(8)]
        QX = [wkr.alloc() for _ in range(4)]

        def qx_consume(c, mw, ps):
            rms_apply([ps[:, 0:N]], N, ones[:, :], 128.0, lambda cc: g_xq[:, l:l + 1], [QX[c][:, 0:N]])
            pfree(ps)
        proj_fm(Wd["x_wq"][l], D, HTn, N, qx_consume)
        wk.free(*HT)
        OX = [wkr.alloc() for _ in range(4)]
        xsc = 128.0 ** -0.5
        if prompt:
            assert N <= 256

            def xscores(h):
                pss = psum()
                for mb in range(2):
                    P.mm(pss[:, mb * 256:mb * 256 + N], MKT[l][:, h, mb * 128:(mb + 1) * 128], QX[h][:, 0:N])
                return pss
            def xexp(pss):
                pT_ = wk5r.alloc()
                P.act(pT_[:, :].re("p (m n) -> p m n", m=2)[:, :, 0:N], pss[:, :].re("p (m n) -> p m n", m=2)[:, :, 0:N],
                      AF.Exp, scale=xsc)
                pfree(pss)
                return pT_
            pT_n = xexp(xscores(0))
            pss_n = xscores(1)
            for h in range(4):
                pT = pT_n
                pso, psd = psum(), psum()
                for mb in range(2):
                    P.mm(pso[:, 0:N], MV[l][:, mb, 128 * h:128 * (h + 1)], pT[:, mb * 256:mb * 256 + N], start=(mb == 0), stop=(mb == 1))
                    P.mm(psd[:, 0:N], ones[:, :], pT[:, mb * 256:mb * 256 + N], start=(mb == 0), stop=(mb == 1))
                wk5.free(pT)
                if h < 3:
                    pT_n = xexp(pss_n)
                    pss_n = xscores(h + 2) if h + 2 < 4 else None
                rd = wk.alloc()
                P.act(rd[:, 0:N], psd[:, 0:N], AF.Ln)
                P.act(rd[:, 0:N], rd[:, 0:N], AF.Exp, scale=-1.0)
                P.tt(RND(OX[h][:, 0:N]), pso[:, 0:N], rd[:, 0:N], ALU.mult)
                wk.free(rd)
                pfree(pso, psd)
        else:
            oxP, dP = psum(), psum()
            for s in range(NSq):
                for mb in range(2):
                    Kc2, Vc2, MKs = wk5.alloc(), wk5.alloc(), wk5r.alloc()
                    P.dma(Kc2.a, cmk[l, s, mb * 128:(mb + 1) * 128].rearrange("t h d -> t (h d)"))
                    P.dma(Vc2.a, cmv[l, s, mb * 128:(mb + 1) * 128].rearrange("t h d -> t (h d)"))
                    ps = psum()
                    for h in range(4):
                        P.transpose(ps[:, h * 128:(h + 1) * 128], Kc2[:, h * 128:(h + 1) * 128], ident[:, :])
                    P.copy(RND(MKs.a), ps.a)
                    pfree(ps)
                    pss = psum()
                    for h in range(4):
                        P.mm(pss[:, h * 4:h * 4 + 4], MKs[:, h * 128:(h + 1) * 128], QX[h][:, 4 * s:4 * s + 4])
                    pT = wkr.alloc()
                    P.act(RND(pT[:, 0:16]), pss[:, 0:16], AF.Exp, scale=xsc)
                    pfree(pss)
                    for h in range(4):
                        P.mm(oxP[:, mb * 256 + 16 * s + 4 * h:mb * 256 + 16 * s + 4 * h + 4], Vc2[:, 128 * h:128 * (h + 1)],
                             pT[:, h * 4:h * 4 + 4])
                    P.mm(dP[:, mb * 256 + 16 * s:mb * 256 + 16 * s + 16], ones[:, :], pT[:, 0:16])
                    wk.free(pT)
                    wk5.free(Kc2, Vc2, MKs)
            NC16 = NSq * 16
            rd, oxs = wk5.alloc(), wk5.alloc()
            P.copy(rd[:, 0:NC16], dP[:, 0:NC16])
            P.tt(rd[:, 0:NC16], rd[:, 0:NC16], dP[:, 256:256 + NC16], ALU.add)
            P.recip(rd[:, 0:NC16], rd[:, 0:NC16])
            P.copy(oxs[:, 0:NC16], oxP[:, 0:NC16])
            P.tt(oxs[:, 0:NC16], oxs[:, 0:NC16], oxP[:, 256:256 + NC16], ALU.add)
            for h in range(4):
                P.tt(RND(OX[h][:, 0:N].re("p (s t) -> p s t", t=4)),
                     oxs[:, 0:NC16].re("p (s h t) -> p s h t", h=4, t=4)[:, :, h, :],
                     rd[:, 0:NC16].re("p (s h t) -> p s h t", h=4, t=4)[:, :, h, :], ALU.mult)
            wk5.free(rd, oxs)
            pfree(oxP, dP)
        wk.free(*QX)
        proj_fm(Wd["x_wo"][l], 512, [OX[h][:, 0:N] for h in range(4)], N, res_consume)
        wk.free(*OX)

        P.tag = "ffn"
        HT = [wkr.alloc() for _ in range(8)]
        rms_apply([XT[c][:, 0:N] for c in range(8)], N, ones[:, :], D,
                  lambda c: g_ffn[:, l * 8 + c:l * 8 + c + 1], [HT[c][:, 0:N] for c in range(8)])
        HTn = [HT[c][:, 0:N] for c in range(8)]
        if prompt and first:
            for ch in range(44):
                P.memset(FC[l][ch].a, 0.0)
        ACTT = [wkr.alloc() for _ in range(NCH_FF)]
        R2 = NSq * 2
        sfc2 = sfc[l].rearrange("s j f -> (s j) f")
        osf2 = o_sf[l].rearrange("s j f -> (s j) f")
        pair = {}
        WL = 2 + Lg
        gbuf = {}

        def up_consume_f(half):
            def up_consume(c, mw, ps):
                ch = half * NCH_FF + c
                if prompt:
                    y = wk.alloc()
                    fw = lambda j: fcw[:, l * 132 + j * 44 + ch:l * 132 + j * 44 + ch + 1]
                    P.act(y[:, 0:N], ps[:, 2:N + 2], AF.Identity, scale=fw(2), bias=fcb[:, l * 44 + ch:l * 44 + ch + 1])
                    P.copy(ps[:, 0:2], FC[l][ch][:, 0:2], eng="dve")
                    P.stt(y[:, 0:N], ps[:, 1:N + 1], fw(1), y[:, 0:N], ALU.mult, ALU.add)
                    P.stt(y[:, 0:N], ps[:, 0:N], fw(0), y[:, 0:N], ALU.mult, ALU.add)
                    P.copy(FC[l][ch][:, 0:2], ps[:, N:N + 2], eng="dve")
                    pfree(ps)

                    def post_p():
                        if half == 0:
                            P.act(ACTT[c][:, 0:N], y[:, 0:N], AF.Silu)
                        else:
                            P.tt(RND(ACTT[c][:, 0:N]), ACTT[c][:, 0:N], y[:, 0:N], ALU.mult, eng="pool")
                        wk.free(y)
                    return post_p
                ps3 = seg3(ps[:, 0:N], Lg)
                if prompt:
                    car3 = FC[l][ch][:, 0:2].re("p (s j) -> p s j", j=2)
                else:
                    if ch % 2 == 0:
                        st = wk5.alloc()
                        P.dma(st[0:R2, 0:256], sfc2[:, ch * 128:(ch + 2) * 128])
                        psx = psum()
                        for q in range(2):
                            P.transpose(psx[:, q * 32:q * 32 + R2], st[0:R2, q * 128:(q + 1) * 128], ident[0:R2, 0:R2])
                        fc2, fn2 = wk.alloc(), wk.alloc()
                        P.copy(fc2[:, 0:64], psx[:, 0:64])
                        pfree(psx)
                        wk5.free(st)
                        pair["fc"], pair["fn"] = fc2, fn2
                    q_ = ch % 2
                    car3 = pair["fc"][:, q_ * 32:q_ * 32 + R2].re("p (s j) -> p s j", j=2)
                y = wk.alloc()
                y3 = seg3(y[:, 0:N], Lg)
                fw = lambda j: fcw[:, l * 132 + j * 44 + ch:l * 132 + j * 44 + ch + 1]
                P.act(y3, ps3, AF.Identity, scale=fw(2), bias=fcb[:, l * 44 + ch:l * 44 + ch + 1])
                P.stt(y3[:, :, 1:Lg], ps3[:, :, 0:Lg - 1], fw(1), y3[:, :, 1:Lg], ALU.mult, ALU.add)
                P.stt(y3[:, :, 2:Lg], ps3[:, :, 0:Lg - 2], fw(0), y3[:, :, 2:Lg], ALU.mult, ALU.add)
                P.stt(y3[:, :, 0:1], car3[:, :, 1:2], fw(1), y3[:, :, 0:1], ALU.mult, ALU.add)
                P.stt(y3[:, :, 0:2], car3[:, :, 0:2], fw(0), y3[:, :, 0:2], ALU.mult, ALU.add)
                if prompt:
                    P.copy(car3, ps3[:, :, Lg - 2:Lg])
                else:
                    P.copy(pair["fn"][:, q_ * 32:q_ * 32 + R2].re("p (s j) -> p s j", j=2), ps3[:, :, Lg - 2:Lg])
                    if q_ == 1:
                        psx = psum()
                        for q in range(2):
                            P.transpose(psx[0:R2, q * 128:(q + 1) * 128], pair["fn"][:, q * 32:q * 32 + R2], ident[:, :])
                        st = wk5.alloc()
                        P.copy(st[0:R2, 0:256], psx[0:R2, 0:256])
                        pfree(psx)
                        P.dma(osf2[:, (ch - 1) * 128:(ch + 1) * 128], st[0:R2, 0:256], eng="pool", out_dma=True)
                        wk5.free(st)
                        wk.free(pair["fc"], pair["fn"])
                pfree(ps)

                def post():
                    if half == 0:
                        P.act(ACTT[c][:, 0:N], y[:, 0:N], AF.Silu)
                    else:
                        P.tt(RND(ACTT[c][:, 0:N]), ACTT[c][:, 0:N], y[:, 0:N], ALU.mult, eng="pool")
                    wk.free(y)
                return post
            return up_consume
        po = 2 if prompt else 0
        proj_fm(Wd["ffn_w_up"][l][:, 0:D_FF], D, HTn, N, up_consume_f(0), ps_off=po)
        proj_fm(Wd["ffn_w_up"][l][:, D_FF:2 * D_FF], D, HTn, N, up_consume_f(1), ps_off=po)
        wk.free(*HT)
        P.tag = "ffn_down"
        for hf in range(2):
            accs = [psum() for _ in range(4)]
            for jb in range(0, NCH_FF, 4):
                nj = min(4, NCH_FF - jb)
                wt = wpool.alloc()
                wv = wt[:, 0:nj * 512].re("p (j c) -> p j c", j=nj)
                P.dma(wv, Wd["ffn_w_down"][l][jb * 128:(jb + nj) * 128, hf * 512:(hf + 1) * 512].rearrange("(j p) c -> p j c", p=128))
                for jj in range(nj):
                    j = jb + jj
                    for n in range(4):
                        P.mm(accs[n][:, 0:N], wv[:, jj, n * 128:(n + 1) * 128], ACTT[j][:, 0:N],
                             start=(j == 0), stop=(j == NCH_FF - 1))
                wpool.free(wt)
            for n in range(4):
                res_consume(hf * 4 + n, 128, accs[n])
        if prompt and last:
            for g0 in range(0, 44, 4):
                ps = psum()
                for q in range(4):
                    P.transpose(ps[0:2, q * 128:(q + 1) * 128], FC[l][g0 + q].a, ident[:, :])
                st = wk5.alloc()
                P.copy(st[0:2, :], ps[0:2, :])
                pfree(ps)
                P.dma(o_pf[l, sq_i][:, g0 * 128:(g0 + 4) * 128], st[0:2, :], eng="pool", out_dma=True)
                wk5.free(st)
        wk.free(*ACTT)

    tiles = []
    for sq in range(NP):
        for ti in range(SEQ // T):
            tiles.append(dict(kind="p", N=T, seq=sq, pos0=ti * T, first=(ti == 0), last=(ti == SEQ // T - 1)))
    if NSq > 0:
        tiles.append(dict(kind="s", N=NST, seq=0, pos0=0, first=True, last=True))
    for tl in tiles:
        N = tl["N"]
        P.tag = "io"
        if tl["kind"] == "p":
            for b in range(NB):
                load_fm(xp[tl["seq"], tl["pos0"] + b * 128:tl["pos0"] + (b + 1) * 128, :], 128, XT, b * 128)
        else:
            load_fm(xs.rearrange("s t d -> (s t) d"), NST, XT, 0)
        for l in range(L):
            layer(l, tl)
        P.tag = "io"
        if tl["kind"] == "p":
            for b in range(NB):
                store_fm(XT, 128, b * 128, y_p[tl["seq"], tl["pos0"] + b * 128:tl["pos0"] + (b + 1) * 128, :])
        else:
            store_fm(XT, NST, 0, y_s.rearrange("s t d -> (s t) d"))

    P.emit()
    P.close()
    return nc, hc, P


OUT_NAMES = ["y_prompt", "y_sample", "p_swa_k", "p_swa_v", "p_lru_h", "p_lru_conv", "p_gla_s", "p_mem_k",
             "p_mem_v", "p_ffn_conv", "s_swa_k", "s_swa_v", "s_lru_h", "s_lru_conv", "s_gla_s", "s_ffn_conv"]
OUT_AXIS = [0, 0, 1, 1, 1, 1, 1, 1, 1, 1, 1, 1, 1, 1, 1, 1]
SAMPLE_IN = {"cache_swa_k": 1, "cache_swa_v": 1, "state_lru_h": 1, "state_lru_conv": 1, "state_gla_s": 1,
             "cache_mem_k": 1, "cache_mem_v": 1, "state_ffn_conv": 1}

CFG = dict(T=256, SEQ=2048, NP=2, NSq=16)
NCORES = 8


def make_in_maps(inputs, cfg, ncores, hc):
    NP, NSq = cfg["NP"], cfg["NSq"]
    maps = []
    for i in range(ncores):
        m = {}
        m["x_prompt"] = np.ascontiguousarray(inputs["x_prompt"][i * NP:(i + 1) * NP])
        m["mem_prompt"] = np.ascontiguousarray(inputs["mem_prompt"][i * NP:(i + 1) * NP])
        m["x_sample"] = np.ascontiguousarray(inputs["x_sample"][i * NSq:(i + 1) * NSq])
        for k in SAMPLE_IN:
            m[k] = np.ascontiguousarray(inputs[k][:, i * NSq:(i + 1) * NSq])
        for k in WEIGHT_SHAPES:
            m[k] = np.ascontiguousarray(inputs[k], dtype=np.float32)
        for k, v in hc.items():
            m["c_" + k] = v
        maps.append(m)
    return maps


def kernel(**inputs):
    inputs = {k: np.asarray(v) for k, v in inputs.items()}
    nc, hc, _ = build(CFG)
    in_maps = make_in_maps(inputs, CFG, NCORES, hc)
    res = run_bass_kernel_spmd(nc, in_maps, core_ids=list(range(NCORES)))
    outs = []
    for name, ax in zip(OUT_NAMES, OUT_AXIS):
        outs.append(np.concatenate([np.asarray(r[name]) for r in res.results], axis=ax).astype(np.float32))
    return tuple(outs)
```
